# Optimizing a Trainium2 kernel written in Bass

```python
import math
import jax, jax.numpy as jnp
from jax import lax
import numpy as np

D_MODEL = 1024
BATCH = 1
SEQ = 16384
DEPTH = 1

D_MIX = D_MODEL
D_ATTN = D_MIX // 2
D_POOL = D_MIX - D_ATTN
HEAD_DIM = 64
N_Q_HEADS = D_ATTN // HEAD_DIM
N_KV_HEADS = 2
Q_PER_KV = N_Q_HEADS // N_KV_HEADS
D_KV = N_KV_HEADS * HEAD_DIM
WINDOW = 128
BLOCK = 128
N_BUCKETS = 32
MAX_DISTANCE = 128
POOL_WINDOWS = (2, 4, 8, 16)
N_POOL_GROUPS = len(POOL_WINDOWS)
POOL_GROUP = D_POOL // N_POOL_GROUPS
D_IN = 2 * D_ATTN + 2 * D_KV + 2 * D_POOL
EPS = 1e-6
NEG = -1e30

kernel_name = "hymba_bidir_swa_pool_hybrid"


def rmsnorm(x, g):
    xf = x.astype(jnp.float32)
    y = xf * lax.rsqrt(jnp.mean(xf * xf, axis=-1, keepdims=True) + EPS)
    return (y * g.astype(jnp.float32)).astype(x.dtype)


def t5_bucket(rel):
    nb = N_BUCKETS // 2
    max_exact = nb // 2
    ret = jnp.where(rel > 0, nb, 0)
    n = jnp.abs(rel)
    nf = jnp.maximum(n, 1).astype(jnp.float32)
    large = max_exact + (jnp.log(nf / max_exact) / math.log(MAX_DISTANCE / max_exact)
                         * (nb - max_exact)).astype(jnp.int32)
    large = jnp.minimum(large, nb - 1)
    return ret + jnp.where(n < max_exact, n, large)


def banded_attention(q, k, v, rel_bias, sink):
    B, S = q.shape[0], q.shape[1]
    nb = S // BLOCK
    qb = q.reshape(B, nb, BLOCK, N_KV_HEADS, Q_PER_KV, HEAD_DIM)
    pad = ((0, 0), (BLOCK, BLOCK), (0, 0), (0, 0))
    kp = jnp.pad(k, pad).reshape(B, nb + 2, BLOCK, N_KV_HEADS, HEAD_DIM)
    vp = jnp.pad(v, pad).reshape(B, nb + 2, BLOCK, N_KV_HEADS, HEAD_DIM)
    kw = jnp.concatenate([kp[:, :-2], kp[:, 1:-1], kp[:, 2:]], axis=2)
    vw = jnp.concatenate([vp[:, :-2], vp[:, 1:-1], vp[:, 2:]], axis=2)

    scale = HEAD_DIM ** -0.5
    s = jnp.einsum('bnqkgd,bnskd->bnkgqs', qb, kw).astype(jnp.float32) * scale

    qi = jnp.arange(BLOCK)[:, None]
    kj = jnp.arange(3 * BLOCK)[None, :]
    rel = (kj - BLOCK) - qi
    bias = rel_bias.astype(jnp.float32)[t5_bucket(rel)]
    bias = jnp.transpose(bias, (2, 0, 1)).reshape(N_KV_HEADS, Q_PER_KV, BLOCK, 3 * BLOCK)
    key_pos = jnp.arange(nb)[:, None] * BLOCK - BLOCK + jnp.arange(3 * BLOCK)[None, :]
    valid = (jnp.abs(rel) <= WINDOW)[None] & ((key_pos >= 0) & (key_pos < S))[:, None, :]
    s = jnp.where(valid[None, :, None, None], s + bias[None, None], NEG)

    sk = sink.astype(jnp.float32).reshape(1, 1, N_KV_HEADS, Q_PER_KV, 1, 1)
    m = jnp.maximum(jnp.max(s, axis=-1, keepdims=True), sk)
    p = jnp.exp(s - m)
    p = p / (jnp.sum(p, axis=-1, keepdims=True) + jnp.exp(sk - m))
    o = jnp.einsum('bnkgqs,bnskd->bnqkgd', p.astype(v.dtype), vw)
    return o.reshape(B, S, N_Q_HEADS * HEAD_DIM)


def multiscale_pool(u, pool_w, pool_scale):
    B, S, _ = u.shape
    uf = u.astype(jnp.float32)
    c = jnp.concatenate([jnp.zeros((B, 1, D_POOL), jnp.float32), jnp.cumsum(uf, axis=1)], axis=1)
    t = jnp.arange(S)
    outs = []
    for gi, w in enumerate(POOL_WINDOWS):
        left = w // 2
        right = w - 1 - left
        lo = jnp.maximum(t - left, 0)
        hi = jnp.minimum(t + right + 1, S)
        sl = slice(gi * POOL_GROUP, (gi + 1) * POOL_GROUP)
        cg = c[..., sl]
        mean = (cg[:, hi] - cg[:, lo]) / (hi - lo).astype(jnp.float32)[None, :, None]
        outs.append(mean - uf[..., sl])
    y = jnp.stack(outs, axis=2)
    y = jnp.einsum('bsgc,gcd->bsgd', y, pool_w.astype(jnp.float32)).reshape(B, S, D_POOL)
    return (y * pool_scale.astype(jnp.float32)).astype(u.dtype)


def setup_inputs(seed: int = 0) -> dict:
    key = jax.random.key(seed)
    ks = jax.random.split(key, 10)
    f32 = jnp.float32
    x = jax.random.normal(ks[0], (BATCH, SEQ, D_MODEL), f32)
    pre_norm_g = 1.0 + 0.05 * jax.random.normal(ks[1], (DEPTH, D_MODEL), f32)
    w_in = jax.random.normal(ks[2], (DEPTH, D_MODEL, D_IN), f32) * D_MODEL ** -0.5
    rel_bias = 0.5 * jax.random.normal(ks[3], (N_BUCKETS, N_Q_HEADS), f32)
    attn_sink = 0.5 * jax.random.normal(ks[4], (DEPTH, N_Q_HEADS), f32)
    pool_w = jax.random.normal(ks[5], (DEPTH, N_POOL_GROUPS, POOL_GROUP, POOL_GROUP), f32) * POOL_GROUP ** -0.5
    pool_scale = 1.0 + 0.1 * jax.random.normal(ks[6], (DEPTH, D_POOL), f32)
    w_out = jax.random.normal(ks[7], (DEPTH, D_MIX, D_MODEL), f32) * D_MIX ** -0.5
    post_norm_g = 1.0 + 0.05 * jax.random.normal(ks[8], (DEPTH, D_MODEL), f32)
    return {"x": x, "pre_norm_g": pre_norm_g, "w_in": w_in, "rel_bias": rel_bias,
            "attn_sink": attn_sink, "pool_w": pool_w, "pool_scale": pool_scale,
            "w_out": w_out, "post_norm_g": post_norm_g}


def reference(x, pre_norm_g, w_in, rel_bias, attn_sink, pool_w, pool_scale, w_out, post_norm_g):
    B, S, _ = x.shape
    splits = [D_ATTN, D_ATTN + D_KV, D_ATTN + 2 * D_KV,
              2 * D_ATTN + 2 * D_KV, 2 * D_ATTN + 2 * D_KV + D_POOL]
    for l in range(DEPTH):
        h = rmsnorm(x, pre_norm_g[l])
        proj = jnp.einsum('bsd,de->bse', h, w_in[l])
        q, k, v, g_a, u_p, g_p = jnp.split(proj, splits, axis=-1)
        q = q.reshape(B, S, N_Q_HEADS, HEAD_DIM)
        k = k.reshape(B, S, N_KV_HEADS, HEAD_DIM)
        v = v.reshape(B, S, N_KV_HEADS, HEAD_DIM)
        a = banded_attention(q, k, v, rel_bias, attn_sink[l]) * jax.nn.silu(g_a)
        p = multiscale_pool(u_p, pool_w[l], pool_scale[l]) * jax.nn.silu(g_p)
        mixed = jnp.einsum('bse,ed->bsd', jnp.concatenate([a, p], axis=-1), w_out[l])
        x = x + rmsnorm(mixed, post_norm_g[l])
    return x
```

```python
import math
import os
import numpy as np
import concourse.bass as bass
import concourse.mybir as mybir
from concourse.bass_utils import run_bass_kernel_spmd

F32 = mybir.dt.float32
BF16 = mybir.dt.bfloat16
AF = mybir.ActivationFunctionType
ALU = mybir.AluOpType

N_CORES = 8
SEQ = 16384
D = 1024
D_IN = 2304
TOK_CORE = SEQ // N_CORES
EXT = TOK_CORE + 256
NSEG = 4
SEG_TOK = 512
SEG_TILES = 6
SEG_EXT = SEG_TILES * 128
EPS = 1e-6
POOL_WINDOWS = (2, 4, 8, 16)

DEBUG = bool(int(os.environ.get("MK_DEBUG", "0")))
POOL_EMULT = True


class Buf:
    __slots__ = ("name", "w", "r")

    def __init__(self, name):
        self.name = name
        self.w = None
        self.r = []


class Sched:
    def __init__(self, nc):
        self.nc = nc
        self.eng = {"pe": nc.tensor, "act": nc.scalar, "dve": nc.vector, "pool": nc.gpsimd, "sp": nc.sync}
        self._ctx = []
        self.semobj = {}
        self.cnt = {}
        self.seen = {k: {} for k in self.eng}
        for k in ("pe", "act", "dve", "pool"):
            self.new_sem(k)

    def new_sem(self, key):
        c = self.nc.semaphore("s_" + key)
        s = c.__enter__()
        self._ctx.append(c)
        self.semobj[key] = s
        self.cnt[key] = 0
        return key

    def wait(self, e, tok):
        if tok is None:
            return
        key, val = tok
        if key == "pe" and e == "pe":
            return
        if self.seen[e].get(key, 0) >= val:
            return
        self.eng[e].wait_ge(self.semobj[key], val)
        self.seen[e][key] = val

    def _deps(self, e, reads, writes):
        for b in reads:
            assert b.w is not None, f"read of never-written buffer {b.name}"
            self.wait(e, b.w)
        for b in writes:
            self.wait(e, b.w)
            for t in b.r:
                self.wait(e, t)

    def _commit(self, tok, reads, writes):
        for b in reads:
            b.r.append(tok)
        for b in writes:
            b.w = tok
            b.r = []

    def run(self, e, fns, reads=(), writes=()):
        self._deps(e, reads, writes)
        if not isinstance(fns, (list, tuple)):
            fns = [fns]
        inst = None
        for f in fns:
            inst = f()
        self.cnt[e] += 1
        inst.then_inc(self.semobj[e], 1)
        tok = (e, self.cnt[e])
        self._commit(tok, reads, writes)
        return tok

    def dma_group(self, q, semkey, fns, buf):
        assert self.cnt[semkey] == 0 and buf.w is None
        for f in fns:
            f().then_inc(self.semobj[semkey], 16)
            self.cnt[semkey] += 16
        buf.w = (semkey, self.cnt[semkey])

    def dma(self, q, semkey, fn, reads=(), writes=()):
        self._deps(q, reads, writes)
        inst = fn()
        self.cnt[semkey] += 16
        inst.then_inc(self.semobj[semkey], 16)
        tok = (semkey, self.cnt[semkey])
        self._commit(tok, reads, writes)
        return tok


def build_nc():
    nc = bass.Bass("TRN2", target_bir_lowering=False)
    S = Sched(nc)
    dbg = {}

    def din(name, shape):
        return nc.dram_tensor(name, list(shape), F32, kind="ExternalInput").ap()

    x_ext = din("x_ext", [EXT, D])
    kmask_pt_d = din("kmask_pt", [128, 18])
    kmask_row_d = din("kmask_row", [EXT])
    gpre_d = din("gpre", [128, 8])
    w_in_d = din("w_in", [D, D_IN])
    relb_d = din("rel_bias", [32, 8])
    onehot_d = din("onehot", [32, 765])
    sink_d = din("attn_sink", [8])
    poolw_d = din("pool_w", [4, 128, 128])
    ps_d = din("pool_scale_pg", [128, 4])
    w_out_d = din("w_out", [D, D])
    gpost_d = din("post_norm_g", [D])
    y_d = nc.dram_tensor("y", [TOK_CORE, D], F32, kind="ExternalOutput").ap()
    tab_scr = nc.dram_tensor("tab_scr", [8, 765], F32, kind="Internal").ap()

    def sb(name, shape, dt):
        return nc.alloc_sbuf_tensor("sb_" + name, list(shape), dt)

    Win_b = sb("Win_b", [128, 8, D_IN], BF16)
    Wout_b = sb("Wout_b", [128, 8, D], BF16)
    poolw_b = sb("poolw_b", [128, 4, 128], BF16)
    E = sb("E", [128, 3, 8, 128], F32)
    gpost = sb("gpost", [128, D], F32)
    ident = sb("ident", [128, 128], BF16)
    identf = sb("identf", [128, 128], F32)
    gpre = sb("gpre", [128, 8], F32)
    ps = sb("ps", [128, 4], F32)
    esink = sb("esink", [128, 8], F32)
    kmask_pt = sb("kmask_pt", [128, 18], F32)
    relb = sb("relb", [32, 8], F32)
    nhalf = sb("nhalf", [128, 2], F32)
    ms = sb("ms", [128, 32], F32)
    r0 = sb("r0", [128, 32], F32)
    rstd = sb("rstd", [128, 32], F32)
    ms2 = sb("ms2", [128, 32], F32)
    r2 = sb("r2", [128, 16], F32)
    rstd2 = sb("rstd2", [128, 16], F32)
    den = sb("den", [128, 2, 8], F32)
    rden = sb("rden", [128, 2, 8], F32)
    mrow = sb("mrow", [128, 2, 24], F32)
    mta = sb("mta", [128, 2, 24], F32)
    mtb = sb("mtb", [128, 2, 24], F32)
    rc = sb("rc", [128, 2, 4, 8], F32)
    e8a = sb("e8a", [128, 8], F32)
    S0 = sb("S0", [128, 2304], F32)
    S1 = sb("S1", [128, 2304], F32)
    hTs = [sb(f"hT{i}", [128, 8, SEG_EXT], BF16) for i in range(2)]
    qT = [sb(f"qT{i}", [128, 4, SEG_TOK], BF16) for i in range(2)]
    kT = [sb(f"kT{i}", [128, SEG_EXT], BF16) for i in range(2)]
    v1 = [sb(f"v1{i}", [128, SEG_TILES, 2, 65], BF16) for i in range(2)]
    GaT = [sb(f"GaT{i}", [128, 4, SEG_TOK], BF16) for i in range(2)]
    GpT = [sb(f"GpT{i}", [128, 4, SEG_TOK], BF16) for i in range(2)]
    upT = sb("upT", [128, 4, SEG_TOK + 16], F32)
    catT = sb("catT", [128, 8, SEG_TOK], BF16)
    hk = catT[:].rearrange("p a b -> p (a b)").bitcast(F32).rearrange("p (s h c) -> p s h c", s=2, h=8)
    xs = sb("xs", [128, 3, D], F32)
    xb = sb("xb", [128, 2, D], BF16)
    junk = sb("junk", [128, D], BF16)
    expS = sb("expS", [128, 3, 512], F32)
    pT = sb("pT", [128, 3, 3, 512], BF16)
    expS_flat = expS[:].rearrange("p a b -> p (a b)")
    ohs = expS_flat[0:32, 0:765]
    tabE = expS_flat[0:8, 768:768 + 765]
    onorm = sb("onorm", [128, 2, 512], BF16)
    ta = sb("ta", [128, SEG_TOK + 16], F32)
    poolw_f = ta[:, 0:512].rearrange("p (g d) -> p g d", g=4)
    tb = sb("tb", [128, SEG_TOK + 16], F32)
    yT = sb("yT", [128, 4, SEG_TOK], BF16)
    trb = nc.alloc_psum_tensor("trb", [128, 8, 128], BF16)
    acc = nc.alloc_psum_tensor("acc", [128, 2, 512], F32)
    sT = nc.alloc_psum_tensor("sT", [128, 3, 512], F32)
    ob = nc.alloc_psum_tensor("ob", [128, 2, 512], F32)

    B = {}

    def buf(name):
        if name not in B:
            B[name] = Buf(name)
        return B[name]

    for k in ("consts", "ld_e", "ld_w0", "ld_w1", "ld_w2", "ld_w3", "xs0", "xs1", "xs2", "rs0", "rs1", "os0", "os1", "dbg", "scr", "hk0", "hk1"):
        S.new_sem(k)

    sp = nc.sync
    boh = buf("onehot")
    S.dma_group("sp", "ld_e", [
        lambda: sp.dma_start(out=relb[:], in_=relb_d[:, :]),
        lambda: sp.dma_start(out=ohs, in_=onehot_d[:, :]),
    ], boh)
    bc_ = buf("consts")
    bconsts = [bc_]
    S.dma_group("sp", "consts", [
        lambda: sp.dma_start(out=gpre[:], in_=gpre_d[:, :]),
        lambda: sp.dma_start(out=kmask_pt[:], in_=kmask_pt_d[:, :]),
        lambda: sp.dma_start(out=ps[:], in_=ps_d[:, :]),
        lambda: sp.dma_start(out=esink[:], in_=bass.AP(sink_d.tensor, 0, [[0, 128], [1, 8]])),
        lambda: sp.dma_start(out=gpost[:], in_=bass.AP(gpost_d.tensor, 0, [[0, 128], [1, D]])),
        lambda: sp.dma_start(out=mrow[:, 0, :], in_=bass.AP(kmask_row_d.tensor, 120, [[0, 128], [1, 24]])),
        lambda: sp.dma_start(out=mrow[:, 1, :], in_=bass.AP(kmask_row_d.tensor, 120 + 1536 + 504, [[0, 128], [1, 24]])),
        lambda: sp.dma_start(out=poolw_f, in_=poolw_d.rearrange("g c d -> c g d")),
    ], bc_)

    bS = [buf(f"S{i}") for i in range(4)]
    Sst = [S0[:, 0:1152], S0[:, 1152:2304], S1[:, 0:1152], S1[:, 1152:2304]]
    ldw = ["ld_w0", "ld_w1", "ld_w2", "ld_w3"]
    wjob = [0]

    WBLK = [("kv", 512, 256), ("u", 1280, 512), ("q", 0, 512), ("ga", 768, 512), ("gp", 1792, 512)]
    wjobs = []
    for (nm_, c0_, nc_) in WBLK:
        ndk_ = 4 if nc_ == 256 else 2
        for dk0_ in range(0, 8, ndk_):
            wjobs.append((nm_, c0_, nc_, dk0_, ndk_))
    bWblk = {nm_: [buf(f"W_{nm_}{dk}") for dk in range(8)] for (nm_, _, _) in WBLK}

    def load_w_in(job):
        if job >= len(wjobs):
            return
        nm_, c0_, nc_, dk0_, ndk_ = wjobs[job]
        sl = job % 4
        S.dma("sp", ldw[sl], lambda: sp.dma_start(
            out=Sst[sl][:, 0:ndk_ * nc_].rearrange("p (k c) -> p k c", k=ndk_),
            in_=w_in_d[dk0_ * 128:(dk0_ + ndk_) * 128, c0_:c0_ + nc_].rearrange("(k p) c -> p k c", p=128)),
            writes=[bS[sl]])

    jobs = [(s_, lt_) for s_ in range(NSEG) for lt_ in range(SEG_TILES)]
    xsem = ["xs0", "xs1", "xs2"]
    bxs = [buf("xs0"), buf("xs1"), buf("xs2")]

    def a_load(k):
        if k >= len(jobs):
            return
        s_, lt_ = jobs[k]
        et = 4 * s_ + lt_
        i = k % 3
        if k < 3:
            S.dma("sp", xsem[i], lambda: sp.dma_start(out=xs[:, i, :], in_=x_ext[et * 128:(et + 1) * 128, :]), writes=[bxs[i]])
        else:
            S.dma("act", xsem[i], lambda: act.dma_start(out=xs[:, i, :], in_=x_ext[et * 128:(et + 1) * 128, :]), writes=[bxs[i]])

    a_load(0)
    a_load(1)
    a_load(2)
    for job_ in range(4):
        load_w_in(job_)

    bident = buf("ident")
    pool = nc.gpsimd
    S.run("pool", lambda: pool.memset(identf[:], 0.0), writes=[bident])
    S.run("pool", lambda: pool.affine_select(out=identf[:], in_=identf[:], pattern=[[-1, 128]], compare_op=ALU.not_equal,
                                             fill=1.0, base=0, channel_multiplier=1), writes=[bident])
    S.run("pool", lambda: pool.tensor_copy(out=ident[:], in_=identf[:]), reads=[bident], writes=[buf("identb")])
    bsm = buf("smalls")
    S.run("pool", lambda: pool.memset(nhalf[:, 0:1], -0.5), writes=[bsm])
    S.run("pool", lambda: pool.memset(nhalf[:, 1:2], EPS), writes=[bsm])
    S.run("pool", lambda: pool.memset(ms[:], 0.0), writes=[buf("ms")])
    S.run("pool", lambda: pool.memset(ms2[:], 0.0), writes=[buf("ms2")])

    bE = buf("E")
    bsT = [buf("sT0"), buf("sT1"), buf("sT2")]
    tt = sb("tt", [128, 2, 512], F32)
    pe = nc.tensor
    act = nc.scalar
    dve = nc.vector
    bexp_all = [buf("exp0"), buf("exp1"), buf("exp2")]
    S.run("pe", lambda: pe.matmul(sT[0:8, 0, 0:510], lhsT=relb[:, :], rhs=ohs[:, 0:510], start=True, stop=True),
          reads=[boh], writes=[bsT[0]])
    S.run("pe", lambda: pe.matmul(sT[0:8, 1, 0:255], lhsT=relb[:, :], rhs=ohs[:, 510:765], start=True, stop=True),
          reads=[boh], writes=[bsT[1]])
    btab = buf("tabE")
    S.run("act", lambda: act.activation(out=tabE[:, 0:510], in_=sT[0:8, 0, 0:510], func=AF.Exp), reads=[bsT[0]], writes=[btab])
    S.run("act", lambda: act.activation(out=tabE[:, 510:765], in_=sT[0:8, 1, 0:255], func=AF.Exp), reads=[bsT[1]], writes=[btab])
    S.run("act", lambda: act.activation(out=esink[:], in_=esink[:], func=AF.Exp), reads=bconsts, writes=[buf("esink")])
    S.run("pool", lambda: pool.memset(tabE[:, 0:127], 0.0), writes=[btab])
    S.run("pool", lambda: pool.memset(tabE[:, 510 + 128:765], 0.0), writes=[btab])
    bscr = buf("scr")
    S.dma("act", "scr", lambda: act.dma_start(out=tab_scr[:, :], in_=tabE), reads=[btab], writes=[bscr])
    bhk = [buf("hk0"), buf("hk1")]

    def hank_dma(j):
        S.dma("act", f"hk{j % 2}", lambda: act.dma_start(
            out=hk[:, j % 2, :, :], in_=bass.AP(tab_scr.tensor, j * 255, [[1, 128], [765, 8], [1, 128]])),
            reads=[bscr], writes=[bhk[j % 2]])

    def hank_rev(j):
        S.run("pool", lambda: pool.tensor_copy(out=E[:, j, :, :], in_=hk[:, j % 2, :, ::-1]), reads=[bhk[j % 2]], writes=[bE])

    hank_dma(0)
    hank_dma(1)

    def finish_E():
        hank_rev(0)
        hank_dma(2)
        hank_rev(1)
        hank_rev(2)
        bcat.r.append(bE.w)
        for b_ in bexp_all:
            b_.r.append(bscr.w)
            b_.r.extend(boh.r)

    bm = buf("medge")
    S.run("pool", lambda: pool.tensor_tensor(out=mta[:, :, 1:24], in0=mrow[:, :, 0:23], in1=mrow[:, :, 1:24], op=ALU.add), reads=bconsts, writes=[bm])
    S.run("pool", lambda: pool.tensor_copy(out=rc[:, :, 0, :], in_=mta[:, :, 8:16]), reads=[bm], writes=[buf("rc")])
    S.run("pool", lambda: pool.tensor_tensor(out=mtb[:, :, 2:23], in0=mta[:, :, 1:22], in1=mta[:, :, 3:24], op=ALU.add), reads=[bm], writes=[bm])
    S.run("pool", lambda: pool.tensor_copy(out=rc[:, :, 1, :], in_=mtb[:, :, 8:16]), reads=[bm], writes=[buf("rc")])
    S.run("pool", lambda: pool.tensor_tensor(out=mta[:, :, 4:21], in0=mtb[:, :, 2:19], in1=mtb[:, :, 6:23], op=ALU.add), reads=[bm], writes=[bm])
    S.run("pool", lambda: pool.tensor_copy(out=rc[:, :, 2, :], in_=mta[:, :, 8:16]), reads=[bm], writes=[buf("rc")])
    S.run("pool", lambda: pool.tensor_tensor(out=mtb[:, :, 8:17], in0=mta[:, :, 4:13], in1=mta[:, :, 12:21], op=ALU.add), reads=[bm], writes=[bm])
    S.run("pool", lambda: pool.tensor_copy(out=rc[:, :, 3, :], in_=mtb[:, :, 8:16]), reads=[bm], writes=[buf("rc")])

    bWin = [b_ for nm_ in bWblk for b_ in bWblk[nm_]]
    cast_i = [0]

    def cast_w_in(job):
        nm_, c0_, nc_, dk0_, ndk_ = wjobs[job]
        sl = job % 4
        st = Sst[sl][:, 0:ndk_ * nc_].rearrange("p (k c) -> p k c", k=ndk_)
        for kk in range(ndk_):
            dk = dk0_ + kk
            e_ = "dve" if cast_i[0] % 2 == 0 else "act"
            cast_i[0] += 1
            g_ = gpre[:, dk:dk + 1]
            if nm_ == "q":
                out_ap = Win_b[:, dk, 0:512].rearrange("p (c hf d) -> p c hf d", c=4, hf=2)
                in_ap = st[:, kk, :].rearrange("p (hf c d) -> p c hf d", hf=2, c=4)
            else:
                out_ap = Win_b[:, dk, c0_:c0_ + nc_]
                in_ap = st[:, kk, :]
            if e_ == "dve":
                S.run("dve", lambda: dve.tensor_scalar(out=out_ap, in0=in_ap, scalar1=g_, scalar2=None, op0=ALU.mult),
                      reads=[bS[sl]] + bconsts, writes=[bWblk[nm_][dk]])
            else:
                S.run("act", lambda: act.activation(out=out_ap, in_=in_ap, func=AF.Copy, scale=g_),
                      reads=[bS[sl]] + bconsts, writes=[bWblk[nm_][dk]])
        load_w_in(job + 4)

    bWout = [buf(f"Wout{ek}") for ek in range(8)]

    def wout_load(ek):
        sl = ek % 4
        S.dma("sp", ldw[sl], lambda: sp.dma_start(out=Sst[sl][:, 0:D], in_=w_out_d[ek * 128:(ek + 1) * 128, :]), writes=[bS[sl]])

    def wout_gen():
        for ek in range(4):
            wout_load(ek)
        for ek in range(8):
            sl = ek % 4
            S.run("act", lambda: act.activation(out=Wout_b[:, ek, :], in_=Sst[sl][:, 0:D], func=AF.Copy, scale=0.5),
                  reads=[bS[sl]], writes=[bWout[ek]])
            if ek + 4 < 8:
                wout_load(ek + 4)
            yield

    bpw = buf("poolw")
    S.run("pool", lambda: pool.tensor_copy(out=poolw_b[:], in_=poolw_f), reads=bconsts, writes=[bpw])
    buf("ta").r.append(bpw.w)

    bxb = [buf("xb0"), buf("xb1")]
    btr = buf("trb")
    bhTs = [buf("hT0"), buf("hT1")]
    bacc = [buf("acc0"), buf("acc1")]
    bob = buf("ob")
    btt = [buf("tt0"), buf("tt1")]
    bexp = [buf("exp0"), buf("exp1"), buf("exp2")]
    bpT = [buf("pT0"), buf("pT1"), buf("pT2")]
    bon = [buf("on0"), buf("on1")]
    bjunk = buf("junk")
    bms = buf("ms")
    bms2 = buf("ms2")
    bms2h = [buf("ms2h0"), buf("ms2h1")]
    bstat = buf("stat")
    bup = buf("upT")
    bcat = buf("catT")
    bta = buf("ta")
    btb = buf("tb")
    byT = [buf(f"yT{i}") for i in range(4)]
    bden = buf("den")
    os_ = [S0[:, 0:D], S0[:, D:2 * D]]
    rs_ = [S1[:, 0:D], S1[:, D:2 * D]]
    bos = [buf("os0"), buf("os1")]
    brs = [buf("rs0"), buf("rs1")]
    ossem = ["os0", "os1"]
    rssem = ["rs0", "rs1"]

    own = {}

    def setown(name, seg):
        own[name] = seg

    def chk(name, seg):
        assert own.get(name) == seg, f"ordering bug: {name} holds segment {own.get(name)}, expected {seg}"

    xs_i = [0]
    acc_i = [0]
    tt_i = [0]
    final_tokens = []

    ring = [(acc[:, 0, :], bacc[0]), (acc[:, 1, :], bacc[1]), (sT[:, 0, :], bsT[0]), (sT[:, 1, :], bsT[1]), (sT[:, 2, :], bsT[2])]

    reserved = set()

    def next_bank(reserve=False):
        for _ in range(len(ring)):
            i = acc_i[0] % len(ring)
            acc_i[0] += 1
            if i not in reserved:
                if reserve:
                    reserved.add(i)
                return ring[i] + (i,)
        raise AssertionError("no free PSUM ring bank")

    def stage_a1(s, lt):
        k = s * SEG_TILES + lt
        i = k % 3
        col = k
        S.run("act", lambda: act.activation(out=junk[:], in_=xs[:, i, :], func=AF.Square, scale=1.0 / 32.0,
                                            accum_out=ms[:, col:col + 1]), reads=[bxs[i], bms], writes=[bjunk, buf(f"ms_{k}")])
        S.run("pool", lambda: pool.tensor_scalar(out=r0[:, col:col + 1], in0=ms[:, col:col + 1], scalar1=EPS, scalar2=None, op0=ALU.add),
              reads=[buf(f"ms_{k}")], writes=[buf(f"stat_{k}")])
        S.run("pool", lambda: pool.tensor_tensor(out=rstd[:, col:col + 1], in0=r0[:, col:col + 1], in1=nhalf[:, 0:1], op=ALU.pow),
              reads=[buf(f"stat_{k}"), bsm], writes=[buf(f"stat_{k}")])

    def stage_a2(s, lt):
        setown(f"hT{s % 2}", s)
        k = s * SEG_TILES + lt
        i = k % 3
        col = k
        j = col % 2
        if s == 0:
            S.run("dve", lambda: dve.tensor_scalar(out=xb[:, j, :], in0=xs[:, i, :], scalar1=rstd[:, col:col + 1], scalar2=None, op0=ALU.mult),
                  reads=[buf(f"stat_{k}"), bxs[i]], writes=[bxb[j]])
        else:
            S.run("act", lambda: act.activation(out=xb[:, j, :], in_=xs[:, i, :], func=AF.Copy, scale=rstd[:, col:col + 1]),
                  reads=[buf(f"stat_{k}"), bxs[i]], writes=[bxb[j]])
        a_load(k + 3)

    def stage_a3(s, lt):
        k = s * SEG_TILES + lt
        j = k % 2
        S.run("pe", [lambda dk=dk: pe.transpose(out=trb[:, dk, :], in_=xb[:, j, dk * 128:(dk + 1) * 128], identity=ident[:]) for dk in range(8)],
              reads=[bxb[j], buf("identb")], writes=[btr])
        S.run("dve", lambda: dve.tensor_copy(out=hTs[s % 2][:, :, lt * 128:(lt + 1) * 128], in_=trb[:, :, :]), reads=[btr], writes=[bhTs[s % 2]])

    def g_a(s):
        stage_a1(s, 0)
        for lt in range(SEG_TILES):
            if lt + 1 < SEG_TILES:
                stage_a1(s, lt + 1)
            stage_a2(s, lt)
            if lt >= 1:
                stage_a3(s, lt - 1)
            yield
        stage_a3(s, SEG_TILES - 1)
        yield

    def stage_kv(s):
        p = s % 2
        hT = hTs[p]
        bhT = bhTs[p]
        chk(f"hT{p}", s)
        setown(f"kT{p}", s)
        setown(f"v1{p}", s)
        bk = buf(f"kT{p}")
        bv = buf(f"v1{p}")
        for (t0, n) in ((0, 512), (512, 256)):
            bk_, bb_, _ = next_bank()
            S.run("pe", [lambda dk=dk: pe.matmul(bk_[:, 0:n], lhsT=Win_b[:, dk, 512:640], rhs=hT[:, dk, t0:t0 + n],
                                                 start=(dk == 0), stop=(dk == 7)) for dk in range(8)],
                  reads=[bhT] + bWblk["kv"], writes=[bb_])
            S.run("act", lambda: act.copy(out=kT[p][:, t0:t0 + n], in_=bk_[:, 0:n]), reads=[bb_], writes=[bk])
            yield
        for (l0, nl) in ((0, 4), (4, 2)):
            bk_, bb_, _ = next_bank()
            fns = []
            for li in range(nl):
                lt = l0 + li
                for dk in range(8):
                    fns.append(lambda lt=lt, li=li, dk=dk: pe.matmul(bk_[:, li * 128:(li + 1) * 128], lhsT=hT[:, dk, lt * 128:(lt + 1) * 128],
                                                                     rhs=Win_b[:, dk, 640:768], start=(dk == 0), stop=(dk == 7)))
            S.run("pe", fns, reads=[bhT] + bWblk["kv"], writes=[bb_])
            S.run("dve", lambda: dve.tensor_copy(out=v1[p][:, l0:l0 + nl, :, 0:64],
                                                 in_=bk_[:, 0:nl * 128].rearrange("p (l g d) -> p l g d", l=nl, g=2)),
                  reads=[bb_], writes=[bv])
            yield
        S.run("pool", lambda: pool.tensor_copy(out=v1[p][:, :, :, 64],
                                               in_=kmask_pt[:, 4 * s:4 * s + 6].unsqueeze(2).to_broadcast([128, 6, 2])),
              reads=bconsts, writes=[bv])

    def stage_b(s, part):
        p = s % 2
        hT = hTs[p]
        bhT = bhTs[p]
        chk(f"hT{p}", s)
        if part == "u":
            setown("upT", s)
        else:
            setown(f"qT{p}", s)
            setown(f"GaT{p}", s)
            setown(f"GpT{p}", s)
        bq = buf(f"qT{p}")
        bga = buf(f"GaT{p}")
        bgp = buf(f"GpT{p}")

        def proj(col, bk_, bb_):
            blk = "q" if col < 512 else ("ga" if col < 1280 else ("u" if col < 1792 else "gp"))
            S.run("pe", [lambda dk=dk: pe.matmul(bk_, lhsT=Win_b[:, dk, col:col + 128], rhs=hT[:, dk, 128:640],
                                                 start=(dk == 0), stop=(dk == 7)) for dk in range(8)],
                  reads=[bhT] + bWblk[blk], writes=[bb_])

        if part == "rest":
            for c in range(4):
                bk_, bb_, _ = next_bank()
                proj(c * 128, bk_, bb_)
                S.run("act", lambda: act.copy(out=qT[p][:, c, :], in_=bk_), reads=[bb_], writes=[bq])
                yield
            for (base, G, bg) in ((768, GaT[p], bga), (1792, GpT[p], bgp)):
                for c in range(4):
                    bk_, bb_, _ = next_bank()
                    proj(base + c * 128, bk_, bb_)
                    ti = tt_i[0] % 2
                    tt_i[0] += 1
                    S.run("act", lambda: act.activation(out=tt[:, ti, :], in_=bk_, func=AF.Tanh, scale=0.5), reads=[bb_], writes=[btt[ti]])
                    S.run("dve", lambda: dve.scalar_tensor_tensor(out=G[:, c, :], in0=tt[:, ti, :], scalar=1.0, in1=bk_,
                                                                  op0=ALU.add, op1=ALU.mult), reads=[btt[ti], bb_], writes=[bg])
                    yield
        if part == "u":
            for c in range(4):
                bk_, bb_, _ = next_bank()
                proj(1280 + c * 128, bk_, bb_)
                S.run("act", lambda: act.copy(out=upT[:, c, 8:8 + SEG_TOK], in_=bk_), reads=[bb_], writes=[bup])
                yield
            bk_, bb_, _ = next_bank()
            fns = []
            for c in range(4):
                col = 1280 + c * 128
                for dk in range(8):
                    fns.append(lambda c=c, col=col, dk=dk: pe.matmul(
                        bk_[:, c * 16:(c + 1) * 16], lhsT=Win_b[:, dk, col:col + 128],
                        rhs=bass.AP(hT, dk * SEG_EXT + 120, [[8 * SEG_EXT, 128], [520, 2], [1, 8]]),
                        start=(dk == 0), stop=(dk == 7)))
            S.run("pe", fns, reads=[bhT] + bWblk["u"], writes=[bb_])
            hv = bk_[:, 0:64].rearrange("p (c s e) -> p c s e", c=4, s=2)
            S.run("dve", lambda: dve.tensor_copy(out=upT[:, :, 0:8], in_=hv[:, :, 0, :]), reads=[bb_], writes=[bup])
            S.run("dve", lambda: dve.tensor_copy(out=upT[:, :, 8 + SEG_TOK:16 + SEG_TOK], in_=hv[:, :, 1, :]), reads=[bb_], writes=[bup])
            yield

    pT_i = [0]
    on_i = [0]

    def stage_c(s, inline_out=False):
        p = s % 2
        bq = buf(f"qT{p}")
        bk = buf(f"kT{p}")
        bv = buf(f"v1{p}")
        bga = buf(f"GaT{p}")
        o4 = ob[:, :, 0:260].rearrange("p g (c e) -> p g c e", e=65)

        def qk(qb, g):
            bks = [next_bank()[0:2] for _ in range(3)]
            for j in range(3):
                S.run("pe", lambda j=j: pe.matmul(bks[j][0], lhsT=kT[p][64 * g:64 * g + 64, (qb + j) * 128:(qb + j + 1) * 128],
                                                  rhs=qT[p][64 * g:64 * g + 64, :, qb * 128:(qb + 1) * 128], start=True, stop=True),
                      reads=[bq, bk], writes=[bks[j][1]])
            return bks

        def softmax_num(qb, g, pi, bks):
            for j in range(3):
                S.run("act", lambda j=j: act.activation(out=expS[:, j, :], in_=bks[j][0], func=AF.Exp, scale=0.125),
                      reads=[bks[j][1]], writes=[bexp[j]])
                if POOL_EMULT and (j == 2 or (j == 1 and s == NSEG - 1)):
                    S.run("pool", lambda j=j: pool.tensor_tensor(out=pT[:, pi, j, :], in0=expS[:, j, :],
                                                                 in1=E[:, j, 4 * g:4 * g + 4, :].rearrange("p h q -> p (h q)"), op=ALU.mult),
                          reads=[bexp[j], bE], writes=[bpT[pi]])
                else:
                    S.run("dve", lambda j=j: dve.tensor_tensor(out=pT[:, pi, j, :], in0=expS[:, j, :],
                                                               in1=E[:, j, 4 * g:4 * g + 4, :].rearrange("p h q -> p (h q)"), op=ALU.mult),
                          reads=[bexp[j], bE], writes=[bpT[pi]])

        def pv(qb, g, pi):
            fns = []
            for c in range(4):
                for j in range(3):
                    fns.append(lambda c=c, j=j: pe.matmul(ob[:, g, c * 65:(c + 1) * 65], lhsT=pT[:, pi, j, c * 128:(c + 1) * 128],
                                                          rhs=v1[p][:, qb + j, g, :], start=(j == 0), stop=(j == 2)))
            S.run("pe", fns, reads=[bpT[pi], bv], writes=[buf(f"ob{g}")])

        def tail_a(qb, oi):
            dcol = qb % 2
            dn = den[:, dcol, :].rearrange("p (g c) -> p g c", g=2)
            rdn = rden[:, dcol, :].rearrange("p (g c) -> p g c", g=2)
            S.run("dve", lambda: dve.tensor_tensor(out=dn, in0=o4[:, :, :, 64], in1=esink[:].rearrange("p (g c) -> p g c", g=2), op=ALU.add),
                  reads=[buf("ob0"), buf("ob1"), buf("esink")], writes=[bden])
            S.run("dve", lambda: dve.reciprocal(out=rdn, in_=dn), reads=[bden], writes=[bden])
            S.run("dve", lambda: dve.tensor_tensor(out=onorm[:, oi, :].rearrange("p (g c d) -> p g c d", g=2, c=4),
                                                   in0=o4[:, :, :, 0:64], in1=rdn.unsqueeze(3).to_broadcast([128, 2, 4, 64]), op=ALU.mult),
                  reads=[buf("ob0"), buf("ob1"), bden], writes=[bon[oi]])

        def tail_b(qb, oi):
            S.run("pe", [lambda ec=ec: pe.transpose(out=trb[:, ec, :], in_=onorm[:, oi, ec * 128:(ec + 1) * 128], identity=ident[:]) for ec in range(4)],
                  reads=[bon[oi]], writes=[btr])
            S.run("dve", lambda: dve.tensor_tensor(out=catT[:, 0:4, qb * 128:(qb + 1) * 128], in0=trb[:, 0:4, :],
                                                   in1=GaT[p][:, :, qb * 128:(qb + 1) * 128], op=ALU.mult),
                  reads=[btr, bga], writes=[bcat])

        for nm in (f"qT{p}", f"kT{p}", f"v1{p}", f"GaT{p}"):
            chk(nm, s)
        setown("catA", s)
        units = [(qb, g) for qb in range(4) for g in range(2)]
        pvq = []
        pend = []
        outq = []

        def flush_tails():
            for it in pend:
                it[2] += 1
            while pend and pend[0][2] >= 1:
                q_, o_, _ = pend.pop(0)
                tail_b(q_, o_)
                if inline_out:
                    outq.append([q_, 0])

        def do_pv():
            qb_, g_, pi_ = pvq.pop(0)
            pv(qb_, g_, pi_)
            flush_tails()
            if g_ == 1:
                oi = on_i[0] % 2
                on_i[0] += 1
                tail_a(qb_, oi)
                pend.append([qb_, oi, 0])

        ui = 0
        deferred = []
        while ui < len(units) or pvq or pend or outq or deferred:
            if ui < len(units):
                qb, g = units[ui]
                ui += 1
                pi = pT_i[0] % 3
                pT_i[0] += 1
                bks = qk(qb, g)
                softmax_num(qb, g, pi, bks)
                pvq.append((qb, g, pi))
                if len(pvq) > 2:
                    do_pv()
            elif pvq:
                do_pv()
            else:
                flush_tails()
            while deferred:
                deferred.pop(0)()
            ready = [o for o in outq if o[1] >= 1]
            for o in outq:
                o[1] += 1
            for o in ready:
                outq.remove(o)
                for _ in stage_out(s, (o[0],), defer=(deferred if ui >= len(units) else None)):
                    pass
            yield

    def pool_pre(s):
        chk("upT", s)
        L = SEG_TOK + 16
        for gi, w in enumerate(POOL_WINDOWS):
            u = upT[:, gi, :]
            S.run("pool", lambda: pool.tensor_tensor(out=ta[:, 1:L], in0=u[:, 0:L - 1], in1=u[:, 1:L], op=ALU.add), reads=[bup], writes=[bta])
            src, bsrc = ta, bta
            if w >= 4:
                S.run("pool", lambda: pool.tensor_tensor(out=tb[:, 2:L - 1], in0=ta[:, 1:L - 2], in1=ta[:, 3:L], op=ALU.add), reads=[bta], writes=[btb])
                src, bsrc = tb, btb
            if w >= 8:
                S.run("pool", lambda: pool.tensor_tensor(out=ta[:, 4:L - 3], in0=tb[:, 2:L - 5], in1=tb[:, 6:L - 1], op=ALU.add), reads=[btb], writes=[bta])
                src, bsrc = ta, bta
            if w >= 16:
                S.run("pool", lambda: pool.tensor_tensor(out=tb[:, 8:L - 7], in0=ta[:, 4:L - 11], in1=ta[:, 12:L - 3], op=ALU.add), reads=[bta], writes=[btb])
                src, bsrc = tb, btb
            yi = gi
            chk("upT", s)
            setown(f"yT{gi}", s)
            S.run("dve", lambda: dve.scalar_tensor_tensor(out=yT[:, yi, :], in0=src[:, 8:8 + SEG_TOK], scalar=1.0 / w, in1=u[:, 8:8 + SEG_TOK],
                                                          op0=ALU.mult, op1=ALU.subtract), reads=[bsrc, bup], writes=[byT[yi]])
            for (edge, seg_e, lo) in ((0, 0, 0), (1, NSEG - 1, SEG_TOK - 8)):
                if s == seg_e:
                    S.run("pool", lambda: pool.tensor_tensor(out=e8a[:], in0=src[:, 8 + lo:16 + lo], in1=rc[:, edge, gi, :], op=ALU.mult),
                          reads=[bsrc, buf("rc")], writes=[buf("e8a")])
                    S.run("pool", lambda: pool.tensor_tensor(out=yT[:, yi, lo:lo + 8], in0=e8a[:], in1=u[:, 8 + lo:16 + lo], op=ALU.subtract),
                          reads=[buf("e8a"), bup], writes=[byT[yi]])
            yield

    def pool_post(s):
        p = s % 2
        bgp = buf(f"GpT{p}")
        for gi in range(4):
            bk_, bb_, _ = next_bank()
            chk(f"yT{gi}", s)
            chk(f"GpT{p}", s)
            setown(f"catP{gi}", s)
            S.run("pe", lambda: pe.matmul(bk_, lhsT=poolw_b[:, gi, :], rhs=yT[:, gi, :], start=True, stop=True),
                  reads=[byT[gi], bpw], writes=[bb_])
            S.run("dve", lambda: dve.scalar_tensor_tensor(out=catT[:, 4 + gi, :], in0=bk_, scalar=ps[:, gi:gi + 1], in1=GpT[p][:, gi, :],
                                                          op0=ALU.mult, op1=ALU.mult), reads=[bb_, bgp] + bconsts, writes=[bcat])
            yield

    fin_i = [0]

    pending_store = []

    def flush_store():
        while pending_store:
            t_, f_ = pending_store.pop(0)
            tok = S.dma("act", ossem[f_], lambda: act.dma_start(out=y_d[t_ * 128:(t_ + 1) * 128, :], in_=os_[f_]), reads=[bos[f_]])
            final_tokens.append(tok)

    def stage_out(s, qbs=(0, 1, 2, 3), defer=None):
        for qb in qbs:
            tile_own = 4 * s + qb
            chk("catA", s)
            for gi_ in range(4):
                chk(f"catP{gi_}", s)
            assert all(b_.w is not None for b_ in bWout)
            r0_, r1_ = next_bank(), next_bank()
            mx = [r0_[0], r1_[0]]
            bmx = [r0_[1], r1_[1]]
            for half in range(2):
                fns = []
                for ek in range(8):
                    fns.append(lambda half=half, ek=ek: pe.matmul(mx[half], lhsT=catT[:, ek, qb * 128:(qb + 1) * 128],
                                                                  rhs=Wout_b[:, ek, half * 512:(half + 1) * 512], start=(ek == 0), stop=(ek == 7)))
                S.run("pe", fns, reads=[bcat] + bWout, writes=[bmx[half]])
            fi = fin_i[0] % 2
            fin_i[0] += 1
            S.dma("act", rssem[fi], lambda: act.dma_start(out=rs_[fi], in_=x_ext[(tile_own + 1) * 128:(tile_own + 2) * 128, :]),
                  writes=[brs[fi]], reads=bWout + bWin)
            c0 = 2 * tile_own
            for half in range(2):
                S.run("act", lambda half=half: act.activation(out=junk[:, 0:512], in_=mx[half], func=AF.Square, scale=1.0 / 32.0,
                                                              accum_out=ms2[:, c0 + half:c0 + half + 1]), reads=[bmx[half]], writes=[bjunk, bms2h[half]])
            flush_store()
            S.run("pool", lambda: pool.tensor_tensor(out=r2[:, tile_own:tile_own + 1], in0=ms2[:, c0:c0 + 1], in1=nhalf[:, 1:2], op=ALU.add),
                  reads=[bms2h[0], bsm], writes=[buf("stat2")])
            S.run("pool", lambda: pool.tensor_tensor(out=r2[:, tile_own:tile_own + 1], in0=r2[:, tile_own:tile_own + 1], in1=ms2[:, c0 + 1:c0 + 2], op=ALU.add),
                  reads=[buf("stat2"), bms2h[1]], writes=[buf("stat2")])
            S.run("pool", lambda: pool.tensor_tensor(out=rstd2[:, tile_own:tile_own + 1], in0=r2[:, tile_own:tile_own + 1], in1=nhalf[:, 0:1], op=ALU.pow),
                  reads=[buf("stat2"), bsm], writes=[buf("stat2")])
            def evac(qb=qb, tile_own=tile_own, fi=fi, mx=mx, bmx=bmx):
                for half in range(2):
                    S.run("dve", lambda half=half: dve.scalar_tensor_tensor(out=os_[fi][:, half * 512:(half + 1) * 512], in0=mx[half],
                                                                            scalar=rstd2[:, tile_own:tile_own + 1], in1=gpost[:, half * 512:(half + 1) * 512],
                                                                            op0=ALU.mult, op1=ALU.mult),
                          reads=[bmx[half], buf("stat2")] + bconsts, writes=[bos[fi]])
                if s >= NSEG - 2:
                    S.run("dve", lambda: dve.tensor_tensor(out=os_[fi], in0=os_[fi], in1=rs_[fi], op=ALU.add), reads=[brs[fi]], writes=[bos[fi]])
                else:
                    S.run("pool", lambda: pool.tensor_tensor(out=os_[fi], in0=os_[fi], in1=rs_[fi], op=ALU.add), reads=[brs[fi]], writes=[bos[fi]])
                pending_store.append((tile_own, fi))
            if defer is None:
                evac()
            else:
                defer.append(evac)
            yield

    def dump(name, sbuf_ap, shape, dt, reads):
        if not DEBUG:
            return
        d = nc.dram_tensor("dbg_" + name, list(shape), dt, kind="ExternalOutput").ap()
        tok = S.dma("sp", "dbg", lambda: sp.dma_start(out=d, in_=sbuf_ap), reads=reads)
        final_tokens.append(tok)
        dbg[name] = d

    W_A = [3.0] * SEG_TILES
    W_KVB = [2.0] * 4 + [2.4] * 17
    W_PRE = [4.0] * 4
    W_POST = [0.5] * 4
    W_C = [2.0] * 8 + [0.7] * 3
    W_O = [4.5] * 4

    def chain(*gens):
        for g in gens:
            yield from g

    def drain(g):
        for _ in g:
            pass

    def interleave(main, wm, fill, wf):
        tm, tf = sum(wm), sum(wf)
        pm = pf = 0.0
        im = jf = 0
        dm = df = False
        while not (dm and df):
            if not dm and (df or pm / tm <= pf / tf):
                try:
                    next(main)
                    pm += wm[min(im, len(wm) - 1)]
                    im += 1
                except StopIteration:
                    dm = True
            else:
                try:
                    next(fill)
                    pf += wf[min(jf, len(wf) - 1)]
                    jf += 1
                except StopIteration:
                    df = True

    def mix(base, extra, every):
        n = 0
        for _ in base:
            yield
            n += 1
            if n % every == 0:
                try:
                    next(extra)
                except StopIteration:
                    pass
        for _ in extra:
            pass

    W_KV = [2.0] * 4
    W_BU = [2.4] * 5
    W_BR = [2.4] * 12

    def fill_gen(s1, extra=None):
        parts, w = [], []
        if s1 < NSEG:
            base = chain(stage_kv(s1), stage_b(s1, "u"), mix(stage_b(s1, "rest"), pool_pre(s1), 3))
            w += W_KV + W_BU + W_BR
        else:
            base = iter(())
        if s1 + 1 < NSEG:
            base = mix(base, g_a(s1 + 1), 3) if s1 < NSEG else g_a(s1 + 1)
        if extra is not None:
            base = mix(base, extra, 2)
        return base, (w if w else [1.0])

    def step(g):
        try:
            next(g)
            return True
        except StopIteration:
            return False

    ga0 = g_a(0)
    cj = [0]

    def casts(n):
        for _ in range(n):
            if cj[0] < len(wjobs):
                cast_w_in(cj[0])
                cj[0] += 1

    casts(2)
    while step(ga0):
        casts(1)
    S.run("dve", lambda: dve.reciprocal(out=rc[:], in_=rc[:]), reads=[buf("rc")], writes=[buf("rc")])
    kv0, bu0, br0 = stage_kv(0), stage_b(0, "u"), stage_b(0, "rest")
    pp0, ga1, wg = pool_pre(0), g_a(1), wout_gen()
    casts(10 - cj[0])
    drain(kv0)
    while step(bu0):
        step(ga1)
    for i_ in range(12):
        if i_ < 8:
            casts(1)
        step(br0)
        step(ga1)
        if i_ >= 8:
            step(wg)
        if i_ % 3 == 2:
            step(pp0)
    drain(pp0)
    drain(ga1)
    finish_E()
    if DEBUG:
        dump("hT", hTs[0][:], [128, 8, SEG_EXT], BF16, [bhTs[0]])
        dump("E", E[:], [128, 3, 8, 128], F32, [bE])
        dump("rc", rc[:], [128, 2, 4, 8], F32, [buf("rc")])
        dump("qT", qT[0][:], [128, 4, SEG_TOK], BF16, [buf("qT0")])
        dump("kT", kT[0][:], [128, SEG_EXT], BF16, [buf("kT0")])
        dump("v1", v1[0][:], [128, SEG_TILES, 2, 65], BF16, [buf("v10")])
        dump("GaT", GaT[0][:], [128, 4, SEG_TOK], BF16, [buf("GaT0")])
        dump("GpT", GpT[0][:], [128, 4, SEG_TOK], BF16, [buf("GpT0")])
        dump("upT", upT[:], [128, 4, SEG_TOK + 16], F32, [bup])
    for s in range(NSEG):
        if s + 1 < NSEG:
            main = chain(stage_c(s), pool_post(s), stage_out(s))
            wm = W_C + W_POST + W_O
        else:
            main = mix(stage_c(s, inline_out=True), pool_post(s), 1)
            wm = W_C
        if s + 1 < NSEG:
            f, wf = fill_gen(s + 1)
            if s == 0:
                f = mix(f, wg, 1)
            interleave(main, wm, f, wf)
        else:
            drain(main)

    flush_store()
    for tok in final_tokens:
        S.wait("act", tok)
    return nc


def _t5_bucket_np(rel):
    nb = 16
    max_exact = 8
    ret = np.where(rel > 0, nb, 0)
    n = np.abs(rel)
    nf = np.maximum(n, 1).astype(np.float32)
    val = (np.log(nf / np.float32(max_exact)) / np.float32(math.log(128 / max_exact)) * np.float32(nb - max_exact)).astype(np.float32)
    large = max_exact + val.astype(np.int32)
    large = np.minimum(large, nb - 1)
    return ret + np.where(n < max_exact, n, large)


def _onehot_table():
    oh = np.zeros((32, 765), np.float32)
    for j in range(3):
        u = np.arange(255)
        rel = (j - 1) * 128 + (u - 127)
        b = _t5_bucket_np(rel)
        oh[b, j * 255 + u] = 1.0
    return oh


_NC_CACHE = {}


def kernel(x, pre_norm_g, w_in, rel_bias, attn_sink, pool_w, pool_scale, w_out, post_norm_g):
    x = np.asarray(x, np.float32)
    xs2 = x.reshape(SEQ, D)
    xpad = np.zeros((SEQ + 256, D), np.float32)
    xpad[128:128 + SEQ] = xs2
    mask_full = np.zeros((SEQ + 256,), np.float32)
    mask_full[128:128 + SEQ] = 1.0
    shared = {
        "gpre": np.ascontiguousarray(np.asarray(pre_norm_g, np.float32).reshape(8, 128).T),
        "w_in": np.ascontiguousarray(np.asarray(w_in, np.float32).reshape(D, D_IN)),
        "rel_bias": np.ascontiguousarray(np.asarray(rel_bias, np.float32)),
        "onehot": _onehot_table(),
        "attn_sink": np.ascontiguousarray(np.asarray(attn_sink, np.float32).reshape(8)),
        "pool_w": np.ascontiguousarray(np.asarray(pool_w, np.float32).reshape(4, 128, 128)),
        "pool_scale_pg": np.ascontiguousarray(np.asarray(pool_scale, np.float32).reshape(4, 128).T),
        "w_out": np.ascontiguousarray(np.asarray(w_out, np.float32).reshape(D, D)),
        "post_norm_g": np.ascontiguousarray(np.asarray(post_norm_g, np.float32).reshape(D)),
    }
    in_maps = []
    for c in range(N_CORES):
        lo = c * TOK_CORE
        m = dict(shared)
        m["x_ext"] = np.ascontiguousarray(xpad[lo:lo + EXT])
        km = mask_full[lo:lo + EXT]
        m["kmask_row"] = np.ascontiguousarray(km)
        m["kmask_pt"] = np.ascontiguousarray(km.reshape(18, 128).T)
        in_maps.append(m)
    if "nc" not in _NC_CACHE:
        _NC_CACHE["nc"] = build_nc()
    nc = _NC_CACHE["nc"]
    res = run_bass_kernel_spmd(nc, in_maps, core_ids=list(range(N_CORES)))
    out = np.concatenate([np.asarray(r["y"], np.float32) for r in res.results], axis=0)
    if DEBUG:
        kernel.debug = [{k: v for k, v in r.items() if k.startswith("dbg_")} for r in res.results]
    return out.reshape(1, SEQ, D)
```

```python
import math
import os
import numpy as np
import concourse.bass as bass
import concourse.mybir as mybir
from concourse.bass_utils import run_bass_kernel_spmd

F32 = mybir.dt.float32
BF16 = mybir.dt.bfloat16
AF = mybir.ActivationFunctionType
ALU = mybir.AluOpType

N_CORES = 8
SEQ = 16384
D = 1024
D_IN = 2304
TOK_CORE = SEQ // N_CORES
EXT = TOK_CORE + 256
NSEG = 4
SEG_TOK = 512
SEG_TILES = 6
SEG_EXT = SEG_TILES * 128
EPS = 1e-6
POOL_WINDOWS = (2, 4, 8, 16)

DEBUG = bool(int(os.environ.get("MK_DEBUG", "0")))
POOL_EMULT = True


class Buf:
    __slots__ = ("name", "w", "r")

    def __init__(self, name):
        self.name = name
        self.w = None
        self.r = []


class Sched:
    def __init__(self, nc):
        self.nc = nc
        self.eng = {"pe": nc.tensor, "act": nc.scalar, "dve": nc.vector, "pool": nc.gpsimd, "sp": nc.sync}
        self._ctx = []
        self.semobj = {}
        self.cnt = {}
        self.seen = {k: {} for k in self.eng}
        for k in ("pe", "act", "dve", "pool"):
            self.new_sem(k)

    def new_sem(self, key):
        c = self.nc.semaphore("s_" + key)
        s = c.__enter__()
        self._ctx.append(c)
        self.semobj[key] = s
        self.cnt[key] = 0
        return key

    def wait(self, e, tok):
        if tok is None:
            return
        key, val = tok
        if key == "pe" and e == "pe":
            return
        if self.seen[e].get(key, 0) >= val:
            return
        self.eng[e].wait_ge(self.semobj[key], val)
        self.seen[e][key] = val

    def _deps(self, e, reads, writes):
        for b in reads:
            assert b.w is not None, f"read of never-written buffer {b.name}"
            self.wait(e, b.w)
        for b in writes:
            self.wait(e, b.w)
            for t in b.r:
                self.wait(e, t)

    def _commit(self, tok, reads, writes):
        for b in reads:
            b.r.append(tok)
        for b in writes:
            b.w = tok
            b.r = []

    def run(self, e, fns, reads=(), writes=()):
        self._deps(e, reads, writes)
        if not isinstance(fns, (list, tuple)):
            fns = [fns]
        inst = None
        for f in fns:
            inst = f()
        self.cnt[e] += 1
        inst.then_inc(self.semobj[e], 1)
        tok = (e, self.cnt[e])
        self._commit(tok, reads, writes)
        return tok

    def dma_group(self, q, semkey, fns, buf):
        assert self.cnt[semkey] == 0 and buf.w is None
        for f in fns:
            f().then_inc(self.semobj[semkey], 16)
            self.cnt[semkey] += 16
        buf.w = (semkey, self.cnt[semkey])

    def dma(self, q, semkey, fn, reads=(), writes=()):
        self._deps(q, reads, writes)
        inst = fn()
        self.cnt[semkey] += 16
        inst.then_inc(self.semobj[semkey], 16)
        tok = (semkey, self.cnt[semkey])
        self._commit(tok, reads, writes)
        return tok


def build_nc():
    nc = bass.Bass("TRN2", target_bir_lowering=False)
    S = Sched(nc)
    dbg = {}

    def din(name, shape):
        return nc.dram_tensor(name, list(shape), F32, kind="ExternalInput").ap()

    x_ext = din("x_ext", [EXT, D])
    kmask_pt_d = din("kmask_pt", [128, 18])
    kmask_row_d = din("kmask_row", [EXT])
    gpre_d = din("gpre", [128, 8])
    w_in_d = din("w_in", [D, D_IN])
    relb_d = din("rel_bias", [32, 8])
    onehot_d = din("onehot", [32, 765])
    sink_d = din("attn_sink", [8])
    poolw_d = din("pool_w", [4, 128, 128])
    ps_d = din("pool_scale_pg", [128, 4])
    w_out_d = din("w_out", [D, D])
    gpost_d = din("post_norm_g", [D])
    y_d = nc.dram_tensor("y", [TOK_CORE, D], F32, kind="ExternalOutput").ap()
    tab_scr = nc.dram_tensor("tab_scr", [8, 765], F32, kind="Internal").ap()

    def sb(name, shape, dt):
        return nc.alloc_sbuf_tensor("sb_" + name, list(shape), dt)

    Win_b = sb("Win_b", [128, 8, D_IN], BF16)
    Wout_b = sb("Wout_b", [128, 8, D], BF16)
    poolw_b = sb("poolw_b", [128, 4, 128], BF16)
    E = sb("E", [128, 3, 8, 128], F32)
    gpost = sb("gpost", [128, D], F32)
    ident = sb("ident", [128, 128], BF16)
    identf = sb("identf", [128, 128], F32)
    gpre = sb("gpre", [128, 8], F32)
    ps = sb("ps", [128, 4], F32)
    esink = sb("esink", [128, 8], F32)
    kmask_pt = sb("kmask_pt", [128, 18], F32)
    relb = sb("relb", [32, 8], F32)
    nhalf = sb("nhalf", [128, 2], F32)
    ms = sb("ms", [128, 32], F32)
    r0 = sb("r0", [128, 32], F32)
    rstd = sb("rstd", [128, 32], F32)
    ms2 = sb("ms2", [128, 32], F32)
    r2 = sb("r2", [128, 16], F32)
    rstd2 = sb("rstd2", [128, 16], F32)
    den = sb("den", [128, 2, 8], F32)
    rden = sb("rden", [128, 2, 8], F32)
    mrow = sb("mrow", [128, 2, 24], F32)
    mta = sb("mta", [128, 2, 24], F32)
    mtb = sb("mtb", [128, 2, 24], F32)
    rc = sb("rc", [128, 2, 4, 8], F32)
    e8a = sb("e8a", [128, 8], F32)
    S0 = sb("S0", [128, 2304], F32)
    S1 = sb("S1", [128, 2304], F32)
    hTs = [sb(f"hT{i}", [128, 8, SEG_EXT], BF16) for i in range(2)]
    qT = [sb(f"qT{i}", [128, 4, SEG_TOK], BF16) for i in range(2)]
    kT = [sb(f"kT{i}", [128, SEG_EXT], BF16) for i in range(2)]
    v1 = [sb(f"v1{i}", [128, SEG_TILES, 2, 65], BF16) for i in range(2)]
    GaT = [sb(f"GaT{i}", [128, 4, SEG_TOK], BF16) for i in range(2)]
    GpT = [sb(f"GpT{i}", [128, 4, SEG_TOK], BF16) for i in range(2)]
    upT = sb("upT", [128, 4, SEG_TOK + 16], F32)
    catT = sb("catT", [128, 8, SEG_TOK], BF16)
    hk = catT[:].rearrange("p a b -> p (a b)").bitcast(F32).rearrange("p (s h c) -> p s h c", s=2, h=8)
    xs = sb("xs", [128, 3, D], F32)
    xb = sb("xb", [128, 2, D], BF16)
    junk = sb("junk", [128, D], BF16)
    expS = sb("expS", [128, 3, 512], F32)
    pT = sb("pT", [128, 3, 3, 512], BF16)
    expS_flat = expS[:].rearrange("p a b -> p (a b)")
    ohs = expS_flat[0:32, 0:765]
    tabE = expS_flat[0:8, 768:768 + 765]
    onorm = sb("onorm", [128, 2, 512], BF16)
    ta = sb("ta", [128, SEG_TOK + 16], F32)
    poolw_f = ta[:, 0:512].rearrange("p (g d) -> p g d", g=4)
    tb = sb("tb", [128, SEG_TOK + 16], F32)
    yT = sb("yT", [128, 4, SEG_TOK], BF16)
    trb = nc.alloc_psum_tensor("trb", [128, 8, 128], BF16)
    acc = nc.alloc_psum_tensor("acc", [128, 2, 512], F32)
    sT = nc.alloc_psum_tensor("sT", [128, 3, 512], F32)
    ob = nc.alloc_psum_tensor("ob", [128, 2, 512], F32)

    B = {}

    def buf(name):
        if name not in B:
            B[name] = Buf(name)
        return B[name]

    for k in ("consts", "ld_e", "ld_w0", "ld_w1", "ld_w2", "ld_w3", "xs0", "xs1", "xs2", "rs0", "rs1", "os0", "os1", "dbg", "scr", "hk0", "hk1"):
        S.new_sem(k)

    sp = nc.sync
    boh = buf("onehot")
    S.dma_group("sp", "ld_e", [
        lambda: sp.dma_start(out=relb[:], in_=relb_d[:, :]),
        lambda: sp.dma_start(out=ohs, in_=onehot_d[:, :]),
    ], boh)
    bc_ = buf("consts")
    bconsts = [bc_]
    S.dma_group("sp", "consts", [
        lambda: sp.dma_start(out=gpre[:], in_=gpre_d[:, :]),
        lambda: sp.dma_start(out=kmask_pt[:], in_=kmask_pt_d[:, :]),
        lambda: sp.dma_start(out=ps[:], in_=ps_d[:, :]),
        lambda: sp.dma_start(out=esink[:], in_=bass.AP(sink_d.tensor, 0, [[0, 128], [1, 8]])),
        lambda: sp.dma_start(out=gpost[:], in_=bass.AP(gpost_d.tensor, 0, [[0, 128], [1, D]])),
        lambda: sp.dma_start(out=mrow[:, 0, :], in_=bass.AP(kmask_row_d.tensor, 120, [[0, 128], [1, 24]])),
        lambda: sp.dma_start(out=mrow[:, 1, :], in_=bass.AP(kmask_row_d.tensor, 120 + 1536 + 504, [[0, 128], [1, 24]])),
        lambda: sp.dma_start(out=poolw_f, in_=poolw_d.rearrange("g c d -> c g d")),
    ], bc_)

    bS = [buf(f"S{i}") for i in range(4)]
    Sst = [S0[:, 0:1152], S0[:, 1152:2304], S1[:, 0:1152], S1[:, 1152:2304]]
    ldw = ["ld_w0", "ld_w1", "ld_w2", "ld_w3"]
    wjob = [0]

    WBLK = [("kv", 512, 256), ("u", 1280, 512), ("q", 0, 512), ("ga", 768, 512), ("gp", 1792, 512)]
    wjobs = []
    for (nm_, c0_, nc_) in WBLK:
        ndk_ = 4 if nc_ == 256 else 2
        for dk0_ in range(0, 8, ndk_):
            wjobs.append((nm_, c0_, nc_, dk0_, ndk_))
    bWblk = {nm_: [buf(f"W_{nm_}{dk}") for dk in range(8)] for (nm_, _, _) in WBLK}

    def load_w_in(job):
        if job >= len(wjobs):
            return
        nm_, c0_, nc_, dk0_, ndk_ = wjobs[job]
        sl = job % 4
        S.dma("sp", ldw[sl], lambda: sp.dma_start(
            out=Sst[sl][:, 0:ndk_ * nc_].rearrange("p (k c) -> p k c", k=ndk_),
            in_=w_in_d[dk0_ * 128:(dk0_ + ndk_) * 128, c0_:c0_ + nc_].rearrange("(k p) c -> p k c", p=128)),
            writes=[bS[sl]])

    jobs = [(s_, lt_) for s_ in range(NSEG) for lt_ in range(SEG_TILES)]
    xsem = ["xs0", "xs1", "xs2"]
    bxs = [buf("xs0"), buf("xs1"), buf("xs2")]

    def a_load(k):
        if k >= len(jobs):
            return
        s_, lt_ = jobs[k]
        et = 4 * s_ + lt_
        i = k % 3
        if k < 3:
            S.dma("sp", xsem[i], lambda: sp.dma_start(out=xs[:, i, :], in_=x_ext[et * 128:(et + 1) * 128, :]), writes=[bxs[i]])
        else:
            S.dma("act", xsem[i], lambda: act.dma_start(out=xs[:, i, :], in_=x_ext[et * 128:(et + 1) * 128, :]), writes=[bxs[i]])

    a_load(0)
    a_load(1)
    a_load(2)
    for job_ in range(4):
        load_w_in(job_)

    bident = buf("ident")
    pool = nc.gpsimd
    S.run("pool", lambda: pool.memset(identf[:], 0.0), writes=[bident])
    S.run("pool", lambda: pool.affine_select(out=identf[:], in_=identf[:], pattern=[[-1, 128]], compare_op=ALU.not_equal,
                                             fill=1.0, base=0, channel_multiplier=1), writes=[bident])
    S.run("pool", lambda: pool.tensor_copy(out=ident[:], in_=identf[:]), reads=[bident], writes=[buf("identb")])
    bsm = buf("smalls")
    S.run("pool", lambda: pool.memset(nhalf[:, 0:1], -0.5), writes=[bsm])
    S.run("pool", lambda: pool.memset(nhalf[:, 1:2], -1.0), writes=[bsm])
    S.run("pool", lambda: pool.memset(ms[:], 0.0), writes=[buf("ms")])
    S.run("pool", lambda: pool.memset(ms2[:], 0.0), writes=[buf("ms2")])

    bE = buf("E")
    bsT = [buf("sT0"), buf("sT1"), buf("sT2")]
    tt = sb("tt", [128, 2, 512], F32)
    pe = nc.tensor
    act = nc.scalar
    dve = nc.vector
    bexp_all = [buf("exp0"), buf("exp1"), buf("exp2")]
    S.run("pe", lambda: pe.matmul(sT[0:8, 0, 0:510], lhsT=relb[:, :], rhs=ohs[:, 0:510], start=True, stop=True),
          reads=[boh], writes=[bsT[0]])
    S.run("pe", lambda: pe.matmul(sT[0:8, 1, 0:255], lhsT=relb[:, :], rhs=ohs[:, 510:765], start=True, stop=True),
          reads=[boh], writes=[bsT[1]])
    btab = buf("tabE")
    S.run("act", lambda: act.activation(out=tabE[:, 0:510], in_=sT[0:8, 0, 0:510], func=AF.Exp), reads=[bsT[0]], writes=[btab])
    S.run("act", lambda: act.activation(out=tabE[:, 510:765], in_=sT[0:8, 1, 0:255], func=AF.Exp), reads=[bsT[1]], writes=[btab])
    S.run("act", lambda: act.activation(out=esink[:], in_=esink[:], func=AF.Exp), reads=bconsts, writes=[buf("esink")])
    S.run("pool", lambda: pool.memset(tabE[:, 0:127], 0.0), writes=[btab])
    S.run("pool", lambda: pool.memset(tabE[:, 510 + 128:765], 0.0), writes=[btab])
    bscr = buf("scr")
    S.dma("act", "scr", lambda: act.dma_start(out=tab_scr[:, :], in_=tabE), reads=[btab], writes=[bscr])
    bhk = [buf("hk0"), buf("hk1")]

    def hank_dma(j):
        S.dma("act", f"hk{j % 2}", lambda: act.dma_start(
            out=hk[:, j % 2, :, :], in_=bass.AP(tab_scr.tensor, j * 255, [[1, 128], [765, 8], [1, 128]])),
            reads=[bscr], writes=[bhk[j % 2]])

    def hank_rev(j):
        S.run("pool", lambda: pool.tensor_copy(out=E[:, j, :, :], in_=hk[:, j % 2, :, ::-1]), reads=[bhk[j % 2]], writes=[bE])

    hank_dma(0)
    hank_dma(1)

    def finish_E():
        hank_rev(0)
        hank_dma(2)
        hank_rev(1)
        hank_rev(2)
        bcat.r.append(bE.w)
        for b_ in bexp_all:
            b_.r.append(bscr.w)
            b_.r.extend(boh.r)

    bm = buf("medge")
    S.run("pool", lambda: pool.tensor_tensor(out=mta[:, :, 1:24], in0=mrow[:, :, 0:23], in1=mrow[:, :, 1:24], op=ALU.add), reads=bconsts, writes=[bm])
    S.run("pool", lambda: pool.tensor_copy(out=rc[:, :, 0, :], in_=mta[:, :, 8:16]), reads=[bm], writes=[buf("rc")])
    S.run("pool", lambda: pool.tensor_tensor(out=mtb[:, :, 2:23], in0=mta[:, :, 1:22], in1=mta[:, :, 3:24], op=ALU.add), reads=[bm], writes=[bm])
    S.run("pool", lambda: pool.tensor_copy(out=rc[:, :, 1, :], in_=mtb[:, :, 8:16]), reads=[bm], writes=[buf("rc")])
    S.run("pool", lambda: pool.tensor_tensor(out=mta[:, :, 4:21], in0=mtb[:, :, 2:19], in1=mtb[:, :, 6:23], op=ALU.add), reads=[bm], writes=[bm])
    S.run("pool", lambda: pool.tensor_copy(out=rc[:, :, 2, :], in_=mta[:, :, 8:16]), reads=[bm], writes=[buf("rc")])
    S.run("pool", lambda: pool.tensor_tensor(out=mtb[:, :, 8:17], in0=mta[:, :, 4:13], in1=mta[:, :, 12:21], op=ALU.add), reads=[bm], writes=[bm])
    S.run("pool", lambda: pool.tensor_copy(out=rc[:, :, 3, :], in_=mtb[:, :, 8:16]), reads=[bm], writes=[buf("rc")])

    bWin = [b_ for nm_ in bWblk for b_ in bWblk[nm_]]
    cast_i = [0]

    def cast_w_in(job):
        nm_, c0_, nc_, dk0_, ndk_ = wjobs[job]
        sl = job % 4
        st = Sst[sl][:, 0:ndk_ * nc_].rearrange("p (k c) -> p k c", k=ndk_)
        for kk in range(ndk_):
            dk = dk0_ + kk
            e_ = "dve" if cast_i[0] % 2 == 0 else "act"
            cast_i[0] += 1
            g_ = gpre[:, dk:dk + 1]
            if nm_ == "q":
                out_ap = Win_b[:, dk, 0:512].rearrange("p (c hf d) -> p c hf d", c=4, hf=2)
                in_ap = st[:, kk, :].rearrange("p (hf c d) -> p c hf d", hf=2, c=4)
            else:
                out_ap = Win_b[:, dk, c0_:c0_ + nc_]
                in_ap = st[:, kk, :]
            if e_ == "dve":
                S.run("dve", lambda: dve.tensor_scalar(out=out_ap, in0=in_ap, scalar1=g_, scalar2=None, op0=ALU.mult),
                      reads=[bS[sl]] + bconsts, writes=[bWblk[nm_][dk]])
            else:
                S.run("act", lambda: act.activation(out=out_ap, in_=in_ap, func=AF.Copy, scale=g_),
                      reads=[bS[sl]] + bconsts, writes=[bWblk[nm_][dk]])
        load_w_in(job + 4)

    bWout = [buf(f"Wout{ek}") for ek in range(8)]

    def wout_load(ek):
        sl = ek % 4
        S.dma("sp", ldw[sl], lambda: sp.dma_start(out=Sst[sl][:, 0:D], in_=w_out_d[ek * 128:(ek + 1) * 128, :]), writes=[bS[sl]])

    def wout_gen():
        for ek in range(4):
            wout_load(ek)
        for ek in range(8):
            sl = ek % 4
            S.run("act", lambda: act.activation(out=Wout_b[:, ek, :], in_=Sst[sl][:, 0:D], func=AF.Copy, scale=0.5),
                  reads=[bS[sl]], writes=[bWout[ek]])
            if ek + 4 < 8:
                wout_load(ek + 4)
            yield

    bpw = buf("poolw")
    S.run("pool", lambda: pool.tensor_copy(out=poolw_b[:], in_=poolw_f), reads=bconsts, writes=[bpw])
    buf("ta").r.append(bpw.w)

    bxb = [buf("xb0"), buf("xb1")]
    btr = buf("trb")
    bhTs = [buf("hT0"), buf("hT1")]
    bacc = [buf("acc0"), buf("acc1")]
    bob = buf("ob")
    btt = [buf("tt0"), buf("tt1")]
    bexp = [buf("exp0"), buf("exp1"), buf("exp2")]
    bpT = [buf("pT0"), buf("pT1"), buf("pT2")]
    bon = [buf("on0"), buf("on1")]
    bjunk = buf("junk")
    bms = buf("ms")
    bms2 = buf("ms2")
    bstat = buf("stat")
    bup = buf("upT")
    bcat = buf("catT")
    bta = buf("ta")
    btb = buf("tb")
    byT = [buf(f"yT{i}") for i in range(4)]
    bden = buf("den")
    os_ = [S0[:, 0:D], S0[:, D:2 * D]]
    rs_ = [S1[:, 0:D], S1[:, D:2 * D]]
    bos = [buf("os0"), buf("os1")]
    brs = [buf("rs0"), buf("rs1")]
    ossem = ["os0", "os1"]
    rssem = ["rs0", "rs1"]

    own = {}

    def setown(name, seg):
        own[name] = seg

    def chk(name, seg):
        assert own.get(name) == seg, f"ordering bug: {name} holds segment {own.get(name)}, expected {seg}"

    xs_i = [0]
    acc_i = [0]
    tt_i = [0]
    final_tokens = []

    ring = [(acc[:, 0, :], bacc[0]), (acc[:, 1, :], bacc[1]), (sT[:, 0, :], bsT[0]), (sT[:, 1, :], bsT[1]), (sT[:, 2, :], bsT[2])]

    reserved = set()

    def next_bank(reserve=False):
        for _ in range(len(ring)):
            i = acc_i[0] % len(ring)
            acc_i[0] += 1
            if i not in reserved:
                if reserve:
                    reserved.add(i)
                return ring[i] + (i,)
        raise AssertionError("no free PSUM ring bank")

    def stage_a1(s, lt):
        k = s * SEG_TILES + lt
        i = k % 3
        col = k
        S.run("act", lambda: act.activation(out=junk[:], in_=xs[:, i, :], func=AF.Square, scale=1.0 / 32.0,
                                            accum_out=ms[:, col:col + 1]), reads=[bxs[i], bms], writes=[bjunk, buf(f"ms_{k}")])
        S.run("pool", lambda: pool.tensor_scalar(out=r0[:, col:col + 1], in0=ms[:, col:col + 1], scalar1=EPS, scalar2=None, op0=ALU.add),
              reads=[buf(f"ms_{k}")], writes=[buf(f"stat_{k}")])
        S.run("pool", lambda: pool.tensor_tensor(out=rstd[:, col:col + 1], in0=r0[:, col:col + 1], in1=nhalf[:, 0:1], op=ALU.pow),
              reads=[buf(f"stat_{k}"), bsm], writes=[buf(f"stat_{k}")])

    def stage_a2(s, lt):
        setown(f"hT{s % 2}", s)
        k = s * SEG_TILES + lt
        i = k % 3
        col = k
        j = col % 2
        if s == 0:
            S.run("dve", lambda: dve.tensor_scalar(out=xb[:, j, :], in0=xs[:, i, :], scalar1=rstd[:, col:col + 1], scalar2=None, op0=ALU.mult),
                  reads=[buf(f"stat_{k}"), bxs[i]], writes=[bxb[j]])
        else:
            S.run("act", lambda: act.activation(out=xb[:, j, :], in_=xs[:, i, :], func=AF.Copy, scale=rstd[:, col:col + 1]),
                  reads=[buf(f"stat_{k}"), bxs[i]], writes=[bxb[j]])
        a_load(k + 3)

    def stage_a3(s, lt):
        k = s * SEG_TILES + lt
        j = k % 2
        S.run("pe", [lambda dk=dk: pe.transpose(out=trb[:, dk, :], in_=xb[:, j, dk * 128:(dk + 1) * 128], identity=ident[:]) for dk in range(8)],
              reads=[bxb[j], buf("identb")], writes=[btr])
        S.run("dve", lambda: dve.tensor_copy(out=hTs[s % 2][:, :, lt * 128:(lt + 1) * 128], in_=trb[:, :, :]), reads=[btr], writes=[bhTs[s % 2]])

    def g_a(s):
        stage_a1(s, 0)
        for lt in range(SEG_TILES):
            if lt + 1 < SEG_TILES:
                stage_a1(s, lt + 1)
            stage_a2(s, lt)
            if lt >= 1:
                stage_a3(s, lt - 1)
            yield
        stage_a3(s, SEG_TILES - 1)
        yield

    def stage_kv(s):
        p = s % 2
        hT = hTs[p]
        bhT = bhTs[p]
        chk(f"hT{p}", s)
        setown(f"kT{p}", s)
        setown(f"v1{p}", s)
        bk = buf(f"kT{p}")
        bv = buf(f"v1{p}")
        for (t0, n) in ((0, 512), (512, 256)):
            bk_, bb_, _ = next_bank()
            S.run("pe", [lambda dk=dk: pe.matmul(bk_[:, 0:n], lhsT=Win_b[:, dk, 512:640], rhs=hT[:, dk, t0:t0 + n],
                                                 start=(dk == 0), stop=(dk == 7)) for dk in range(8)],
                  reads=[bhT] + bWblk["kv"], writes=[bb_])
            S.run("act", lambda: act.copy(out=kT[p][:, t0:t0 + n], in_=bk_[:, 0:n]), reads=[bb_], writes=[bk])
            yield
        for (l0, nl) in ((0, 4), (4, 2)):
            bk_, bb_, _ = next_bank()
            fns = []
            for li in range(nl):
                lt = l0 + li
                for dk in range(8):
                    fns.append(lambda lt=lt, li=li, dk=dk: pe.matmul(bk_[:, li * 128:(li + 1) * 128], lhsT=hT[:, dk, lt * 128:(lt + 1) * 128],
                                                                     rhs=Win_b[:, dk, 640:768], start=(dk == 0), stop=(dk == 7)))
            S.run("pe", fns, reads=[bhT] + bWblk["kv"], writes=[bb_])
            S.run("dve", lambda: dve.tensor_copy(out=v1[p][:, l0:l0 + nl, :, 0:64],
                                                 in_=bk_[:, 0:nl * 128].rearrange("p (l g d) -> p l g d", l=nl, g=2)),
                  reads=[bb_], writes=[bv])
            yield
        S.run("pool", lambda: pool.tensor_copy(out=v1[p][:, :, :, 64],
                                               in_=kmask_pt[:, 4 * s:4 * s + 6].unsqueeze(2).to_broadcast([128, 6, 2])),
              reads=bconsts, writes=[bv])

    def stage_b(s, part):
        p = s % 2
        hT = hTs[p]
        bhT = bhTs[p]
        chk(f"hT{p}", s)
        if part == "u":
            setown("upT", s)
        else:
            setown(f"qT{p}", s)
            setown(f"GaT{p}", s)
            setown(f"GpT{p}", s)
        bq = buf(f"qT{p}")
        bga = buf(f"GaT{p}")
        bgp = buf(f"GpT{p}")

        def proj(col, bk_, bb_):
            blk = "q" if col < 512 else ("ga" if col < 1280 else ("u" if col < 1792 else "gp"))
            S.run("pe", [lambda dk=dk: pe.matmul(bk_, lhsT=Win_b[:, dk, col:col + 128], rhs=hT[:, dk, 128:640],
                                                 start=(dk == 0), stop=(dk == 7)) for dk in range(8)],
                  reads=[bhT] + bWblk[blk], writes=[bb_])

        if part == "rest":
            for c in range(4):
                bk_, bb_, _ = next_bank()
                proj(c * 128, bk_, bb_)
                S.run("act", lambda: act.copy(out=qT[p][:, c, :], in_=bk_), reads=[bb_], writes=[bq])
                yield
            for (base, G, bg) in ((768, GaT[p], bga), (1792, GpT[p], bgp)):
                for c in range(4):
                    bk_, bb_, _ = next_bank()
                    proj(base + c * 128, bk_, bb_)
                    ti = tt_i[0] % 2
                    tt_i[0] += 1
                    S.run("act", lambda: act.activation(out=tt[:, ti, :], in_=bk_, func=AF.Tanh, scale=0.5), reads=[bb_], writes=[btt[ti]])
                    S.run("dve", lambda: dve.scalar_tensor_tensor(out=G[:, c, :], in0=tt[:, ti, :], scalar=1.0, in1=bk_,
                                                                  op0=ALU.add, op1=ALU.mult), reads=[btt[ti], bb_], writes=[bg])
                    yield
        if part == "u":
            for c in range(4):
                bk_, bb_, _ = next_bank()
                proj(1280 + c * 128, bk_, bb_)
                S.run("act", lambda: act.copy(out=upT[:, c, 8:8 + SEG_TOK], in_=bk_), reads=[bb_], writes=[bup])
                yield
            bk_, bb_, _ = next_bank()
            fns = []
            for c in range(4):
                col = 1280 + c * 128
                for dk in range(8):
                    fns.append(lambda c=c, col=col, dk=dk: pe.matmul(
                        bk_[:, c * 16:(c + 1) * 16], lhsT=Win_b[:, dk, col:col + 128],
                        rhs=bass.AP(hT, dk * SEG_EXT + 120, [[8 * SEG_EXT, 128], [520, 2], [1, 8]]),
                        start=(dk == 0), stop=(dk == 7)))
            S.run("pe", fns, reads=[bhT] + bWblk["u"], writes=[bb_])
            hv = bk_[:, 0:64].rearrange("p (c s e) -> p c s e", c=4, s=2)
            S.run("dve", lambda: dve.tensor_copy(out=upT[:, :, 0:8], in_=hv[:, :, 0, :]), reads=[bb_], writes=[bup])
            S.run("dve", lambda: dve.tensor_copy(out=upT[:, :, 8 + SEG_TOK:16 + SEG_TOK], in_=hv[:, :, 1, :]), reads=[bb_], writes=[bup])
            yield

    pT_i = [0]
    on_i = [0]

    def stage_c(s, inline_out=False):
        p = s % 2
        bq = buf(f"qT{p}")
        bk = buf(f"kT{p}")
        bv = buf(f"v1{p}")
        bga = buf(f"GaT{p}")
        o4 = ob[:, :, 0:260].rearrange("p g (c e) -> p g c e", e=65)

        def qk(qb, g):
            bks = [next_bank()[0:2] for _ in range(3)]
            for j in range(3):
                S.run("pe", lambda j=j: pe.matmul(bks[j][0], lhsT=kT[p][64 * g:64 * g + 64, (qb + j) * 128:(qb + j + 1) * 128],
                                                  rhs=qT[p][64 * g:64 * g + 64, :, qb * 128:(qb + 1) * 128], start=True, stop=True),
                      reads=[bq, bk], writes=[bks[j][1]])
            return bks

        def softmax_num(qb, g, pi, bks):
            for j in range(3):
                S.run("act", lambda j=j: act.activation(out=expS[:, j, :], in_=bks[j][0], func=AF.Exp, scale=0.125),
                      reads=[bks[j][1]], writes=[bexp[j]])
                if POOL_EMULT and (j == 2 or (j == 1 and s == NSEG - 1)):
                    S.run("pool", lambda j=j: pool.tensor_tensor(out=pT[:, pi, j, :], in0=expS[:, j, :],
                                                                 in1=E[:, j, 4 * g:4 * g + 4, :].rearrange("p h q -> p (h q)"), op=ALU.mult),
                          reads=[bexp[j], bE], writes=[bpT[pi]])
                else:
                    S.run("dve", lambda j=j: dve.tensor_tensor(out=pT[:, pi, j, :], in0=expS[:, j, :],
                                                               in1=E[:, j, 4 * g:4 * g + 4, :].rearrange("p h q -> p (h q)"), op=ALU.mult),
                          reads=[bexp[j], bE], writes=[bpT[pi]])

        def pv(qb, g, pi):
            fns = []
            for c in range(4):
                for j in range(3):
                    fns.append(lambda c=c, j=j: pe.matmul(ob[:, g, c * 65:(c + 1) * 65], lhsT=pT[:, pi, j, c * 128:(c + 1) * 128],
                                                          rhs=v1[p][:, qb + j, g, :], start=(j == 0), stop=(j == 2)))
            S.run("pe", fns, reads=[bpT[pi], bv], writes=[buf(f"ob{g}")])

        def tail_a(qb, oi):
            dcol = qb % 2
            dn = den[:, dcol, :].rearrange("p (g c) -> p g c", g=2)
            rdn = rden[:, dcol, :].rearrange("p (g c) -> p g c", g=2)
            S.run("dve", lambda: dve.tensor_tensor(out=dn, in0=o4[:, :, :, 64], in1=esink[:].rearrange("p (g c) -> p g c", g=2), op=ALU.add),
                  reads=[buf("ob0"), buf("ob1"), buf("esink")], writes=[bden])
            S.run("dve", lambda: dve.reciprocal(out=rdn, in_=dn), reads=[bden], writes=[bden])
            S.run("dve", lambda: dve.tensor_tensor(out=onorm[:, oi, :].rearrange("p (g c d) -> p g c d", g=2, c=4),
                                                   in0=o4[:, :, :, 0:64], in1=rdn.unsqueeze(3).to_broadcast([128, 2, 4, 64]), op=ALU.mult),
                  reads=[buf("ob0"), buf("ob1"), bden], writes=[bon[oi]])

        def tail_b(qb, oi):
            S.run("pe", [lambda ec=ec: pe.transpose(out=trb[:, ec, :], in_=onorm[:, oi, ec * 128:(ec + 1) * 128], identity=ident[:]) for ec in range(4)],
                  reads=[bon[oi]], writes=[btr])
            S.run("dve", lambda: dve.tensor_tensor(out=catT[:, 0:4, qb * 128:(qb + 1) * 128], in0=trb[:, 0:4, :],
                                                   in1=GaT[p][:, :, qb * 128:(qb + 1) * 128], op=ALU.mult),
                  reads=[btr, bga], writes=[bcat])

        for nm in (f"qT{p}", f"kT{p}", f"v1{p}", f"GaT{p}"):
            chk(nm, s)
        setown("catA", s)
        units = [(qb, g) for qb in range(4) for g in range(2)]
        pvq = []
        pend = []
        outq = []

        def flush_tails():
            for it in pend:
                it[2] += 1
            while pend and pend[0][2] >= 1:
                q_, o_, _ = pend.pop(0)
                tail_b(q_, o_)
                if inline_out:
                    outq.append([q_, 0])

        def do_pv():
            qb_, g_, pi_ = pvq.pop(0)
            pv(qb_, g_, pi_)
            flush_tails()
            if g_ == 1:
                oi = on_i[0] % 2
                on_i[0] += 1
                tail_a(qb_, oi)
                pend.append([qb_, oi, 0])

        ui = 0
        deferred = []
        while ui < len(units) or pvq or pend or outq or deferred:
            if ui < len(units):
                qb, g = units[ui]
                ui += 1
                pi = pT_i[0] % 3
                pT_i[0] += 1
                bks = qk(qb, g)
                softmax_num(qb, g, pi, bks)
                pvq.append((qb, g, pi))
                if len(pvq) > 2:
                    do_pv()
            elif pvq:
                do_pv()
            else:
                flush_tails()
            while deferred and not pvq:
                deferred.pop(0)()
            ready = [o for o in outq if o[1] >= 1]
            for o in outq:
                o[1] += 1
            for o in ready:
                outq.remove(o)
                for _ in stage_out(s, (o[0],), defer=(deferred if (s == NSEG - 1 and ui >= len(units)) else None)):
                    pass
            yield

    def pool_pre(s):
        chk("upT", s)
        L = SEG_TOK + 16
        for gi, w in enumerate(POOL_WINDOWS):
            u = upT[:, gi, :]
            S.run("pool", lambda: pool.tensor_tensor(out=ta[:, 1:L], in0=u[:, 0:L - 1], in1=u[:, 1:L], op=ALU.add), reads=[bup], writes=[bta])
            src, bsrc = ta, bta
            if w >= 4:
                S.run("pool", lambda: pool.tensor_tensor(out=tb[:, 2:L - 1], in0=ta[:, 1:L - 2], in1=ta[:, 3:L], op=ALU.add), reads=[bta], writes=[btb])
                src, bsrc = tb, btb
            if w >= 8:
                S.run("pool", lambda: pool.tensor_tensor(out=ta[:, 4:L - 3], in0=tb[:, 2:L - 5], in1=tb[:, 6:L - 1], op=ALU.add), reads=[btb], writes=[bta])
                src, bsrc = ta, bta
            if w >= 16:
                S.run("pool", lambda: pool.tensor_tensor(out=tb[:, 8:L - 7], in0=ta[:, 4:L - 11], in1=ta[:, 12:L - 3], op=ALU.add), reads=[bta], writes=[btb])
                src, bsrc = tb, btb
            yi = gi
            chk("upT", s)
            setown(f"yT{gi}", s)
            S.run("dve", lambda: dve.scalar_tensor_tensor(out=yT[:, yi, :], in0=src[:, 8:8 + SEG_TOK], scalar=1.0 / w, in1=u[:, 8:8 + SEG_TOK],
                                                          op0=ALU.mult, op1=ALU.subtract), reads=[bsrc, bup], writes=[byT[yi]])
            for (edge, seg_e, lo) in ((0, 0, 0), (1, NSEG - 1, SEG_TOK - 8)):
                if s == seg_e:
                    S.run("pool", lambda: pool.tensor_tensor(out=e8a[:], in0=src[:, 8 + lo:16 + lo], in1=rc[:, edge, gi, :], op=ALU.mult),
                          reads=[bsrc, buf("rc")], writes=[buf("e8a")])
                    S.run("pool", lambda: pool.tensor_tensor(out=yT[:, yi, lo:lo + 8], in0=e8a[:], in1=u[:, 8 + lo:16 + lo], op=ALU.subtract),
                          reads=[buf("e8a"), bup], writes=[byT[yi]])
            yield

    def pool_post(s):
        p = s % 2
        bgp = buf(f"GpT{p}")
        for gi in range(4):
            bk_, bb_, _ = next_bank()
            chk(f"yT{gi}", s)
            chk(f"GpT{p}", s)
            setown(f"catP{gi}", s)
            S.run("pe", lambda: pe.matmul(bk_, lhsT=poolw_b[:, gi, :], rhs=yT[:, gi, :], start=True, stop=True),
                  reads=[byT[gi], bpw], writes=[bb_])
            S.run("dve", lambda: dve.scalar_tensor_tensor(out=catT[:, 4 + gi, :], in0=bk_, scalar=ps[:, gi:gi + 1], in1=GpT[p][:, gi, :],
                                                          op0=ALU.mult, op1=ALU.mult), reads=[bb_, bgp] + bconsts, writes=[bcat])
            yield

    fin_i = [0]

    pending_store = []

    def flush_store():
        while pending_store:
            t_, f_ = pending_store.pop(0)
            tok = S.dma("act", ossem[f_], lambda: act.dma_start(out=y_d[t_ * 128:(t_ + 1) * 128, :], in_=os_[f_]), reads=[bos[f_]])
            final_tokens.append(tok)

    def stage_out(s, qbs=(0, 1, 2, 3), defer=None):
        for qb in qbs:
            tile_own = 4 * s + qb
            chk("catA", s)
            for gi_ in range(4):
                chk(f"catP{gi_}", s)
            assert all(b_.w is not None for b_ in bWout)
            r0_, r1_ = next_bank(), next_bank()
            mx = [r0_[0], r1_[0]]
            bmx = [r0_[1], r1_[1]]
            for half in range(2):
                fns = []
                for ek in range(8):
                    fns.append(lambda half=half, ek=ek: pe.matmul(mx[half], lhsT=catT[:, ek, qb * 128:(qb + 1) * 128],
                                                                  rhs=Wout_b[:, ek, half * 512:(half + 1) * 512], start=(ek == 0), stop=(ek == 7)))
                S.run("pe", fns, reads=[bcat] + bWout, writes=[bmx[half]])
            fi = fin_i[0] % 2
            fin_i[0] += 1
            S.dma("act", rssem[fi], lambda: act.dma_start(out=rs_[fi], in_=x_ext[(tile_own + 1) * 128:(tile_own + 2) * 128, :]),
                  writes=[brs[fi]], reads=bWout + bWin)
            c0 = 2 * tile_own
            for half in range(2):
                S.run("act", lambda half=half: act.activation(out=junk[:, 0:512], in_=mx[half], func=AF.Square, scale=1.0 / 32.0,
                                                              accum_out=ms2[:, c0 + half:c0 + half + 1]), reads=[bmx[half]], writes=[bjunk, bms2])
            flush_store()
            S.run("pool", lambda: pool.tensor_tensor(out=r2[:, tile_own:tile_own + 1], in0=ms2[:, c0:c0 + 1], in1=ms2[:, c0 + 1:c0 + 2], op=ALU.add),
                  reads=[bms2], writes=[buf("stat2")])
            S.run("pool", lambda: pool.tensor_scalar(out=r2[:, tile_own:tile_own + 1], in0=r2[:, tile_own:tile_own + 1], scalar1=EPS, scalar2=None, op0=ALU.add),
                  reads=[buf("stat2")], writes=[buf("stat2")])
            S.run("pool", lambda: pool.tensor_tensor(out=rstd2[:, tile_own:tile_own + 1], in0=r2[:, tile_own:tile_own + 1], in1=nhalf[:, 0:1], op=ALU.pow),
                  reads=[buf("stat2"), bsm], writes=[buf("stat2")])
            def evac(qb=qb, tile_own=tile_own, fi=fi, mx=mx, bmx=bmx):
                for half in range(2):
                    S.run("dve", lambda half=half: dve.scalar_tensor_tensor(out=os_[fi][:, half * 512:(half + 1) * 512], in0=mx[half],
                                                                            scalar=rstd2[:, tile_own:tile_own + 1], in1=gpost[:, half * 512:(half + 1) * 512],
                                                                            op0=ALU.mult, op1=ALU.mult),
                          reads=[bmx[half], buf("stat2")] + bconsts, writes=[bos[fi]])
                if s >= NSEG - 2:
                    S.run("dve", lambda: dve.tensor_tensor(out=os_[fi], in0=os_[fi], in1=rs_[fi], op=ALU.add), reads=[brs[fi]], writes=[bos[fi]])
                else:
                    S.run("pool", lambda: pool.tensor_tensor(out=os_[fi], in0=os_[fi], in1=rs_[fi], op=ALU.add), reads=[brs[fi]], writes=[bos[fi]])
                pending_store.append((tile_own, fi))
            if defer is None:
                evac()
            else:
                defer.append(evac)
            yield

    def dump(name, sbuf_ap, shape, dt, reads):
        if not DEBUG:
            return
        d = nc.dram_tensor("dbg_" + name, list(shape), dt, kind="ExternalOutput").ap()
        tok = S.dma("sp", "dbg", lambda: sp.dma_start(out=d, in_=sbuf_ap), reads=reads)
        final_tokens.append(tok)
        dbg[name] = d

    W_A = [3.0] * SEG_TILES
    W_KVB = [2.0] * 4 + [2.4] * 17
    W_PRE = [4.0] * 4
    W_POST = [0.5] * 4
    W_C = [2.0] * 8 + [0.7] * 3
    W_O = [4.5] * 4

    def chain(*gens):
        for g in gens:
            yield from g

    def drain(g):
        for _ in g:
            pass

    def interleave(main, wm, fill, wf):
        tm, tf = sum(wm), sum(wf)
        pm = pf = 0.0
        im = jf = 0
        dm = df = False
        while not (dm and df):
            if not dm and (df or pm / tm <= pf / tf):
                try:
                    next(main)
                    pm += wm[min(im, len(wm) - 1)]
                    im += 1
                except StopIteration:
                    dm = True
            else:
                try:
                    next(fill)
                    pf += wf[min(jf, len(wf) - 1)]
                    jf += 1
                except StopIteration:
                    df = True

    def mix(base, extra, every):
        n = 0
        for _ in base:
            yield
            n += 1
            if n % every == 0:
                try:
                    next(extra)
                except StopIteration:
                    pass
        for _ in extra:
            pass

    W_KV = [2.0] * 4
    W_BU = [2.4] * 5
    W_BR = [2.4] * 12

    def fill_gen(s1, extra=None):
        parts, w = [], []
        if s1 < NSEG:
            base = chain(stage_kv(s1), stage_b(s1, "u"), mix(stage_b(s1, "rest"), pool_pre(s1), 3))
            w += W_KV + W_BU + W_BR
        else:
            base = iter(())
        if s1 + 1 < NSEG:
            base = mix(base, g_a(s1 + 1), 3) if s1 < NSEG else g_a(s1 + 1)
        if extra is not None:
            base = mix(base, extra, 2)
        return base, (w if w else [1.0])

    def step(g):
        try:
            next(g)
            return True
        except StopIteration:
            return False

    ga0 = g_a(0)
    cj = [0]

    def casts(n):
        for _ in range(n):
            if cj[0] < len(wjobs):
                cast_w_in(cj[0])
                cj[0] += 1

    casts(2)
    while step(ga0):
        casts(1)
    S.run("dve", lambda: dve.reciprocal(out=rc[:], in_=rc[:]), reads=[buf("rc")], writes=[buf("rc")])
    kv0, bu0, br0 = stage_kv(0), stage_b(0, "u"), stage_b(0, "rest")
    pp0, ga1, wg = pool_pre(0), g_a(1), wout_gen()
    casts(10 - cj[0])
    drain(kv0)
    while step(bu0):
        step(ga1)
    for i_ in range(12):
        if i_ < 8:
            casts(1)
        step(br0)
        step(ga1)
        if i_ >= 8:
            step(wg)
        if i_ % 3 == 2:
            step(pp0)
    drain(pp0)
    drain(ga1)
    finish_E()
    if DEBUG:
        dump("hT", hTs[0][:], [128, 8, SEG_EXT], BF16, [bhTs[0]])
        dump("E", E[:], [128, 3, 8, 128], F32, [bE])
        dump("rc", rc[:], [128, 2, 4, 8], F32, [buf("rc")])
        dump("qT", qT[0][:], [128, 4, SEG_TOK], BF16, [buf("qT0")])
        dump("kT", kT[0][:], [128, SEG_EXT], BF16, [buf("kT0")])
        dump("v1", v1[0][:], [128, SEG_TILES, 2, 65], BF16, [buf("v10")])
        dump("GaT", GaT[0][:], [128, 4, SEG_TOK], BF16, [buf("GaT0")])
        dump("GpT", GpT[0][:], [128, 4, SEG_TOK], BF16, [buf("GpT0")])
        dump("upT", upT[:], [128, 4, SEG_TOK + 16], F32, [bup])
    for s in range(NSEG):
        if s == 0:
            main = chain(stage_c(s), pool_post(s), stage_out(s))
            wm = W_C + W_POST + W_O
        elif s + 1 < NSEG:
            main = mix(stage_c(s, inline_out=True), pool_post(s), 1)
            wm = [2.0, 2.5, 2.5, 2.5, 2.5, 6.5, 2.0, 6.5, 0.7, 5.2, 0.7, 4.5, 0.5]
        else:
            main = mix(stage_c(s, inline_out=True), pool_post(s), 1)
            wm = W_C
        if s + 1 < NSEG:
            f, wf = fill_gen(s + 1)
            if s == 0:
                f = mix(f, wg, 1)
            interleave(main, wm, f, wf)
        else:
            drain(main)

    flush_store()
    for tok in final_tokens:
        S.wait("act", tok)
    return nc


def _t5_bucket_np(rel):
    nb = 16
    max_exact = 8
    ret = np.where(rel > 0, nb, 0)
    n = np.abs(rel)
    nf = np.maximum(n, 1).astype(np.float32)
    val = (np.log(nf / np.float32(max_exact)) / np.float32(math.log(128 / max_exact)) * np.float32(nb - max_exact)).astype(np.float32)
    large = max_exact + val.astype(np.int32)
    large = np.minimum(large, nb - 1)
    return ret + np.where(n < max_exact, n, large)


def _onehot_table():
    oh = np.zeros((32, 765), np.float32)
    for j in range(3):
        u = np.arange(255)
        rel = (j - 1) * 128 + (u - 127)
        b = _t5_bucket_np(rel)
        oh[b, j * 255 + u] = 1.0
    return oh


_NC_CACHE = {}


def kernel(x, pre_norm_g, w_in, rel_bias, attn_sink, pool_w, pool_scale, w_out, post_norm_g):
    x = np.asarray(x, np.float32)
    xs2 = x.reshape(SEQ, D)
    xpad = np.zeros((SEQ + 256, D), np.float32)
    xpad[128:128 + SEQ] = xs2
    mask_full = np.zeros((SEQ + 256,), np.float32)
    mask_full[128:128 + SEQ] = 1.0
    shared = {
        "gpre": np.ascontiguousarray(np.asarray(pre_norm_g, np.float32).reshape(8, 128).T),
        "w_in": np.ascontiguousarray(np.asarray(w_in, np.float32).reshape(D, D_IN)),
        "rel_bias": np.ascontiguousarray(np.asarray(rel_bias, np.float32)),
        "onehot": _onehot_table(),
        "attn_sink": np.ascontiguousarray(np.asarray(attn_sink, np.float32).reshape(8)),
        "pool_w": np.ascontiguousarray(np.asarray(pool_w, np.float32).reshape(4, 128, 128)),
        "pool_scale_pg": np.ascontiguousarray(np.asarray(pool_scale, np.float32).reshape(4, 128).T),
        "w_out": np.ascontiguousarray(np.asarray(w_out, np.float32).reshape(D, D)),
        "post_norm_g": np.ascontiguousarray(np.asarray(post_norm_g, np.float32).reshape(D)),
    }
    in_maps = []
    for c in range(N_CORES):
        lo = c * TOK_CORE
        m = dict(shared)
        m["x_ext"] = np.ascontiguousarray(xpad[lo:lo + EXT])
        km = mask_full[lo:lo + EXT]
        m["kmask_row"] = np.ascontiguousarray(km)
        m["kmask_pt"] = np.ascontiguousarray(km.reshape(18, 128).T)
        in_maps.append(m)
    if "nc" not in _NC_CACHE:
        _NC_CACHE["nc"] = build_nc()
    nc = _NC_CACHE["nc"]
    res = run_bass_kernel_spmd(nc, in_maps, core_ids=list(range(N_CORES)))
    out = np.concatenate([np.asarray(r["y"], np.float32) for r in res.results], axis=0)
    if DEBUG:
        kernel.debug = [{k: v for k, v in r.items() if k.startswith("dbg_")} for r in res.results]
    return out.reshape(1, SEQ, D)
```

```python
import math
import os
import numpy as np
import concourse.bass as bass
import concourse.mybir as mybir
from concourse.bass_utils import run_bass_kernel_spmd

F32 = mybir.dt.float32
BF16 = mybir.dt.bfloat16
AF = mybir.ActivationFunctionType
ALU = mybir.AluOpType

N_CORES = 8
SEQ = 16384
D = 1024
D_IN = 2304
TOK_CORE = SEQ // N_CORES
EXT = TOK_CORE + 256
NSEG = 4
SEG_TOK = 512
SEG_TILES = 6
SEG_EXT = SEG_TILES * 128
EPS = 1e-6
POOL_WINDOWS = (2, 4, 8, 16)

DEBUG = bool(int(os.environ.get("MK_DEBUG", "0")))
POOL_EMULT = True


class Buf:
    __slots__ = ("name", "w", "r")

    def __init__(self, name):
        self.name = name
        self.w = None
        self.r = []


class Sched:
    def __init__(self, nc):
        self.nc = nc
        self.eng = {"pe": nc.tensor, "act": nc.scalar, "dve": nc.vector, "pool": nc.gpsimd, "sp": nc.sync}
        self._ctx = []
        self.semobj = {}
        self.cnt = {}
        self.seen = {k: {} for k in self.eng}
        for k in ("pe", "act", "dve", "pool"):
            self.new_sem(k)

    def new_sem(self, key):
        c = self.nc.semaphore("s_" + key)
        s = c.__enter__()
        self._ctx.append(c)
        self.semobj[key] = s
        self.cnt[key] = 0
        return key

    def wait(self, e, tok):
        if tok is None:
            return
        key, val = tok
        if key == "pe" and e == "pe":
            return
        if self.seen[e].get(key, 0) >= val:
            return
        self.eng[e].wait_ge(self.semobj[key], val)
        self.seen[e][key] = val

    def _deps(self, e, reads, writes):
        for b in reads:
            assert b.w is not None, f"read of never-written buffer {b.name}"
            self.wait(e, b.w)
        for b in writes:
            self.wait(e, b.w)
            for t in b.r:
                self.wait(e, t)

    def _commit(self, tok, reads, writes):
        for b in reads:
            b.r.append(tok)
        for b in writes:
            b.w = tok
            b.r = []

    def run(self, e, fns, reads=(), writes=()):
        self._deps(e, reads, writes)
        if not isinstance(fns, (list, tuple)):
            fns = [fns]
        inst = None
        for f in fns:
            inst = f()
        self.cnt[e] += 1
        inst.then_inc(self.semobj[e], 1)
        tok = (e, self.cnt[e])
        self._commit(tok, reads, writes)
        return tok

    def dma_group(self, q, semkey, fns, buf):
        assert self.cnt[semkey] == 0 and buf.w is None
        for f in fns:
            f().then_inc(self.semobj[semkey], 16)
            self.cnt[semkey] += 16
        buf.w = (semkey, self.cnt[semkey])

    def dma(self, q, semkey, fn, reads=(), writes=()):
        self._deps(q, reads, writes)
        inst = fn()
        self.cnt[semkey] += 16
        inst.then_inc(self.semobj[semkey], 16)
        tok = (semkey, self.cnt[semkey])
        self._commit(tok, reads, writes)
        return tok


def build_nc():
    nc = bass.Bass("TRN2", target_bir_lowering=False)
    S = Sched(nc)
    dbg = {}

    def din(name, shape):
        return nc.dram_tensor(name, list(shape), F32, kind="ExternalInput").ap()

    x_ext = din("x_ext", [EXT, D])
    kmask_pt_d = din("kmask_pt", [128, 18])
    kmask_row_d = din("kmask_row", [EXT])
    gpre_d = din("gpre", [128, 8])
    w_in_d = din("w_in", [D, D_IN])
    relb_d = din("rel_bias", [32, 8])
    onehot_d = din("onehot", [32, 765])
    sink_d = din("attn_sink", [8])
    poolw_d = din("pool_w", [4, 128, 128])
    ps_d = din("pool_scale_pg", [128, 4])
    w_out_d = din("w_out", [D, D])
    gpost_d = din("post_norm_g", [D])
    y_d = nc.dram_tensor("y", [TOK_CORE, D], F32, kind="ExternalOutput").ap()
    tab_scr = nc.dram_tensor("tab_scr", [8, 765], F32, kind="Internal").ap()

    def sb(name, shape, dt):
        return nc.alloc_sbuf_tensor("sb_" + name, list(shape), dt)

    Win_b = sb("Win_b", [128, 8, D_IN], BF16)
    Wout_b = sb("Wout_b", [128, 8, D], BF16)
    poolw_b = sb("poolw_b", [128, 4, 128], BF16)
    E = sb("E", [128, 3, 8, 128], F32)
    gpost = sb("gpost", [128, D], F32)
    ident = sb("ident", [128, 128], BF16)
    identf = sb("identf", [128, 128], F32)
    gpre = sb("gpre", [128, 8], F32)
    ps = sb("ps", [128, 4], F32)
    esink = sb("esink", [128, 8], F32)
    kmask_pt = sb("kmask_pt", [128, 18], F32)
    relb = sb("relb", [32, 8], F32)
    nhalf = sb("nhalf", [128, 2], F32)
    ms = sb("ms", [128, 32], F32)
    r0 = sb("r0", [128, 32], F32)
    rstd = sb("rstd", [128, 32], F32)
    ms2 = sb("ms2", [128, 32], F32)
    r2 = sb("r2", [128, 16], F32)
    rstd2 = sb("rstd2", [128, 16], F32)
    den = sb("den", [128, 2, 8], F32)
    rden = sb("rden", [128, 2, 8], F32)
    mrow = sb("mrow", [128, 2, 24], F32)
    mta = sb("mta", [128, 2, 24], F32)
    mtb = sb("mtb", [128, 2, 24], F32)
    rc = sb("rc", [128, 2, 4, 8], F32)
    e8a = sb("e8a", [128, 8], F32)
    S0 = sb("S0", [128, 2304], F32)
    S1 = sb("S1", [128, 2304], F32)
    hTs = [sb(f"hT{i}", [128, 8, SEG_EXT], BF16) for i in range(2)]
    qT = [sb(f"qT{i}", [128, 4, SEG_TOK], BF16) for i in range(2)]
    kT = [sb(f"kT{i}", [128, SEG_EXT], BF16) for i in range(2)]
    v1 = [sb(f"v1{i}", [128, SEG_TILES, 2, 65], BF16) for i in range(2)]
    GaT = [sb(f"GaT{i}", [128, 4, SEG_TOK], BF16) for i in range(2)]
    GpT = [sb(f"GpT{i}", [128, 4, SEG_TOK], BF16) for i in range(2)]
    upT = sb("upT", [128, 4, SEG_TOK + 16], F32)
    catT = sb("catT", [128, 8, SEG_TOK], BF16)
    hk = catT[:].rearrange("p a b -> p (a b)").bitcast(F32).rearrange("p (s h c) -> p s h c", s=2, h=8)
    xs = sb("xs", [128, 3, D], F32)
    xb = sb("xb", [128, 2, D], BF16)
    junk = sb("junk", [128, D], BF16)
    expS = sb("expS", [128, 3, 512], F32)
    pT = sb("pT", [128, 3, 3, 512], BF16)
    expS_flat = expS[:].rearrange("p a b -> p (a b)")
    ohs = expS_flat[0:32, 0:765]
    tabE = expS_flat[0:8, 768:768 + 765]
    onorm = sb("onorm", [128, 2, 512], BF16)
    ta = sb("ta", [128, SEG_TOK + 16], F32)
    poolw_f = ta[:, 0:512].rearrange("p (g d) -> p g d", g=4)
    tb = sb("tb", [128, SEG_TOK + 16], F32)
    yT = sb("yT", [128, 4, SEG_TOK], BF16)
    trb = nc.alloc_psum_tensor("trb", [128, 8, 128], BF16)
    acc = nc.alloc_psum_tensor("acc", [128, 2, 512], F32)
    sT = nc.alloc_psum_tensor("sT", [128, 3, 512], F32)
    ob = nc.alloc_psum_tensor("ob", [128, 2, 512], F32)

    B = {}

    def buf(name):
        if name not in B:
            B[name] = Buf(name)
        return B[name]

    for k in ("consts", "ld_e", "ld_w0", "ld_w1", "ld_w2", "ld_w3", "xs0", "xs1", "xs2", "rs0", "rs1", "os0", "os1", "dbg", "scr", "hk0", "hk1"):
        S.new_sem(k)

    sp = nc.sync
    boh = buf("onehot")
    S.dma_group("sp", "ld_e", [
        lambda: sp.dma_start(out=relb[:], in_=relb_d[:, :]),
        lambda: sp.dma_start(out=ohs, in_=onehot_d[:, :]),
    ], boh)
    bc_ = buf("consts")
    bconsts = [bc_]
    S.dma_group("sp", "consts", [
        lambda: sp.dma_start(out=gpre[:], in_=gpre_d[:, :]),
        lambda: sp.dma_start(out=kmask_pt[:], in_=kmask_pt_d[:, :]),
        lambda: sp.dma_start(out=ps[:], in_=ps_d[:, :]),
        lambda: sp.dma_start(out=esink[:], in_=bass.AP(sink_d.tensor, 0, [[0, 128], [1, 8]])),
        lambda: sp.dma_start(out=gpost[:], in_=bass.AP(gpost_d.tensor, 0, [[0, 128], [1, D]])),
        lambda: sp.dma_start(out=mrow[:, 0, :], in_=bass.AP(kmask_row_d.tensor, 120, [[0, 128], [1, 24]])),
        lambda: sp.dma_start(out=mrow[:, 1, :], in_=bass.AP(kmask_row_d.tensor, 120 + 1536 + 504, [[0, 128], [1, 24]])),
        lambda: sp.dma_start(out=poolw_f, in_=poolw_d.rearrange("g c d -> c g d")),
    ], bc_)

    bS = [buf(f"S{i}") for i in range(4)]
    Sst = [S0[:, 0:1152], S0[:, 1152:2304], S1[:, 0:1152], S1[:, 1152:2304]]
    ldw = ["ld_w0", "ld_w1", "ld_w2", "ld_w3"]
    wjob = [0]

    WBLK = [("kv", 512, 256), ("u", 1280, 512), ("q", 0, 512), ("ga", 768, 512), ("gp", 1792, 512)]
    wjobs = []
    for (nm_, c0_, nc_) in WBLK:
        ndk_ = 4 if nc_ == 256 else 2
        for dk0_ in range(0, 8, ndk_):
            wjobs.append((nm_, c0_, nc_, dk0_, ndk_))
    bWblk = {nm_: [buf(f"W_{nm_}{dk}") for dk in range(8)] for (nm_, _, _) in WBLK}

    def load_w_in(job):
        if job >= len(wjobs):
            return
        nm_, c0_, nc_, dk0_, ndk_ = wjobs[job]
        sl = job % 4
        S.dma("sp", ldw[sl], lambda: sp.dma_start(
            out=Sst[sl][:, 0:ndk_ * nc_].rearrange("p (k c) -> p k c", k=ndk_),
            in_=w_in_d[dk0_ * 128:(dk0_ + ndk_) * 128, c0_:c0_ + nc_].rearrange("(k p) c -> p k c", p=128)),
            writes=[bS[sl]])

    jobs = [(s_, lt_) for s_ in range(NSEG) for lt_ in range(SEG_TILES)]
    xsem = ["xs0", "xs1", "xs2"]
    bxs = [buf("xs0"), buf("xs1"), buf("xs2")]

    def a_load(k):
        if k >= len(jobs):
            return
        s_, lt_ = jobs[k]
        et = 4 * s_ + lt_
        i = k % 3
        if k < 3:
            S.dma("sp", xsem[i], lambda: sp.dma_start(out=xs[:, i, :], in_=x_ext[et * 128:(et + 1) * 128, :]), writes=[bxs[i]])
        else:
            S.dma("act", xsem[i], lambda: act.dma_start(out=xs[:, i, :], in_=x_ext[et * 128:(et + 1) * 128, :]), writes=[bxs[i]])

    a_load(0)
    a_load(1)
    a_load(2)
    for job_ in range(4):
        load_w_in(job_)

    bident = buf("ident")
    pool = nc.gpsimd
    S.run("pool", lambda: pool.memset(identf[:], 0.0), writes=[bident])
    S.run("pool", lambda: pool.affine_select(out=identf[:], in_=identf[:], pattern=[[-1, 128]], compare_op=ALU.not_equal,
                                             fill=1.0, base=0, channel_multiplier=1), writes=[bident])
    S.run("pool", lambda: pool.tensor_copy(out=ident[:], in_=identf[:]), reads=[bident], writes=[buf("identb")])
    bsm = buf("smalls")
    S.run("pool", lambda: pool.memset(nhalf[:, 0:1], -0.5), writes=[bsm])
    S.run("pool", lambda: pool.memset(nhalf[:, 1:2], -1.0), writes=[bsm])
    S.run("pool", lambda: pool.memset(ms[:], 0.0), writes=[buf("ms")])
    S.run("pool", lambda: pool.memset(ms2[:], 0.0), writes=[buf("ms2")])

    bE = buf("E")
    bsT = [buf("sT0"), buf("sT1"), buf("sT2")]
    tt = sb("tt", [128, 2, 512], F32)
    pe = nc.tensor
    act = nc.scalar
    dve = nc.vector
    bexp_all = [buf("exp0"), buf("exp1"), buf("exp2")]
    S.run("pe", lambda: pe.matmul(sT[0:8, 0, 0:510], lhsT=relb[:, :], rhs=ohs[:, 0:510], start=True, stop=True),
          reads=[boh], writes=[bsT[0]])
    S.run("pe", lambda: pe.matmul(sT[0:8, 1, 0:255], lhsT=relb[:, :], rhs=ohs[:, 510:765], start=True, stop=True),
          reads=[boh], writes=[bsT[1]])
    btab = buf("tabE")
    S.run("act", lambda: act.activation(out=tabE[:, 0:510], in_=sT[0:8, 0, 0:510], func=AF.Exp), reads=[bsT[0]], writes=[btab])
    S.run("act", lambda: act.activation(out=tabE[:, 510:765], in_=sT[0:8, 1, 0:255], func=AF.Exp), reads=[bsT[1]], writes=[btab])
    S.run("act", lambda: act.activation(out=esink[:], in_=esink[:], func=AF.Exp), reads=bconsts, writes=[buf("esink")])
    S.run("pool", lambda: pool.memset(tabE[:, 0:127], 0.0), writes=[btab])
    S.run("pool", lambda: pool.memset(tabE[:, 510 + 128:765], 0.0), writes=[btab])
    bscr = buf("scr")
    S.dma("act", "scr", lambda: act.dma_start(out=tab_scr[:, :], in_=tabE), reads=[btab], writes=[bscr])
    bhk = [buf("hk0"), buf("hk1")]

    def hank_dma(j):
        S.dma("act", f"hk{j % 2}", lambda: act.dma_start(
            out=hk[:, j % 2, :, :], in_=bass.AP(tab_scr.tensor, j * 255, [[1, 128], [765, 8], [1, 128]])),
            reads=[bscr], writes=[bhk[j % 2]])

    def hank_rev(j):
        S.run("pool", lambda: pool.tensor_copy(out=E[:, j, :, :], in_=hk[:, j % 2, :, ::-1]), reads=[bhk[j % 2]], writes=[bE])

    hank_dma(0)
    hank_dma(1)

    def finish_E():
        hank_rev(0)
        hank_dma(2)
        hank_rev(1)
        hank_rev(2)
        bcat.r.append(bE.w)
        for b_ in bexp_all:
            b_.r.append(bscr.w)
            b_.r.extend(boh.r)

    bm = buf("medge")
    S.run("pool", lambda: pool.tensor_tensor(out=mta[:, :, 1:24], in0=mrow[:, :, 0:23], in1=mrow[:, :, 1:24], op=ALU.add), reads=bconsts, writes=[bm])
    S.run("pool", lambda: pool.tensor_copy(out=rc[:, :, 0, :], in_=mta[:, :, 8:16]), reads=[bm], writes=[buf("rc")])
    S.run("pool", lambda: pool.tensor_tensor(out=mtb[:, :, 2:23], in0=mta[:, :, 1:22], in1=mta[:, :, 3:24], op=ALU.add), reads=[bm], writes=[bm])
    S.run("pool", lambda: pool.tensor_copy(out=rc[:, :, 1, :], in_=mtb[:, :, 8:16]), reads=[bm], writes=[buf("rc")])
    S.run("pool", lambda: pool.tensor_tensor(out=mta[:, :, 4:21], in0=mtb[:, :, 2:19], in1=mtb[:, :, 6:23], op=ALU.add), reads=[bm], writes=[bm])
    S.run("pool", lambda: pool.tensor_copy(out=rc[:, :, 2, :], in_=mta[:, :, 8:16]), reads=[bm], writes=[buf("rc")])
    S.run("pool", lambda: pool.tensor_tensor(out=mtb[:, :, 8:17], in0=mta[:, :, 4:13], in1=mta[:, :, 12:21], op=ALU.add), reads=[bm], writes=[bm])
    S.run("pool", lambda: pool.tensor_copy(out=rc[:, :, 3, :], in_=mtb[:, :, 8:16]), reads=[bm], writes=[buf("rc")])

    bWin = [b_ for nm_ in bWblk for b_ in bWblk[nm_]]
    cast_i = [0]

    def cast_w_in(job):
        nm_, c0_, nc_, dk0_, ndk_ = wjobs[job]
        sl = job % 4
        st = Sst[sl][:, 0:ndk_ * nc_].rearrange("p (k c) -> p k c", k=ndk_)
        for kk in range(ndk_):
            dk = dk0_ + kk
            e_ = "dve" if cast_i[0] % 2 == 0 else "act"
            cast_i[0] += 1
            g_ = gpre[:, dk:dk + 1]
            if nm_ == "q":
                out_ap = Win_b[:, dk, 0:512].rearrange("p (c hf d) -> p c hf d", c=4, hf=2)
                in_ap = st[:, kk, :].rearrange("p (hf c d) -> p c hf d", hf=2, c=4)
            else:
                out_ap = Win_b[:, dk, c0_:c0_ + nc_]
                in_ap = st[:, kk, :]
            if e_ == "dve":
                S.run("dve", lambda: dve.tensor_scalar(out=out_ap, in0=in_ap, scalar1=g_, scalar2=None, op0=ALU.mult),
                      reads=[bS[sl]] + bconsts, writes=[bWblk[nm_][dk]])
            else:
                S.run("act", lambda: act.activation(out=out_ap, in_=in_ap, func=AF.Copy, scale=g_),
                      reads=[bS[sl]] + bconsts, writes=[bWblk[nm_][dk]])
        load_w_in(job + 4)

    bWout = [buf(f"Wout{ek}") for ek in range(8)]

    def wout_load(ek):
        sl = ek % 4
        S.dma("sp", ldw[sl], lambda: sp.dma_start(out=Sst[sl][:, 0:D], in_=w_out_d[ek * 128:(ek + 1) * 128, :]), writes=[bS[sl]])

    def wout_gen():
        for ek in range(4):
            wout_load(ek)
        for ek in range(8):
            sl = ek % 4
            S.run("act", lambda: act.activation(out=Wout_b[:, ek, :], in_=Sst[sl][:, 0:D], func=AF.Copy, scale=0.5),
                  reads=[bS[sl]], writes=[bWout[ek]])
            if ek + 4 < 8:
                wout_load(ek + 4)
            yield

    bpw = buf("poolw")
    S.run("pool", lambda: pool.tensor_copy(out=poolw_b[:], in_=poolw_f), reads=bconsts, writes=[bpw])
    buf("ta").r.append(bpw.w)

    bxb = [buf("xb0"), buf("xb1")]
    btr = buf("trb")
    bhTs = [buf("hT0"), buf("hT1")]
    bacc = [buf("acc0"), buf("acc1")]
    bob = buf("ob")
    btt = [buf("tt0"), buf("tt1")]
    bexp = [buf("exp0"), buf("exp1"), buf("exp2")]
    bpT = [buf("pT0"), buf("pT1"), buf("pT2")]
    bon = [buf("on0"), buf("on1")]
    bjunk = buf("junk")
    bms = buf("ms")
    bms2 = buf("ms2")
    bstat = buf("stat")
    bup = buf("upT")
    bcat = buf("catT")
    bta = buf("ta")
    btb = buf("tb")
    byT = [buf(f"yT{i}") for i in range(4)]
    bden = buf("den")
    os_ = [S0[:, 0:D], S0[:, D:2 * D]]
    rs_ = [S1[:, 0:D], S1[:, D:2 * D]]
    bos = [buf("os0"), buf("os1")]
    brs = [buf("rs0"), buf("rs1")]
    ossem = ["os0", "os1"]
    rssem = ["rs0", "rs1"]

    own = {}

    def setown(name, seg):
        own[name] = seg

    def chk(name, seg):
        assert own.get(name) == seg, f"ordering bug: {name} holds segment {own.get(name)}, expected {seg}"

    xs_i = [0]
    acc_i = [0]
    tt_i = [0]
    final_tokens = []

    ring = [(acc[:, 0, :], bacc[0]), (acc[:, 1, :], bacc[1]), (sT[:, 0, :], bsT[0]), (sT[:, 1, :], bsT[1]), (sT[:, 2, :], bsT[2])]

    reserved = set()

    def next_bank(reserve=False):
        for _ in range(len(ring)):
            i = acc_i[0] % len(ring)
            acc_i[0] += 1
            if i not in reserved:
                if reserve:
                    reserved.add(i)
                return ring[i] + (i,)
        raise AssertionError("no free PSUM ring bank")

    def stage_a1(s, lt):
        k = s * SEG_TILES + lt
        i = k % 3
        col = k
        S.run("act", lambda: act.activation(out=junk[:], in_=xs[:, i, :], func=AF.Square, scale=1.0 / 32.0,
                                            accum_out=ms[:, col:col + 1]), reads=[bxs[i], bms], writes=[bjunk, buf(f"ms_{k}")])
        S.run("pool", lambda: pool.tensor_scalar(out=r0[:, col:col + 1], in0=ms[:, col:col + 1], scalar1=EPS, scalar2=None, op0=ALU.add),
              reads=[buf(f"ms_{k}")], writes=[buf(f"stat_{k}")])
        S.run("pool", lambda: pool.tensor_tensor(out=rstd[:, col:col + 1], in0=r0[:, col:col + 1], in1=nhalf[:, 0:1], op=ALU.pow),
              reads=[buf(f"stat_{k}"), bsm], writes=[buf(f"stat_{k}")])

    def stage_a2(s, lt):
        setown(f"hT{s % 2}", s)
        k = s * SEG_TILES + lt
        i = k % 3
        col = k
        j = col % 2
        if s == 0:
            S.run("dve", lambda: dve.tensor_scalar(out=xb[:, j, :], in0=xs[:, i, :], scalar1=rstd[:, col:col + 1], scalar2=None, op0=ALU.mult),
                  reads=[buf(f"stat_{k}"), bxs[i]], writes=[bxb[j]])
        else:
            S.run("act", lambda: act.activation(out=xb[:, j, :], in_=xs[:, i, :], func=AF.Copy, scale=rstd[:, col:col + 1]),
                  reads=[buf(f"stat_{k}"), bxs[i]], writes=[bxb[j]])
        a_load(k + 3)

    def stage_a3(s, lt):
        k = s * SEG_TILES + lt
        j = k % 2
        S.run("pe", [lambda dk=dk: pe.transpose(out=trb[:, dk, :], in_=xb[:, j, dk * 128:(dk + 1) * 128], identity=ident[:]) for dk in range(8)],
              reads=[bxb[j], buf("identb")], writes=[btr])
        S.run("dve", lambda: dve.tensor_copy(out=hTs[s % 2][:, :, lt * 128:(lt + 1) * 128], in_=trb[:, :, :]), reads=[btr], writes=[bhTs[s % 2]])

    def g_a(s):
        stage_a1(s, 0)
        for lt in range(SEG_TILES):
            if lt + 1 < SEG_TILES:
                stage_a1(s, lt + 1)
            stage_a2(s, lt)
            if lt >= 1:
                stage_a3(s, lt - 1)
            yield
        stage_a3(s, SEG_TILES - 1)
        yield

    def stage_kv(s):
        p = s % 2
        hT = hTs[p]
        bhT = bhTs[p]
        chk(f"hT{p}", s)
        setown(f"kT{p}", s)
        setown(f"v1{p}", s)
        bk = buf(f"kT{p}")
        bv = buf(f"v1{p}")
        for (t0, n) in ((0, 512), (512, 256)):
            bk_, bb_, _ = next_bank()
            S.run("pe", [lambda dk=dk: pe.matmul(bk_[:, 0:n], lhsT=Win_b[:, dk, 512:640], rhs=hT[:, dk, t0:t0 + n],
                                                 start=(dk == 0), stop=(dk == 7)) for dk in range(8)],
                  reads=[bhT] + bWblk["kv"], writes=[bb_])
            S.run("act", lambda: act.copy(out=kT[p][:, t0:t0 + n], in_=bk_[:, 0:n]), reads=[bb_], writes=[bk])
            yield
        for (l0, nl) in ((0, 4), (4, 2)):
            bk_, bb_, _ = next_bank()
            fns = []
            for li in range(nl):
                lt = l0 + li
                for dk in range(8):
                    fns.append(lambda lt=lt, li=li, dk=dk: pe.matmul(bk_[:, li * 128:(li + 1) * 128], lhsT=hT[:, dk, lt * 128:(lt + 1) * 128],
                                                                     rhs=Win_b[:, dk, 640:768], start=(dk == 0), stop=(dk == 7)))
            S.run("pe", fns, reads=[bhT] + bWblk["kv"], writes=[bb_])
            S.run("dve", lambda: dve.tensor_copy(out=v1[p][:, l0:l0 + nl, :, 0:64],
                                                 in_=bk_[:, 0:nl * 128].rearrange("p (l g d) -> p l g d", l=nl, g=2)),
                  reads=[bb_], writes=[bv])
            yield
        S.run("pool", lambda: pool.tensor_copy(out=v1[p][:, :, :, 64],
                                               in_=kmask_pt[:, 4 * s:4 * s + 6].unsqueeze(2).to_broadcast([128, 6, 2])),
              reads=bconsts, writes=[bv])

    def stage_b(s, part):
        p = s % 2
        hT = hTs[p]
        bhT = bhTs[p]
        chk(f"hT{p}", s)
        if part == "u":
            setown("upT", s)
        else:
            setown(f"qT{p}", s)
            setown(f"GaT{p}", s)
            setown(f"GpT{p}", s)
        bq = buf(f"qT{p}")
        bga = buf(f"GaT{p}")
        bgp = buf(f"GpT{p}")

        def proj(col, bk_, bb_):
            blk = "q" if col < 512 else ("ga" if col < 1280 else ("u" if col < 1792 else "gp"))
            S.run("pe", [lambda dk=dk: pe.matmul(bk_, lhsT=Win_b[:, dk, col:col + 128], rhs=hT[:, dk, 128:640],
                                                 start=(dk == 0), stop=(dk == 7)) for dk in range(8)],
                  reads=[bhT] + bWblk[blk], writes=[bb_])

        if part == "rest":
            for c in range(4):
                bk_, bb_, _ = next_bank()
                proj(c * 128, bk_, bb_)
                S.run("act", lambda: act.copy(out=qT[p][:, c, :], in_=bk_), reads=[bb_], writes=[bq])
                yield
            for (base, G, bg) in ((768, GaT[p], bga), (1792, GpT[p], bgp)):
                for c in range(4):
                    bk_, bb_, _ = next_bank()
                    proj(base + c * 128, bk_, bb_)
                    ti = tt_i[0] % 2
                    tt_i[0] += 1
                    S.run("act", lambda: act.activation(out=tt[:, ti, :], in_=bk_, func=AF.Tanh, scale=0.5), reads=[bb_], writes=[btt[ti]])
                    S.run("dve", lambda: dve.scalar_tensor_tensor(out=G[:, c, :], in0=tt[:, ti, :], scalar=1.0, in1=bk_,
                                                                  op0=ALU.add, op1=ALU.mult), reads=[btt[ti], bb_], writes=[bg])
                    yield
        if part == "u":
            for c in range(4):
                bk_, bb_, _ = next_bank()
                proj(1280 + c * 128, bk_, bb_)
                S.run("act", lambda: act.copy(out=upT[:, c, 8:8 + SEG_TOK], in_=bk_), reads=[bb_], writes=[bup])
                yield
            bk_, bb_, _ = next_bank()
            fns = []
            for c in range(4):
                col = 1280 + c * 128
                for dk in range(8):
                    fns.append(lambda c=c, col=col, dk=dk: pe.matmul(
                        bk_[:, c * 16:(c + 1) * 16], lhsT=Win_b[:, dk, col:col + 128],
                        rhs=bass.AP(hT, dk * SEG_EXT + 120, [[8 * SEG_EXT, 128], [520, 2], [1, 8]]),
                        start=(dk == 0), stop=(dk == 7)))
            S.run("pe", fns, reads=[bhT] + bWblk["u"], writes=[bb_])
            hv = bk_[:, 0:64].rearrange("p (c s e) -> p c s e", c=4, s=2)
            S.run("dve", lambda: dve.tensor_copy(out=upT[:, :, 0:8], in_=hv[:, :, 0, :]), reads=[bb_], writes=[bup])
            S.run("dve", lambda: dve.tensor_copy(out=upT[:, :, 8 + SEG_TOK:16 + SEG_TOK], in_=hv[:, :, 1, :]), reads=[bb_], writes=[bup])
            yield

    pT_i = [0]
    on_i = [0]

    def stage_c(s, inline_out=False):
        p = s % 2
        bq = buf(f"qT{p}")
        bk = buf(f"kT{p}")
        bv = buf(f"v1{p}")
        bga = buf(f"GaT{p}")
        o4 = ob[:, :, 0:260].rearrange("p g (c e) -> p g c e", e=65)

        def qk(qb, g):
            bks = [next_bank()[0:2] for _ in range(3)]
            for j in range(3):
                S.run("pe", lambda j=j: pe.matmul(bks[j][0], lhsT=kT[p][64 * g:64 * g + 64, (qb + j) * 128:(qb + j + 1) * 128],
                                                  rhs=qT[p][64 * g:64 * g + 64, :, qb * 128:(qb + 1) * 128], start=True, stop=True),
                      reads=[bq, bk], writes=[bks[j][1]])
            return bks

        def softmax_num(qb, g, pi, bks):
            for j in range(3):
                S.run("act", lambda j=j: act.activation(out=expS[:, j, :], in_=bks[j][0], func=AF.Exp, scale=0.125),
                      reads=[bks[j][1]], writes=[bexp[j]])
                if POOL_EMULT and (j == 2 or (j == 1 and s == NSEG - 1)):
                    S.run("pool", lambda j=j: pool.tensor_tensor(out=pT[:, pi, j, :], in0=expS[:, j, :],
                                                                 in1=E[:, j, 4 * g:4 * g + 4, :].rearrange("p h q -> p (h q)"), op=ALU.mult),
                          reads=[bexp[j], bE], writes=[bpT[pi]])
                else:
                    S.run("dve", lambda j=j: dve.tensor_tensor(out=pT[:, pi, j, :], in0=expS[:, j, :],
                                                               in1=E[:, j, 4 * g:4 * g + 4, :].rearrange("p h q -> p (h q)"), op=ALU.mult),
                          reads=[bexp[j], bE], writes=[bpT[pi]])

        def pv(qb, g, pi):
            fns = []
            for c in range(4):
                for j in range(3):
                    fns.append(lambda c=c, j=j: pe.matmul(ob[:, g, c * 65:(c + 1) * 65], lhsT=pT[:, pi, j, c * 128:(c + 1) * 128],
                                                          rhs=v1[p][:, qb + j, g, :], start=(j == 0), stop=(j == 2)))
            S.run("pe", fns, reads=[bpT[pi], bv], writes=[buf(f"ob{g}")])

        def tail_a(qb, oi):
            dcol = qb % 2
            dn = den[:, dcol, :].rearrange("p (g c) -> p g c", g=2)
            rdn = rden[:, dcol, :].rearrange("p (g c) -> p g c", g=2)
            S.run("dve", lambda: dve.tensor_tensor(out=dn, in0=o4[:, :, :, 64], in1=esink[:].rearrange("p (g c) -> p g c", g=2), op=ALU.add),
                  reads=[buf("ob0"), buf("ob1"), buf("esink")], writes=[bden])
            S.run("dve", lambda: dve.reciprocal(out=rdn, in_=dn), reads=[bden], writes=[bden])
            S.run("dve", lambda: dve.tensor_tensor(out=onorm[:, oi, :].rearrange("p (g c d) -> p g c d", g=2, c=4),
                                                   in0=o4[:, :, :, 0:64], in1=rdn.unsqueeze(3).to_broadcast([128, 2, 4, 64]), op=ALU.mult),
                  reads=[buf("ob0"), buf("ob1"), bden], writes=[bon[oi]])

        def tail_b(qb, oi):
            S.run("pe", [lambda ec=ec: pe.transpose(out=trb[:, ec, :], in_=onorm[:, oi, ec * 128:(ec + 1) * 128], identity=ident[:]) for ec in range(4)],
                  reads=[bon[oi]], writes=[btr])
            S.run("dve", lambda: dve.tensor_tensor(out=catT[:, 0:4, qb * 128:(qb + 1) * 128], in0=trb[:, 0:4, :],
                                                   in1=GaT[p][:, :, qb * 128:(qb + 1) * 128], op=ALU.mult),
                  reads=[btr, bga], writes=[bcat])

        for nm in (f"qT{p}", f"kT{p}", f"v1{p}", f"GaT{p}"):
            chk(nm, s)
        setown("catA", s)
        units = [(qb, g) for qb in range(4) for g in range(2)]
        pvq = []
        pend = []
        outq = []

        def flush_tails():
            for it in pend:
                it[2] += 1
            while pend and pend[0][2] >= 1:
                q_, o_, _ = pend.pop(0)
                tail_b(q_, o_)
                if inline_out:
                    outq.append([q_, 0])

        def do_pv():
            qb_, g_, pi_ = pvq.pop(0)
            pv(qb_, g_, pi_)
            flush_tails()
            if g_ == 1:
                oi = on_i[0] % 2
                on_i[0] += 1
                tail_a(qb_, oi)
                pend.append([qb_, oi, 0])

        ui = 0
        deferred = []
        while ui < len(units) or pvq or pend or outq or deferred:
            if ui < len(units):
                qb, g = units[ui]
                ui += 1
                pi = pT_i[0] % 3
                pT_i[0] += 1
                bks = qk(qb, g)
                softmax_num(qb, g, pi, bks)
                pvq.append((qb, g, pi))
                if len(pvq) > 2:
                    do_pv()
            elif pvq:
                do_pv()
            else:
                flush_tails()
            while deferred and not pvq:
                deferred.pop(0)()
            ready = [o for o in outq if o[1] >= 1]
            for o in outq:
                o[1] += 1
            for o in ready:
                outq.remove(o)
                for _ in stage_out(s, (o[0],), defer=(deferred if (s == NSEG - 1 and ui >= len(units)) else None)):
                    pass
            yield

    def pool_pre(s):
        chk("upT", s)
        L = SEG_TOK + 16
        for gi, w in enumerate(POOL_WINDOWS):
            u = upT[:, gi, :]
            S.run("pool", lambda: pool.tensor_tensor(out=ta[:, 1:L], in0=u[:, 0:L - 1], in1=u[:, 1:L], op=ALU.add), reads=[bup], writes=[bta])
            src, bsrc = ta, bta
            if w >= 4:
                S.run("pool", lambda: pool.tensor_tensor(out=tb[:, 2:L - 1], in0=ta[:, 1:L - 2], in1=ta[:, 3:L], op=ALU.add), reads=[bta], writes=[btb])
                src, bsrc = tb, btb
            if w >= 8:
                S.run("pool", lambda: pool.tensor_tensor(out=ta[:, 4:L - 3], in0=tb[:, 2:L - 5], in1=tb[:, 6:L - 1], op=ALU.add), reads=[btb], writes=[bta])
                src, bsrc = ta, bta
            if w >= 16:
                S.run("pool", lambda: pool.tensor_tensor(out=tb[:, 8:L - 7], in0=ta[:, 4:L - 11], in1=ta[:, 12:L - 3], op=ALU.add), reads=[bta], writes=[btb])
                src, bsrc = tb, btb
            yi = gi
            chk("upT", s)
            setown(f"yT{gi}", s)
            S.run("dve", lambda: dve.scalar_tensor_tensor(out=yT[:, yi, :], in0=src[:, 8:8 + SEG_TOK], scalar=1.0 / w, in1=u[:, 8:8 + SEG_TOK],
                                                          op0=ALU.mult, op1=ALU.subtract), reads=[bsrc, bup], writes=[byT[yi]])
            for (edge, seg_e, lo) in ((0, 0, 0), (1, NSEG - 1, SEG_TOK - 8)):
                if s == seg_e:
                    S.run("pool", lambda: pool.tensor_tensor(out=e8a[:], in0=src[:, 8 + lo:16 + lo], in1=rc[:, edge, gi, :], op=ALU.mult),
                          reads=[bsrc, buf("rc")], writes=[buf("e8a")])
                    S.run("pool", lambda: pool.tensor_tensor(out=yT[:, yi, lo:lo + 8], in0=e8a[:], in1=u[:, 8 + lo:16 + lo], op=ALU.subtract),
                          reads=[buf("e8a"), bup], writes=[byT[yi]])
            yield

    def pool_post(s):
        p = s % 2
        bgp = buf(f"GpT{p}")
        for gi in range(4):
            bk_, bb_, _ = next_bank()
            chk(f"yT{gi}", s)
            chk(f"GpT{p}", s)
            setown(f"catP{gi}", s)
            S.run("pe", lambda: pe.matmul(bk_, lhsT=poolw_b[:, gi, :], rhs=yT[:, gi, :], start=True, stop=True),
                  reads=[byT[gi], bpw], writes=[bb_])
            S.run("dve", lambda: dve.scalar_tensor_tensor(out=catT[:, 4 + gi, :], in0=bk_, scalar=ps[:, gi:gi + 1], in1=GpT[p][:, gi, :],
                                                          op0=ALU.mult, op1=ALU.mult), reads=[bb_, bgp] + bconsts, writes=[bcat])
            yield

    fin_i = [0]

    pending_store = []

    def flush_store():
        while pending_store:
            t_, f_ = pending_store.pop(0)
            tok = S.dma("act", ossem[f_], lambda: act.dma_start(out=y_d[t_ * 128:(t_ + 1) * 128, :], in_=os_[f_]), reads=[bos[f_]])
            final_tokens.append(tok)

    def stage_out(s, qbs=(0, 1, 2, 3), defer=None):
        for qb in qbs:
            tile_own = 4 * s + qb
            chk("catA", s)
            for gi_ in range(4):
                chk(f"catP{gi_}", s)
            assert all(b_.w is not None for b_ in bWout)
            r0_, r1_ = next_bank(), next_bank()
            mx = [r0_[0], r1_[0]]
            bmx = [r0_[1], r1_[1]]
            for half in range(2):
                fns = []
                for ek in range(8):
                    fns.append(lambda half=half, ek=ek: pe.matmul(mx[half], lhsT=catT[:, ek, qb * 128:(qb + 1) * 128],
                                                                  rhs=Wout_b[:, ek, half * 512:(half + 1) * 512], start=(ek == 0), stop=(ek == 7)))
                S.run("pe", fns, reads=[bcat] + bWout, writes=[bmx[half]])
            fi = fin_i[0] % 2
            fin_i[0] += 1
            S.dma("act", rssem[fi], lambda: act.dma_start(out=rs_[fi], in_=x_ext[(tile_own + 1) * 128:(tile_own + 2) * 128, :]),
                  writes=[brs[fi]], reads=bWout + bWin)
            c0 = 2 * tile_own
            for half in range(2):
                S.run("act", lambda half=half: act.activation(out=junk[:, 0:512], in_=mx[half], func=AF.Square, scale=1.0 / 32.0,
                                                              accum_out=ms2[:, c0 + half:c0 + half + 1]), reads=[bmx[half]], writes=[bjunk, bms2])
            flush_store()
            S.run("pool", lambda: pool.tensor_tensor(out=r2[:, tile_own:tile_own + 1], in0=ms2[:, c0:c0 + 1], in1=ms2[:, c0 + 1:c0 + 2], op=ALU.add),
                  reads=[bms2], writes=[buf("stat2")])
            S.run("pool", lambda: pool.tensor_scalar(out=r2[:, tile_own:tile_own + 1], in0=r2[:, tile_own:tile_own + 1], scalar1=EPS, scalar2=None, op0=ALU.add),
                  reads=[buf("stat2")], writes=[buf("stat2")])
            S.run("pool", lambda: pool.tensor_tensor(out=rstd2[:, tile_own:tile_own + 1], in0=r2[:, tile_own:tile_own + 1], in1=nhalf[:, 0:1], op=ALU.pow),
                  reads=[buf("stat2"), bsm], writes=[buf("stat2")])
            def evac(qb=qb, tile_own=tile_own, fi=fi, mx=mx, bmx=bmx):
                for half in range(2):
                    S.run("dve", lambda half=half: dve.scalar_tensor_tensor(out=os_[fi][:, half * 512:(half + 1) * 512], in0=mx[half],
                                                                            scalar=rstd2[:, tile_own:tile_own + 1], in1=gpost[:, half * 512:(half + 1) * 512],
                                                                            op0=ALU.mult, op1=ALU.mult),
                          reads=[bmx[half], buf("stat2")] + bconsts, writes=[bos[fi]])
                if s >= NSEG - 2:
                    S.run("dve", lambda: dve.tensor_tensor(out=os_[fi], in0=os_[fi], in1=rs_[fi], op=ALU.add), reads=[brs[fi]], writes=[bos[fi]])
                else:
                    S.run("pool", lambda: pool.tensor_tensor(out=os_[fi], in0=os_[fi], in1=rs_[fi], op=ALU.add), reads=[brs[fi]], writes=[bos[fi]])
                pending_store.append((tile_own, fi))
            if defer is None:
                evac()
            else:
                defer.append(evac)
            yield

    def dump(name, sbuf_ap, shape, dt, reads):
        if not DEBUG:
            return
        d = nc.dram_tensor("dbg_" + name, list(shape), dt, kind="ExternalOutput").ap()
        tok = S.dma("sp", "dbg", lambda: sp.dma_start(out=d, in_=sbuf_ap), reads=reads)
        final_tokens.append(tok)
        dbg[name] = d

    W_A = [3.0] * SEG_TILES
    W_KVB = [2.0] * 4 + [2.4] * 17
    W_PRE = [4.0] * 4
    W_POST = [0.5] * 4
    W_C = [2.0] * 8 + [0.7] * 3
    W_O = [4.5] * 4

    def chain(*gens):
        for g in gens:
            yield from g

    def drain(g):
        for _ in g:
            pass

    def interleave(main, wm, fill, wf):
        tm, tf = sum(wm), sum(wf)
        pm = pf = 0.0
        im = jf = 0
        dm = df = False
        while not (dm and df):
            if not dm and (df or pm / tm <= pf / tf):
                try:
                    next(main)
                    pm += wm[min(im, len(wm) - 1)]
                    im += 1
                except StopIteration:
                    dm = True
            else:
                try:
                    next(fill)
                    pf += wf[min(jf, len(wf) - 1)]
                    jf += 1
                except StopIteration:
                    df = True

    def mix(base, extra, every):
        n = 0
        for _ in base:
            yield
            n += 1
            if n % every == 0:
                try:
                    next(extra)
                except StopIteration:
                    pass
        for _ in extra:
            pass

    W_KV = [2.0] * 4
    W_BU = [2.4] * 5
    W_BR = [2.4] * 12

    def fill_gen(s1, extra=None):
        parts, w = [], []
        if s1 < NSEG:
            base = chain(stage_kv(s1), stage_b(s1, "u"), mix(stage_b(s1, "rest"), pool_pre(s1), 3))
            w += W_KV + W_BU + W_BR
        else:
            base = iter(())
        if s1 + 1 < NSEG:
            base = mix(base, g_a(s1 + 1), 3) if s1 < NSEG else g_a(s1 + 1)
        if extra is not None:
            base = mix(base, extra, 2)
        return base, (w if w else [1.0])

    def step(g):
        try:
            next(g)
            return True
        except StopIteration:
            return False

    ga0 = g_a(0)
    cj = [0]

    def casts(n):
        for _ in range(n):
            if cj[0] < len(wjobs):
                cast_w_in(cj[0])
                cj[0] += 1

    casts(2)
    while step(ga0):
        casts(1)
    S.run("dve", lambda: dve.reciprocal(out=rc[:], in_=rc[:]), reads=[buf("rc")], writes=[buf("rc")])
    kv0, bu0, br0 = stage_kv(0), stage_b(0, "u"), stage_b(0, "rest")
    pp0, ga1, wg = pool_pre(0), g_a(1), wout_gen()
    casts(10 - cj[0])
    drain(kv0)
    while step(bu0):
        step(ga1)
    for i_ in range(12):
        if i_ < 8:
            casts(1)
        step(br0)
        step(ga1)
        if i_ >= 8:
            step(wg)
        if i_ % 3 == 2:
            step(pp0)
    drain(pp0)
    drain(ga1)
    finish_E()
    if DEBUG:
        dump("hT", hTs[0][:], [128, 8, SEG_EXT], BF16, [bhTs[0]])
        dump("E", E[:], [128, 3, 8, 128], F32, [bE])
        dump("rc", rc[:], [128, 2, 4, 8], F32, [buf("rc")])
        dump("qT", qT[0][:], [128, 4, SEG_TOK], BF16, [buf("qT0")])
        dump("kT", kT[0][:], [128, SEG_EXT], BF16, [buf("kT0")])
        dump("v1", v1[0][:], [128, SEG_TILES, 2, 65], BF16, [buf("v10")])
        dump("GaT", GaT[0][:], [128, 4, SEG_TOK], BF16, [buf("GaT0")])
        dump("GpT", GpT[0][:], [128, 4, SEG_TOK], BF16, [buf("GpT0")])
        dump("upT", upT[:], [128, 4, SEG_TOK + 16], F32, [bup])
    for s in range(NSEG):
        if s < 0:
            main = chain(stage_c(s), pool_post(s), stage_out(s))
            wm = W_C + W_POST + W_O
        elif s + 1 < NSEG:
            main = mix(stage_c(s, inline_out=True), pool_post(s), 1)
            wm = [2.0, 2.5, 2.5, 2.5, 2.5, 6.5, 2.0, 6.5, 0.7, 5.2, 0.7, 4.5, 0.5]
        else:
            main = mix(stage_c(s, inline_out=True), pool_post(s), 1)
            wm = W_C
        if s + 1 < NSEG:
            f, wf = fill_gen(s + 1)
            if s == 0:
                f = mix(f, wg, 1)
            interleave(main, wm, f, wf)
        else:
            drain(main)

    flush_store()
    for tok in final_tokens:
        S.wait("act", tok)
    return nc


def _t5_bucket_np(rel):
    nb = 16
    max_exact = 8
    ret = np.where(rel > 0, nb, 0)
    n = np.abs(rel)
    nf = np.maximum(n, 1).astype(np.float32)
    val = (np.log(nf / np.float32(max_exact)) / np.float32(math.log(128 / max_exact)) * np.float32(nb - max_exact)).astype(np.float32)
    large = max_exact + val.astype(np.int32)
    large = np.minimum(large, nb - 1)
    return ret + np.where(n < max_exact, n, large)


def _onehot_table():
    oh = np.zeros((32, 765), np.float32)
    for j in range(3):
        u = np.arange(255)
        rel = (j - 1) * 128 + (u - 127)
        b = _t5_bucket_np(rel)
        oh[b, j * 255 + u] = 1.0
    return oh


_NC_CACHE = {}


def kernel(x, pre_norm_g, w_in, rel_bias, attn_sink, pool_w, pool_scale, w_out, post_norm_g):
    x = np.asarray(x, np.float32)
    xs2 = x.reshape(SEQ, D)
    xpad = np.zeros((SEQ + 256, D), np.float32)
    xpad[128:128 + SEQ] = xs2
    mask_full = np.zeros((SEQ + 256,), np.float32)
    mask_full[128:128 + SEQ] = 1.0
    shared = {
        "gpre": np.ascontiguousarray(np.asarray(pre_norm_g, np.float32).reshape(8, 128).T),
        "w_in": np.ascontiguousarray(np.asarray(w_in, np.float32).reshape(D, D_IN)),
        "rel_bias": np.ascontiguousarray(np.asarray(rel_bias, np.float32)),
        "onehot": _onehot_table(),
        "attn_sink": np.ascontiguousarray(np.asarray(attn_sink, np.float32).reshape(8)),
        "pool_w": np.ascontiguousarray(np.asarray(pool_w, np.float32).reshape(4, 128, 128)),
        "pool_scale_pg": np.ascontiguousarray(np.asarray(pool_scale, np.float32).reshape(4, 128).T),
        "w_out": np.ascontiguousarray(np.asarray(w_out, np.float32).reshape(D, D)),
        "post_norm_g": np.ascontiguousarray(np.asarray(post_norm_g, np.float32).reshape(D)),
    }
    in_maps = []
    for c in range(N_CORES):
        lo = c * TOK_CORE
        m = dict(shared)
        m["x_ext"] = np.ascontiguousarray(xpad[lo:lo + EXT])
        km = mask_full[lo:lo + EXT]
        m["kmask_row"] = np.ascontiguousarray(km)
        m["kmask_pt"] = np.ascontiguousarray(km.reshape(18, 128).T)
        in_maps.append(m)
    if "nc" not in _NC_CACHE:
        _NC_CACHE["nc"] = build_nc()
    nc = _NC_CACHE["nc"]
    res = run_bass_kernel_spmd(nc, in_maps, core_ids=list(range(N_CORES)))
    out = np.concatenate([np.asarray(r["y"], np.float32) for r in res.results], axis=0)
    if DEBUG:
        kernel.debug = [{k: v for k, v in r.items() if k.startswith("dbg_")} for r in res.results]
    return out.reshape(1, SEQ, D)
```

```python
import math
import os
import numpy as np
import concourse.bass as bass
import concourse.mybir as mybir
from concourse.bass_utils import run_bass_kernel_spmd

F32 = mybir.dt.float32
BF16 = mybir.dt.bfloat16
AF = mybir.ActivationFunctionType
ALU = mybir.AluOpType

N_CORES = 8
SEQ = 16384
D = 1024
D_IN = 2304
TOK_CORE = SEQ // N_CORES
EXT = TOK_CORE + 256
NSEG = 4
SEG_TOK = 512
SEG_TILES = 6
SEG_EXT = SEG_TILES * 128
EPS = 1e-6
POOL_WINDOWS = (2, 4, 8, 16)

DEBUG = bool(int(os.environ.get("MK_DEBUG", "0")))
POOL_EMULT = True


class Buf:
    __slots__ = ("name", "w", "r")

    def __init__(self, name):
        self.name = name
        self.w = None
        self.r = []


class Sched:
    def __init__(self, nc):
        self.nc = nc
        self.eng = {"pe": nc.tensor, "act": nc.scalar, "dve": nc.vector, "pool": nc.gpsimd, "sp": nc.sync}
        self._ctx = []
        self.semobj = {}
        self.cnt = {}
        self.seen = {k: {} for k in self.eng}
        for k in ("pe", "act", "dve", "pool"):
            self.new_sem(k)

    def new_sem(self, key):
        c = self.nc.semaphore("s_" + key)
        s = c.__enter__()
        self._ctx.append(c)
        self.semobj[key] = s
        self.cnt[key] = 0
        return key

    def wait(self, e, tok):
        if tok is None:
            return
        key, val = tok
        if key == "pe" and e == "pe":
            return
        if self.seen[e].get(key, 0) >= val:
            return
        self.eng[e].wait_ge(self.semobj[key], val)
        self.seen[e][key] = val

    def _deps(self, e, reads, writes):
        for b in reads:
            assert b.w is not None, f"read of never-written buffer {b.name}"
            self.wait(e, b.w)
        for b in writes:
            self.wait(e, b.w)
            for t in b.r:
                self.wait(e, t)

    def _commit(self, tok, reads, writes):
        for b in reads:
            b.r.append(tok)
        for b in writes:
            b.w = tok
            b.r = []

    def run(self, e, fns, reads=(), writes=()):
        self._deps(e, reads, writes)
        if not isinstance(fns, (list, tuple)):
            fns = [fns]
        inst = None
        for f in fns:
            inst = f()
        self.cnt[e] += 1
        inst.then_inc(self.semobj[e], 1)
        tok = (e, self.cnt[e])
        self._commit(tok, reads, writes)
        return tok

    def dma_group(self, q, semkey, fns, buf):
        assert self.cnt[semkey] == 0 and buf.w is None
        for f in fns:
            f().then_inc(self.semobj[semkey], 16)
            self.cnt[semkey] += 16
        buf.w = (semkey, self.cnt[semkey])

    def dma(self, q, semkey, fn, reads=(), writes=()):
        self._deps(q, reads, writes)
        inst = fn()
        self.cnt[semkey] += 16
        inst.then_inc(self.semobj[semkey], 16)
        tok = (semkey, self.cnt[semkey])
        self._commit(tok, reads, writes)
        return tok


def build_nc():
    nc = bass.Bass("TRN2", target_bir_lowering=False)
    S = Sched(nc)
    dbg = {}

    def din(name, shape):
        return nc.dram_tensor(name, list(shape), F32, kind="ExternalInput").ap()

    x_ext = din("x_ext", [EXT, D])
    kmask_pt_d = din("kmask_pt", [128, 18])
    kmask_row_d = din("kmask_row", [EXT])
    gpre_d = din("gpre", [128, 8])
    w_in_d = din("w_in", [D, D_IN])
    relb_d = din("rel_bias", [32, 8])
    onehot_d = din("onehot", [32, 765])
    sink_d = din("attn_sink", [8])
    poolw_d = din("pool_w", [4, 128, 128])
    ps_d = din("pool_scale_pg", [128, 4])
    w_out_d = din("w_out", [D, D])
    gpost_d = din("post_norm_g", [D])
    y_d = nc.dram_tensor("y", [TOK_CORE, D], F32, kind="ExternalOutput").ap()
    tab_scr = nc.dram_tensor("tab_scr", [8, 765], F32, kind="Internal").ap()

    def sb(name, shape, dt):
        return nc.alloc_sbuf_tensor("sb_" + name, list(shape), dt)

    Win_b = sb("Win_b", [128, 8, D_IN], BF16)
    Wout_b = sb("Wout_b", [128, 8, D], BF16)
    poolw_b = sb("poolw_b", [128, 4, 128], BF16)
    E = sb("E", [128, 3, 8, 128], F32)
    gpost = sb("gpost", [128, D], F32)
    ident = sb("ident", [128, 128], BF16)
    identf = sb("identf", [128, 128], F32)
    gpre = sb("gpre", [128, 8], F32)
    ps = sb("ps", [128, 4], F32)
    esink = sb("esink", [128, 8], F32)
    kmask_pt = sb("kmask_pt", [128, 18], F32)
    relb = sb("relb", [32, 8], F32)
    nhalf = sb("nhalf", [128, 2], F32)
    ms = sb("ms", [128, 32], F32)
    r0 = sb("r0", [128, 32], F32)
    rstd = sb("rstd", [128, 32], F32)
    ms2 = sb("ms2", [128, 32], F32)
    r2 = sb("r2", [128, 16], F32)
    rstd2 = sb("rstd2", [128, 16], F32)
    den = sb("den", [128, 2, 8], F32)
    rden = sb("rden", [128, 2, 8], F32)
    mrow = sb("mrow", [128, 2, 24], F32)
    mta = sb("mta", [128, 2, 24], F32)
    mtb = sb("mtb", [128, 2, 24], F32)
    rc = sb("rc", [128, 2, 4, 8], F32)
    e8a = sb("e8a", [128, 8], F32)
    S0 = sb("S0", [128, 2304], F32)
    S1 = sb("S1", [128, 2304], F32)
    hTs = [sb(f"hT{i}", [128, 8, SEG_EXT], BF16) for i in range(2)]
    qT = [sb(f"qT{i}", [128, 4, SEG_TOK], BF16) for i in range(2)]
    kT = [sb(f"kT{i}", [128, SEG_EXT], BF16) for i in range(2)]
    v1 = [sb(f"v1{i}", [128, SEG_TILES, 2, 65], BF16) for i in range(2)]
    GaT = [sb(f"GaT{i}", [128, 4, SEG_TOK], BF16) for i in range(2)]
    GpT = [sb(f"GpT{i}", [128, 4, SEG_TOK], BF16) for i in range(2)]
    upT = sb("upT", [128, 4, SEG_TOK + 16], F32)
    catT = sb("catT", [128, 8, SEG_TOK], BF16)
    hk = catT[:].rearrange("p a b -> p (a b)").bitcast(F32).rearrange("p (s h c) -> p s h c", s=2, h=8)
    xs = sb("xs", [128, 3, D], F32)
    xb = sb("xb", [128, 2, D], BF16)
    junk = sb("junk", [128, D], BF16)
    expS = sb("expS", [128, 3, 512], F32)
    pT = sb("pT", [128, 3, 3, 512], BF16)
    expS_flat = expS[:].rearrange("p a b -> p (a b)")
    ohs = expS_flat[0:32, 0:765]
    tabE = expS_flat[0:8, 768:768 + 765]
    onorm = sb("onorm", [128, 2, 512], BF16)
    ta = sb("ta", [128, SEG_TOK + 16], F32)
    poolw_f = ta[:, 0:512].rearrange("p (g d) -> p g d", g=4)
    tb = sb("tb", [128, SEG_TOK + 16], F32)
    yT = sb("yT", [128, 4, SEG_TOK], BF16)
    trb = nc.alloc_psum_tensor("trb", [128, 8, 128], BF16)
    acc = nc.alloc_psum_tensor("acc", [128, 2, 512], F32)
    sT = nc.alloc_psum_tensor("sT", [128, 3, 512], F32)
    ob = nc.alloc_psum_tensor("ob", [128, 2, 512], F32)

    B = {}

    def buf(name):
        if name not in B:
            B[name] = Buf(name)
        return B[name]

    for k in ("consts", "ld_e", "ld_w0", "ld_w1", "ld_w2", "ld_w3", "xs0", "xs1", "xs2", "rs0", "rs1", "os0", "os1", "dbg", "scr", "hk0", "hk1"):
        S.new_sem(k)

    sp = nc.sync
    boh = buf("onehot")
    S.dma_group("sp", "ld_e", [
        lambda: sp.dma_start(out=relb[:], in_=relb_d[:, :]),
        lambda: sp.dma_start(out=ohs, in_=onehot_d[:, :]),
    ], boh)
    bc_ = buf("consts")
    bconsts = [bc_]
    S.dma_group("sp", "consts", [
        lambda: sp.dma_start(out=gpre[:], in_=gpre_d[:, :]),
        lambda: sp.dma_start(out=kmask_pt[:], in_=kmask_pt_d[:, :]),
        lambda: sp.dma_start(out=ps[:], in_=ps_d[:, :]),
        lambda: sp.dma_start(out=esink[:], in_=bass.AP(sink_d.tensor, 0, [[0, 128], [1, 8]])),
        lambda: sp.dma_start(out=gpost[:], in_=bass.AP(gpost_d.tensor, 0, [[0, 128], [1, D]])),
        lambda: sp.dma_start(out=mrow[:, 0, :], in_=bass.AP(kmask_row_d.tensor, 120, [[0, 128], [1, 24]])),
        lambda: sp.dma_start(out=mrow[:, 1, :], in_=bass.AP(kmask_row_d.tensor, 120 + 1536 + 504, [[0, 128], [1, 24]])),
        lambda: sp.dma_start(out=poolw_f, in_=poolw_d.rearrange("g c d -> c g d")),
    ], bc_)

    bS = [buf(f"S{i}") for i in range(4)]
    Sst = [S0[:, 0:1152], S0[:, 1152:2304], S1[:, 0:1152], S1[:, 1152:2304]]
    ldw = ["ld_w0", "ld_w1", "ld_w2", "ld_w3"]
    wjob = [0]

    WBLK = [("kv", 512, 256), ("u", 1280, 512), ("q", 0, 512), ("ga", 768, 512), ("gp", 1792, 512)]
    wjobs = []
    for (nm_, c0_, nc_) in WBLK:
        ndk_ = 4 if nc_ == 256 else 2
        for dk0_ in range(0, 8, ndk_):
            wjobs.append((nm_, c0_, nc_, dk0_, ndk_))
    bWblk = {nm_: [buf(f"W_{nm_}{dk}") for dk in range(8)] for (nm_, _, _) in WBLK}

    def load_w_in(job):
        if job >= len(wjobs):
            return
        nm_, c0_, nc_, dk0_, ndk_ = wjobs[job]
        sl = job % 4
        S.dma("sp", ldw[sl], lambda: sp.dma_start(
            out=Sst[sl][:, 0:ndk_ * nc_].rearrange("p (k c) -> p k c", k=ndk_),
            in_=w_in_d[dk0_ * 128:(dk0_ + ndk_) * 128, c0_:c0_ + nc_].rearrange("(k p) c -> p k c", p=128)),
            writes=[bS[sl]])

    jobs = [(s_, lt_) for s_ in range(NSEG) for lt_ in range(SEG_TILES)]
    xsem = ["xs0", "xs1", "xs2"]
    bxs = [buf("xs0"), buf("xs1"), buf("xs2")]

    def a_load(k):
        if k >= len(jobs):
            return
        s_, lt_ = jobs[k]
        et = 4 * s_ + lt_
        i = k % 3
        if k < 3:
            S.dma("sp", xsem[i], lambda: sp.dma_start(out=xs[:, i, :], in_=x_ext[et * 128:(et + 1) * 128, :]), writes=[bxs[i]])
        else:
            S.dma("act", xsem[i], lambda: act.dma_start(out=xs[:, i, :], in_=x_ext[et * 128:(et + 1) * 128, :]), writes=[bxs[i]])

    a_load(0)
    a_load(1)
    a_load(2)
    for job_ in range(4):
        load_w_in(job_)

    bident = buf("ident")
    pool = nc.gpsimd
    S.run("pool", lambda: pool.memset(identf[:], 0.0), writes=[bident])
    S.run("pool", lambda: pool.affine_select(out=identf[:], in_=identf[:], pattern=[[-1, 128]], compare_op=ALU.not_equal,
                                             fill=1.0, base=0, channel_multiplier=1), writes=[bident])
    S.run("pool", lambda: pool.tensor_copy(out=ident[:], in_=identf[:]), reads=[bident], writes=[buf("identb")])
    bsm = buf("smalls")
    S.run("pool", lambda: pool.memset(nhalf[:, 0:1], -0.5), writes=[bsm])
    S.run("pool", lambda: pool.memset(nhalf[:, 1:2], EPS), writes=[bsm])
    S.run("pool", lambda: pool.memset(ms[:], 0.0), writes=[buf("ms")])
    S.run("pool", lambda: pool.memset(ms2[:], 0.0), writes=[buf("ms2")])

    bE = buf("E")
    bsT = [buf("sT0"), buf("sT1"), buf("sT2")]
    tt = sb("tt", [128, 2, 512], F32)
    pe = nc.tensor
    act = nc.scalar
    dve = nc.vector
    bexp_all = [buf("exp0"), buf("exp1"), buf("exp2")]
    S.run("pe", lambda: pe.matmul(sT[0:8, 0, 0:510], lhsT=relb[:, :], rhs=ohs[:, 0:510], start=True, stop=True),
          reads=[boh], writes=[bsT[0]])
    S.run("pe", lambda: pe.matmul(sT[0:8, 1, 0:255], lhsT=relb[:, :], rhs=ohs[:, 510:765], start=True, stop=True),
          reads=[boh], writes=[bsT[1]])
    btab = buf("tabE")
    S.run("act", lambda: act.activation(out=tabE[:, 0:510], in_=sT[0:8, 0, 0:510], func=AF.Exp), reads=[bsT[0]], writes=[btab])
    S.run("act", lambda: act.activation(out=tabE[:, 510:765], in_=sT[0:8, 1, 0:255], func=AF.Exp), reads=[bsT[1]], writes=[btab])
    S.run("act", lambda: act.activation(out=esink[:], in_=esink[:], func=AF.Exp), reads=bconsts, writes=[buf("esink")])
    S.run("pool", lambda: pool.memset(tabE[:, 0:127], 0.0), writes=[btab])
    S.run("pool", lambda: pool.memset(tabE[:, 510 + 128:765], 0.0), writes=[btab])
    bscr = buf("scr")
    S.dma("act", "scr", lambda: act.dma_start(out=tab_scr[:, :], in_=tabE), reads=[btab], writes=[bscr])
    bhk = [buf("hk0"), buf("hk1")]

    def hank_dma(j):
        S.dma("act", f"hk{j % 2}", lambda: act.dma_start(
            out=hk[:, j % 2, :, :], in_=bass.AP(tab_scr.tensor, j * 255, [[1, 128], [765, 8], [1, 128]])),
            reads=[bscr], writes=[bhk[j % 2]])

    def hank_rev(j):
        S.run("pool", lambda: pool.tensor_copy(out=E[:, j, :, :], in_=hk[:, j % 2, :, ::-1]), reads=[bhk[j % 2]], writes=[bE])

    hank_dma(0)
    hank_dma(1)

    def finish_E():
        hank_rev(0)
        hank_dma(2)
        hank_rev(1)
        hank_rev(2)
        bcat.r.append(bE.w)
        for b_ in bexp_all:
            b_.r.append(bscr.w)
            b_.r.extend(boh.r)

    bm = buf("medge")
    S.run("pool", lambda: pool.tensor_tensor(out=mta[:, :, 1:24], in0=mrow[:, :, 0:23], in1=mrow[:, :, 1:24], op=ALU.add), reads=bconsts, writes=[bm])
    S.run("pool", lambda: pool.tensor_copy(out=rc[:, :, 0, :], in_=mta[:, :, 8:16]), reads=[bm], writes=[buf("rc")])
    S.run("pool", lambda: pool.tensor_tensor(out=mtb[:, :, 2:23], in0=mta[:, :, 1:22], in1=mta[:, :, 3:24], op=ALU.add), reads=[bm], writes=[bm])
    S.run("pool", lambda: pool.tensor_copy(out=rc[:, :, 1, :], in_=mtb[:, :, 8:16]), reads=[bm], writes=[buf("rc")])
    S.run("pool", lambda: pool.tensor_tensor(out=mta[:, :, 4:21], in0=mtb[:, :, 2:19], in1=mtb[:, :, 6:23], op=ALU.add), reads=[bm], writes=[bm])
    S.run("pool", lambda: pool.tensor_copy(out=rc[:, :, 2, :], in_=mta[:, :, 8:16]), reads=[bm], writes=[buf("rc")])
    S.run("pool", lambda: pool.tensor_tensor(out=mtb[:, :, 8:17], in0=mta[:, :, 4:13], in1=mta[:, :, 12:21], op=ALU.add), reads=[bm], writes=[bm])
    S.run("pool", lambda: pool.tensor_copy(out=rc[:, :, 3, :], in_=mtb[:, :, 8:16]), reads=[bm], writes=[buf("rc")])

    bWin = [b_ for nm_ in bWblk for b_ in bWblk[nm_]]
    cast_i = [0]

    def cast_w_in(job):
        nm_, c0_, nc_, dk0_, ndk_ = wjobs[job]
        sl = job % 4
        st = Sst[sl][:, 0:ndk_ * nc_].rearrange("p (k c) -> p k c", k=ndk_)
        for kk in range(ndk_):
            dk = dk0_ + kk
            e_ = "dve" if cast_i[0] % 2 == 0 else "act"
            cast_i[0] += 1
            g_ = gpre[:, dk:dk + 1]
            if nm_ == "q":
                out_ap = Win_b[:, dk, 0:512].rearrange("p (c hf d) -> p c hf d", c=4, hf=2)
                in_ap = st[:, kk, :].rearrange("p (hf c d) -> p c hf d", hf=2, c=4)
            else:
                out_ap = Win_b[:, dk, c0_:c0_ + nc_]
                in_ap = st[:, kk, :]
            if e_ == "dve":
                S.run("dve", lambda: dve.tensor_scalar(out=out_ap, in0=in_ap, scalar1=g_, scalar2=None, op0=ALU.mult),
                      reads=[bS[sl]] + bconsts, writes=[bWblk[nm_][dk]])
            else:
                S.run("act", lambda: act.activation(out=out_ap, in_=in_ap, func=AF.Copy, scale=g_),
                      reads=[bS[sl]] + bconsts, writes=[bWblk[nm_][dk]])
        load_w_in(job + 4)

    bWout = [buf(f"Wout{ek}") for ek in range(8)]

    def wout_load(ek):
        sl = ek % 4
        S.dma("sp", ldw[sl], lambda: sp.dma_start(out=Sst[sl][:, 0:D], in_=w_out_d[ek * 128:(ek + 1) * 128, :]), writes=[bS[sl]])

    def wout_gen():
        for ek in range(4):
            wout_load(ek)
        for ek in range(8):
            sl = ek % 4
            S.run("act", lambda: act.activation(out=Wout_b[:, ek, :], in_=Sst[sl][:, 0:D], func=AF.Copy, scale=0.5),
                  reads=[bS[sl]], writes=[bWout[ek]])
            if ek + 4 < 8:
                wout_load(ek + 4)
            yield

    bpw = buf("poolw")
    S.run("pool", lambda: pool.tensor_copy(out=poolw_b[:], in_=poolw_f), reads=bconsts, writes=[bpw])
    buf("ta").r.append(bpw.w)

    bxb = [buf("xb0"), buf("xb1")]
    btr = buf("trb")
    bhTs = [buf("hT0"), buf("hT1")]
    bacc = [buf("acc0"), buf("acc1")]
    bob = buf("ob")
    btt = [buf("tt0"), buf("tt1")]
    bexp = [buf("exp0"), buf("exp1"), buf("exp2")]
    bpT = [buf("pT0"), buf("pT1"), buf("pT2")]
    bon = [buf("on0"), buf("on1")]
    bjunk = buf("junk")
    bms = buf("ms")
    bms2 = buf("ms2")
    bms2h = [buf("ms2h0"), buf("ms2h1")]
    bstat = buf("stat")
    bup = buf("upT")
    bcat = buf("catT")
    bta = buf("ta")
    btb = buf("tb")
    byT = [buf(f"yT{i}") for i in range(4)]
    bden = buf("den")
    os_ = [S0[:, 0:D], S0[:, D:2 * D]]
    rs_ = [S1[:, 0:D], S1[:, D:2 * D]]
    bos = [buf("os0"), buf("os1")]
    brs = [buf("rs0"), buf("rs1")]
    ossem = ["os0", "os1"]
    rssem = ["rs0", "rs1"]

    own = {}

    def setown(name, seg):
        own[name] = seg

    def chk(name, seg):
        assert own.get(name) == seg, f"ordering bug: {name} holds segment {own.get(name)}, expected {seg}"

    xs_i = [0]
    acc_i = [0]
    tt_i = [0]
    final_tokens = []

    ring = [(acc[:, 0, :], bacc[0]), (acc[:, 1, :], bacc[1]), (sT[:, 0, :], bsT[0]), (sT[:, 1, :], bsT[1]), (sT[:, 2, :], bsT[2])]

    reserved = set()

    def next_bank(reserve=False):
        for _ in range(len(ring)):
            i = acc_i[0] % len(ring)
            acc_i[0] += 1
            if i not in reserved:
                if reserve:
                    reserved.add(i)
                return ring[i] + (i,)
        raise AssertionError("no free PSUM ring bank")

    def stage_a1(s, lt):
        k = s * SEG_TILES + lt
        i = k % 3
        col = k
        S.run("act", lambda: act.activation(out=junk[:], in_=xs[:, i, :], func=AF.Square, scale=1.0 / 32.0,
                                            accum_out=ms[:, col:col + 1]), reads=[bxs[i], bms], writes=[bjunk, buf(f"ms_{k}")])
        S.run("pool", lambda: pool.tensor_scalar(out=r0[:, col:col + 1], in0=ms[:, col:col + 1], scalar1=EPS, scalar2=None, op0=ALU.add),
              reads=[buf(f"ms_{k}")], writes=[buf(f"stat_{k}")])
        S.run("pool", lambda: pool.tensor_tensor(out=rstd[:, col:col + 1], in0=r0[:, col:col + 1], in1=nhalf[:, 0:1], op=ALU.pow),
              reads=[buf(f"stat_{k}"), bsm], writes=[buf(f"stat_{k}")])

    def stage_a2(s, lt):
        setown(f"hT{s % 2}", s)
        k = s * SEG_TILES + lt
        i = k % 3
        col = k
        j = col % 2
        if s == 0:
            S.run("dve", lambda: dve.tensor_scalar(out=xb[:, j, :], in0=xs[:, i, :], scalar1=rstd[:, col:col + 1], scalar2=None, op0=ALU.mult),
                  reads=[buf(f"stat_{k}"), bxs[i]], writes=[bxb[j]])
        else:
            S.run("act", lambda: act.activation(out=xb[:, j, :], in_=xs[:, i, :], func=AF.Copy, scale=rstd[:, col:col + 1]),
                  reads=[buf(f"stat_{k}"), bxs[i]], writes=[bxb[j]])
        a_load(k + 3)

    def stage_a3(s, lt):
        k = s * SEG_TILES + lt
        j = k % 2
        S.run("pe", [lambda dk=dk: pe.transpose(out=trb[:, dk, :], in_=xb[:, j, dk * 128:(dk + 1) * 128], identity=ident[:]) for dk in range(8)],
              reads=[bxb[j], buf("identb")], writes=[btr])
        S.run("dve", lambda: dve.tensor_copy(out=hTs[s % 2][:, :, lt * 128:(lt + 1) * 128], in_=trb[:, :, :]), reads=[btr], writes=[bhTs[s % 2]])

    def g_a(s):
        stage_a1(s, 0)
        for lt in range(SEG_TILES):
            if lt + 1 < SEG_TILES:
                stage_a1(s, lt + 1)
            stage_a2(s, lt)
            if lt >= 1:
                stage_a3(s, lt - 1)
            yield
        stage_a3(s, SEG_TILES - 1)
        yield

    def stage_kv(s):
        p = s % 2
        hT = hTs[p]
        bhT = bhTs[p]
        chk(f"hT{p}", s)
        setown(f"kT{p}", s)
        setown(f"v1{p}", s)
        bk = buf(f"kT{p}")
        bv = buf(f"v1{p}")
        for (t0, n) in ((0, 512), (512, 256)):
            bk_, bb_, _ = next_bank()
            S.run("pe", [lambda dk=dk: pe.matmul(bk_[:, 0:n], lhsT=Win_b[:, dk, 512:640], rhs=hT[:, dk, t0:t0 + n],
                                                 start=(dk == 0), stop=(dk == 7)) for dk in range(8)],
                  reads=[bhT] + bWblk["kv"], writes=[bb_])
            S.run("act", lambda: act.copy(out=kT[p][:, t0:t0 + n], in_=bk_[:, 0:n]), reads=[bb_], writes=[bk])
            yield
        for (l0, nl) in ((0, 4), (4, 2)):
            bk_, bb_, _ = next_bank()
            fns = []
            for li in range(nl):
                lt = l0 + li
                for dk in range(8):
                    fns.append(lambda lt=lt, li=li, dk=dk: pe.matmul(bk_[:, li * 128:(li + 1) * 128], lhsT=hT[:, dk, lt * 128:(lt + 1) * 128],
                                                                     rhs=Win_b[:, dk, 640:768], start=(dk == 0), stop=(dk == 7)))
            S.run("pe", fns, reads=[bhT] + bWblk["kv"], writes=[bb_])
            S.run("dve", lambda: dve.tensor_copy(out=v1[p][:, l0:l0 + nl, :, 0:64],
                                                 in_=bk_[:, 0:nl * 128].rearrange("p (l g d) -> p l g d", l=nl, g=2)),
                  reads=[bb_], writes=[bv])
            yield
        S.run("pool", lambda: pool.tensor_copy(out=v1[p][:, :, :, 64],
                                               in_=kmask_pt[:, 4 * s:4 * s + 6].unsqueeze(2).to_broadcast([128, 6, 2])),
              reads=bconsts, writes=[bv])

    def stage_b(s, part):
        p = s % 2
        hT = hTs[p]
        bhT = bhTs[p]
        chk(f"hT{p}", s)
        if part == "u":
            setown("upT", s)
        else:
            setown(f"qT{p}", s)
            setown(f"GaT{p}", s)
            setown(f"GpT{p}", s)
        bq = buf(f"qT{p}")
        bga = buf(f"GaT{p}")
        bgp = buf(f"GpT{p}")

        def proj(col, bk_, bb_):
            blk = "q" if col < 512 else ("ga" if col < 1280 else ("u" if col < 1792 else "gp"))
            S.run("pe", [lambda dk=dk: pe.matmul(bk_, lhsT=Win_b[:, dk, col:col + 128], rhs=hT[:, dk, 128:640],
                                                 start=(dk == 0), stop=(dk == 7)) for dk in range(8)],
                  reads=[bhT] + bWblk[blk], writes=[bb_])

        if part == "rest":
            for c in range(4):
                bk_, bb_, _ = next_bank()
                proj(c * 128, bk_, bb_)
                S.run("act", lambda: act.copy(out=qT[p][:, c, :], in_=bk_), reads=[bb_], writes=[bq])
                yield
            for (base, G, bg) in ((768, GaT[p], bga), (1792, GpT[p], bgp)):
                for c in range(4):
                    bk_, bb_, _ = next_bank()
                    proj(base + c * 128, bk_, bb_)
                    ti = tt_i[0] % 2
                    tt_i[0] += 1
                    S.run("act", lambda: act.activation(out=tt[:, ti, :], in_=bk_, func=AF.Tanh, scale=0.5), reads=[bb_], writes=[btt[ti]])
                    S.run("dve", lambda: dve.scalar_tensor_tensor(out=G[:, c, :], in0=tt[:, ti, :], scalar=1.0, in1=bk_,
                                                                  op0=ALU.add, op1=ALU.mult), reads=[btt[ti], bb_], writes=[bg])
                    yield
        if part == "u":
            for c in range(4):
                bk_, bb_, _ = next_bank()
                proj(1280 + c * 128, bk_, bb_)
                S.run("act", lambda: act.copy(out=upT[:, c, 8:8 + SEG_TOK], in_=bk_), reads=[bb_], writes=[bup])
                yield
            bk_, bb_, _ = next_bank()
            fns = []
            for c in range(4):
                col = 1280 + c * 128
                for dk in range(8):
                    fns.append(lambda c=c, col=col, dk=dk: pe.matmul(
                        bk_[:, c * 16:(c + 1) * 16], lhsT=Win_b[:, dk, col:col + 128],
                        rhs=bass.AP(hT, dk * SEG_EXT + 120, [[8 * SEG_EXT, 128], [520, 2], [1, 8]]),
                        start=(dk == 0), stop=(dk == 7)))
            S.run("pe", fns, reads=[bhT] + bWblk["u"], writes=[bb_])
            hv = bk_[:, 0:64].rearrange("p (c s e) -> p c s e", c=4, s=2)
            S.run("dve", lambda: dve.tensor_copy(out=upT[:, :, 0:8], in_=hv[:, :, 0, :]), reads=[bb_], writes=[bup])
            S.run("dve", lambda: dve.tensor_copy(out=upT[:, :, 8 + SEG_TOK:16 + SEG_TOK], in_=hv[:, :, 1, :]), reads=[bb_], writes=[bup])
            yield

    pT_i = [0]
    on_i = [0]

    def stage_c(s, inline_out=False):
        p = s % 2
        bq = buf(f"qT{p}")
        bk = buf(f"kT{p}")
        bv = buf(f"v1{p}")
        bga = buf(f"GaT{p}")
        o4 = ob[:, :, 0:260].rearrange("p g (c e) -> p g c e", e=65)

        def qk(qb, g):
            bks = [next_bank()[0:2] for _ in range(3)]
            for j in range(3):
                S.run("pe", lambda j=j: pe.matmul(bks[j][0], lhsT=kT[p][64 * g:64 * g + 64, (qb + j) * 128:(qb + j + 1) * 128],
                                                  rhs=qT[p][64 * g:64 * g + 64, :, qb * 128:(qb + 1) * 128], start=True, stop=True),
                      reads=[bq, bk], writes=[bks[j][1]])
            return bks

        def softmax_num(qb, g, pi, bks):
            for j in range(3):
                S.run("act", lambda j=j: act.activation(out=expS[:, j, :], in_=bks[j][0], func=AF.Exp, scale=0.125),
                      reads=[bks[j][1]], writes=[bexp[j]])
                if POOL_EMULT and (j == 2 or (j == 1 and s == NSEG - 1)):
                    S.run("pool", lambda j=j: pool.tensor_tensor(out=pT[:, pi, j, :], in0=expS[:, j, :],
                                                                 in1=E[:, j, 4 * g:4 * g + 4, :].rearrange("p h q -> p (h q)"), op=ALU.mult),
                          reads=[bexp[j], bE], writes=[bpT[pi]])
                else:
                    S.run("dve", lambda j=j: dve.tensor_tensor(out=pT[:, pi, j, :], in0=expS[:, j, :],
                                                               in1=E[:, j, 4 * g:4 * g + 4, :].rearrange("p h q -> p (h q)"), op=ALU.mult),
                          reads=[bexp[j], bE], writes=[bpT[pi]])

        def pv(qb, g, pi):
            fns = []
            for c in range(4):
                for j in range(3):
                    fns.append(lambda c=c, j=j: pe.matmul(ob[:, g, c * 65:(c + 1) * 65], lhsT=pT[:, pi, j, c * 128:(c + 1) * 128],
                                                          rhs=v1[p][:, qb + j, g, :], start=(j == 0), stop=(j == 2)))
            S.run("pe", fns, reads=[bpT[pi], bv], writes=[buf(f"ob{g}")])

        def tail_a(qb, oi):
            dcol = qb % 2
            dn = den[:, dcol, :].rearrange("p (g c) -> p g c", g=2)
            rdn = rden[:, dcol, :].rearrange("p (g c) -> p g c", g=2)
            S.run("dve", lambda: dve.tensor_tensor(out=dn, in0=o4[:, :, :, 64], in1=esink[:].rearrange("p (g c) -> p g c", g=2), op=ALU.add),
                  reads=[buf("ob0"), buf("ob1"), buf("esink")], writes=[bden])
            S.run("dve", lambda: dve.reciprocal(out=rdn, in_=dn), reads=[bden], writes=[bden])
            S.run("dve", lambda: dve.tensor_tensor(out=onorm[:, oi, :].rearrange("p (g c d) -> p g c d", g=2, c=4),
                                                   in0=o4[:, :, :, 0:64], in1=rdn.unsqueeze(3).to_broadcast([128, 2, 4, 64]), op=ALU.mult),
                  reads=[buf("ob0"), buf("ob1"), bden], writes=[bon[oi]])

        def tail_b(qb, oi):
            S.run("pe", [lambda ec=ec: pe.transpose(out=trb[:, ec, :], in_=onorm[:, oi, ec * 128:(ec + 1) * 128], identity=ident[:]) for ec in range(4)],
                  reads=[bon[oi]], writes=[btr])
            S.run("dve", lambda: dve.tensor_tensor(out=catT[:, 0:4, qb * 128:(qb + 1) * 128], in0=trb[:, 0:4, :],
                                                   in1=GaT[p][:, :, qb * 128:(qb + 1) * 128], op=ALU.mult),
                  reads=[btr, bga], writes=[bcat])

        for nm in (f"qT{p}", f"kT{p}", f"v1{p}", f"GaT{p}"):
            chk(nm, s)
        setown("catA", s)
        units = [(qb, g) for qb in range(4) for g in range(2)]
        pvq = []
        pend = []
        outq = []

        def flush_tails():
            for it in pend:
                it[2] += 1
            while pend and pend[0][2] >= 1:
                q_, o_, _ = pend.pop(0)
                tail_b(q_, o_)
                if inline_out:
                    outq.append([q_, 0])

        def do_pv():
            qb_, g_, pi_ = pvq.pop(0)
            pv(qb_, g_, pi_)
            flush_tails()
            if g_ == 1:
                oi = on_i[0] % 2
                on_i[0] += 1
                tail_a(qb_, oi)
                pend.append([qb_, oi, 0])

        ui = 0
        deferred = []
        while ui < len(units) or pvq or pend or outq or deferred:
            if ui < len(units):
                qb, g = units[ui]
                ui += 1
                pi = pT_i[0] % 3
                pT_i[0] += 1
                bks = qk(qb, g)
                softmax_num(qb, g, pi, bks)
                pvq.append((qb, g, pi))
                if len(pvq) > 2:
                    do_pv()
            elif pvq:
                do_pv()
            else:
                flush_tails()
            while deferred and not pvq:
                deferred.pop(0)()
            ready = [o for o in outq if o[1] >= 1]
            for o in outq:
                o[1] += 1
            for o in ready:
                outq.remove(o)
                for _ in stage_out(s, (o[0],), defer=(deferred if (s == NSEG - 1 and ui >= len(units)) else None)):
                    pass
            yield

    def pool_pre(s):
        chk("upT", s)
        L = SEG_TOK + 16
        for gi, w in enumerate(POOL_WINDOWS):
            u = upT[:, gi, :]
            S.run("pool", lambda: pool.tensor_tensor(out=ta[:, 1:L], in0=u[:, 0:L - 1], in1=u[:, 1:L], op=ALU.add), reads=[bup], writes=[bta])
            src, bsrc = ta, bta
            if w >= 4:
                S.run("pool", lambda: pool.tensor_tensor(out=tb[:, 2:L - 1], in0=ta[:, 1:L - 2], in1=ta[:, 3:L], op=ALU.add), reads=[bta], writes=[btb])
                src, bsrc = tb, btb
            if w >= 8:
                S.run("pool", lambda: pool.tensor_tensor(out=ta[:, 4:L - 3], in0=tb[:, 2:L - 5], in1=tb[:, 6:L - 1], op=ALU.add), reads=[btb], writes=[bta])
                src, bsrc = ta, bta
            if w >= 16:
                S.run("pool", lambda: pool.tensor_tensor(out=tb[:, 8:L - 7], in0=ta[:, 4:L - 11], in1=ta[:, 12:L - 3], op=ALU.add), reads=[bta], writes=[btb])
                src, bsrc = tb, btb
            yi = gi
            chk("upT", s)
            setown(f"yT{gi}", s)
            S.run("dve", lambda: dve.scalar_tensor_tensor(out=yT[:, yi, :], in0=src[:, 8:8 + SEG_TOK], scalar=1.0 / w, in1=u[:, 8:8 + SEG_TOK],
                                                          op0=ALU.mult, op1=ALU.subtract), reads=[bsrc, bup], writes=[byT[yi]])
            for (edge, seg_e, lo) in ((0, 0, 0), (1, NSEG - 1, SEG_TOK - 8)):
                if s == seg_e:
                    S.run("pool", lambda: pool.tensor_tensor(out=e8a[:], in0=src[:, 8 + lo:16 + lo], in1=rc[:, edge, gi, :], op=ALU.mult),
                          reads=[bsrc, buf("rc")], writes=[buf("e8a")])
                    S.run("pool", lambda: pool.tensor_tensor(out=yT[:, yi, lo:lo + 8], in0=e8a[:], in1=u[:, 8 + lo:16 + lo], op=ALU.subtract),
                          reads=[buf("e8a"), bup], writes=[byT[yi]])
            yield

    def pool_post(s):
        p = s % 2
        bgp = buf(f"GpT{p}")
        for gi in range(4):
            bk_, bb_, _ = next_bank()
            chk(f"yT{gi}", s)
            chk(f"GpT{p}", s)
            setown(f"catP{gi}", s)
            S.run("pe", lambda: pe.matmul(bk_, lhsT=poolw_b[:, gi, :], rhs=yT[:, gi, :], start=True, stop=True),
                  reads=[byT[gi], bpw], writes=[bb_])
            S.run("dve", lambda: dve.scalar_tensor_tensor(out=catT[:, 4 + gi, :], in0=bk_, scalar=ps[:, gi:gi + 1], in1=GpT[p][:, gi, :],
                                                          op0=ALU.mult, op1=ALU.mult), reads=[bb_, bgp] + bconsts, writes=[bcat])
            yield

    fin_i = [0]

    pending_store = []

    def flush_store():
        while pending_store:
            t_, f_ = pending_store.pop(0)
            tok = S.dma("act", ossem[f_], lambda: act.dma_start(out=y_d[t_ * 128:(t_ + 1) * 128, :], in_=os_[f_]), reads=[bos[f_]])
            final_tokens.append(tok)

    def stage_out(s, qbs=(0, 1, 2, 3), defer=None):
        for qb in qbs:
            tile_own = 4 * s + qb
            chk("catA", s)
            for gi_ in range(4):
                chk(f"catP{gi_}", s)
            assert all(b_.w is not None for b_ in bWout)
            r0_, r1_ = next_bank(), next_bank()
            mx = [r0_[0], r1_[0]]
            bmx = [r0_[1], r1_[1]]
            for half in range(2):
                fns = []
                for ek in range(8):
                    fns.append(lambda half=half, ek=ek: pe.matmul(mx[half], lhsT=catT[:, ek, qb * 128:(qb + 1) * 128],
                                                                  rhs=Wout_b[:, ek, half * 512:(half + 1) * 512], start=(ek == 0), stop=(ek == 7)))
                S.run("pe", fns, reads=[bcat] + bWout, writes=[bmx[half]])
            fi = fin_i[0] % 2
            fin_i[0] += 1
            S.dma("act", rssem[fi], lambda: act.dma_start(out=rs_[fi], in_=x_ext[(tile_own + 1) * 128:(tile_own + 2) * 128, :]),
                  writes=[brs[fi]], reads=bWout + bWin)
            c0 = 2 * tile_own
            for half in range(2):
                S.run("act", lambda half=half: act.activation(out=junk[:, 0:512], in_=mx[half], func=AF.Square, scale=1.0 / 32.0,
                                                              accum_out=ms2[:, c0 + half:c0 + half + 1]), reads=[bmx[half]], writes=[bjunk, bms2h[half]])
            flush_store()
            S.run("pool", lambda: pool.tensor_tensor(out=r2[:, tile_own:tile_own + 1], in0=ms2[:, c0:c0 + 1], in1=nhalf[:, 1:2], op=ALU.add),
                  reads=[bms2h[0], bsm], writes=[buf("stat2")])
            S.run("pool", lambda: pool.tensor_tensor(out=r2[:, tile_own:tile_own + 1], in0=r2[:, tile_own:tile_own + 1], in1=ms2[:, c0 + 1:c0 + 2], op=ALU.add),
                  reads=[buf("stat2"), bms2h[1]], writes=[buf("stat2")])
            S.run("pool", lambda: pool.tensor_tensor(out=rstd2[:, tile_own:tile_own + 1], in0=r2[:, tile_own:tile_own + 1], in1=nhalf[:, 0:1], op=ALU.pow),
                  reads=[buf("stat2"), bsm], writes=[buf("stat2")])
            def evac(qb=qb, tile_own=tile_own, fi=fi, mx=mx, bmx=bmx):
                for half in range(2):
                    S.run("dve", lambda half=half: dve.scalar_tensor_tensor(out=os_[fi][:, half * 512:(half + 1) * 512], in0=mx[half],
                                                                            scalar=rstd2[:, tile_own:tile_own + 1], in1=gpost[:, half * 512:(half + 1) * 512],
                                                                            op0=ALU.mult, op1=ALU.mult),
                          reads=[bmx[half], buf("stat2")] + bconsts, writes=[bos[fi]])
                if s >= NSEG - 2:
                    S.run("dve", lambda: dve.tensor_tensor(out=os_[fi], in0=os_[fi], in1=rs_[fi], op=ALU.add), reads=[brs[fi]], writes=[bos[fi]])
                else:
                    S.run("pool", lambda: pool.tensor_tensor(out=os_[fi], in0=os_[fi], in1=rs_[fi], op=ALU.add), reads=[brs[fi]], writes=[bos[fi]])
                pending_store.append((tile_own, fi))
            if defer is None:
                evac()
            else:
                defer.append(evac)
            yield

    def dump(name, sbuf_ap, shape, dt, reads):
        if not DEBUG:
            return
        d = nc.dram_tensor("dbg_" + name, list(shape), dt, kind="ExternalOutput").ap()
        tok = S.dma("sp", "dbg", lambda: sp.dma_start(out=d, in_=sbuf_ap), reads=reads)
        final_tokens.append(tok)
        dbg[name] = d

    W_A = [3.0] * SEG_TILES
    W_KVB = [2.0] * 4 + [2.4] * 17
    W_PRE = [4.0] * 4
    W_POST = [0.5] * 4
    W_C = [2.0] * 8 + [0.7] * 3
    W_O = [4.5] * 4

    def chain(*gens):
        for g in gens:
            yield from g

    def drain(g):
        for _ in g:
            pass

    def interleave(main, wm, fill, wf):
        tm, tf = sum(wm), sum(wf)
        pm = pf = 0.0
        im = jf = 0
        dm = df = False
        while not (dm and df):
            if not dm and (df or pm / tm <= pf / tf):
                try:
                    next(main)
                    pm += wm[min(im, len(wm) - 1)]
                    im += 1
                except StopIteration:
                    dm = True
            else:
                try:
                    next(fill)
                    pf += wf[min(jf, len(wf) - 1)]
                    jf += 1
                except StopIteration:
                    df = True

    def mix(base, extra, every):
        n = 0
        for _ in base:
            yield
            n += 1
            if n % every == 0:
                try:
                    next(extra)
                except StopIteration:
                    pass
        for _ in extra:
            pass

    W_KV = [2.0] * 4
    W_BU = [2.4] * 5
    W_BR = [2.4] * 12

    def fill_gen(s1, extra=None):
        parts, w = [], []
        if s1 < NSEG:
            base = chain(stage_kv(s1), stage_b(s1, "u"), mix(stage_b(s1, "rest"), pool_pre(s1), 3))
            w += W_KV + W_BU + W_BR
        else:
            base = iter(())
        if s1 + 1 < NSEG:
            base = mix(base, g_a(s1 + 1), 3) if s1 < NSEG else g_a(s1 + 1)
        if extra is not None:
            base = mix(base, extra, 2)
        return base, (w if w else [1.0])

    def step(g):
        try:
            next(g)
            return True
        except StopIteration:
            return False

    ga0 = g_a(0)
    cj = [0]

    def casts(n):
        for _ in range(n):
            if cj[0] < len(wjobs):
                cast_w_in(cj[0])
                cj[0] += 1

    casts(2)
    while step(ga0):
        casts(1)
    S.run("dve", lambda: dve.reciprocal(out=rc[:], in_=rc[:]), reads=[buf("rc")], writes=[buf("rc")])
    kv0, bu0, br0 = stage_kv(0), stage_b(0, "u"), stage_b(0, "rest")
    pp0, ga1, wg = pool_pre(0), g_a(1), wout_gen()
    casts(10 - cj[0])
    drain(kv0)
    while step(bu0):
        step(ga1)
    for i_ in range(12):
        if i_ < 8:
            casts(1)
        step(br0)
        step(ga1)
        if i_ >= 8:
            step(wg)
        if i_ % 3 == 2:
            step(pp0)
    drain(pp0)
    drain(ga1)
    finish_E()
    if DEBUG:
        dump("hT", hTs[0][:], [128, 8, SEG_EXT], BF16, [bhTs[0]])
        dump("E", E[:], [128, 3, 8, 128], F32, [bE])
        dump("rc", rc[:], [128, 2, 4, 8], F32, [buf("rc")])
        dump("qT", qT[0][:], [128, 4, SEG_TOK], BF16, [buf("qT0")])
        dump("kT", kT[0][:], [128, SEG_EXT], BF16, [buf("kT0")])
        dump("v1", v1[0][:], [128, SEG_TILES, 2, 65], BF16, [buf("v10")])
        dump("GaT", GaT[0][:], [128, 4, SEG_TOK], BF16, [buf("GaT0")])
        dump("GpT", GpT[0][:], [128, 4, SEG_TOK], BF16, [buf("GpT0")])
        dump("upT", upT[:], [128, 4, SEG_TOK + 16], F32, [bup])
    for s in range(NSEG):
        if s < 0:
            main = chain(stage_c(s), pool_post(s), stage_out(s))
            wm = W_C + W_POST + W_O
        elif s + 1 < NSEG:
            main = mix(stage_c(s, inline_out=True), pool_post(s), 1)
            wm = [2.0, 2.5, 2.5, 2.5, 2.5, 6.5, 2.0, 6.5, 0.7, 5.2, 0.7, 4.5, 0.5]
        else:
            main = mix(stage_c(s, inline_out=True), pool_post(s), 1)
            wm = W_C
        if s + 1 < NSEG:
            f, wf = fill_gen(s + 1)
            if s == 0:
                f = mix(f, wg, 1)
            interleave(main, wm, f, wf)
        else:
            drain(main)

    flush_store()
    for tok in final_tokens:
        S.wait("act", tok)
    return nc


def _t5_bucket_np(rel):
    nb = 16
    max_exact = 8
    ret = np.where(rel > 0, nb, 0)
    n = np.abs(rel)
    nf = np.maximum(n, 1).astype(np.float32)
    val = (np.log(nf / np.float32(max_exact)) / np.float32(math.log(128 / max_exact)) * np.float32(nb - max_exact)).astype(np.float32)
    large = max_exact + val.astype(np.int32)
    large = np.minimum(large, nb - 1)
    return ret + np.where(n < max_exact, n, large)


def _onehot_table():
    oh = np.zeros((32, 765), np.float32)
    for j in range(3):
        u = np.arange(255)
        rel = (j - 1) * 128 + (u - 127)
        b = _t5_bucket_np(rel)
        oh[b, j * 255 + u] = 1.0
    return oh


_NC_CACHE = {}


def kernel(x, pre_norm_g, w_in, rel_bias, attn_sink, pool_w, pool_scale, w_out, post_norm_g):
    x = np.asarray(x, np.float32)
    xs2 = x.reshape(SEQ, D)
    xpad = np.zeros((SEQ + 256, D), np.float32)
    xpad[128:128 + SEQ] = xs2
    mask_full = np.zeros((SEQ + 256,), np.float32)
    mask_full[128:128 + SEQ] = 1.0
    shared = {
        "gpre": np.ascontiguousarray(np.asarray(pre_norm_g, np.float32).reshape(8, 128).T),
        "w_in": np.ascontiguousarray(np.asarray(w_in, np.float32).reshape(D, D_IN)),
        "rel_bias": np.ascontiguousarray(np.asarray(rel_bias, np.float32)),
        "onehot": _onehot_table(),
        "attn_sink": np.ascontiguousarray(np.asarray(attn_sink, np.float32).reshape(8)),
        "pool_w": np.ascontiguousarray(np.asarray(pool_w, np.float32).reshape(4, 128, 128)),
        "pool_scale_pg": np.ascontiguousarray(np.asarray(pool_scale, np.float32).reshape(4, 128).T),
        "w_out": np.ascontiguousarray(np.asarray(w_out, np.float32).reshape(D, D)),
        "post_norm_g": np.ascontiguousarray(np.asarray(post_norm_g, np.float32).reshape(D)),
    }
    in_maps = []
    for c in range(N_CORES):
        lo = c * TOK_CORE
        m = dict(shared)
        m["x_ext"] = np.ascontiguousarray(xpad[lo:lo + EXT])
        km = mask_full[lo:lo + EXT]
        m["kmask_row"] = np.ascontiguousarray(km)
        m["kmask_pt"] = np.ascontiguousarray(km.reshape(18, 128).T)
        in_maps.append(m)
    if "nc" not in _NC_CACHE:
        _NC_CACHE["nc"] = build_nc()
    nc = _NC_CACHE["nc"]
    res = run_bass_kernel_spmd(nc, in_maps, core_ids=list(range(N_CORES)))
    out = np.concatenate([np.asarray(r["y"], np.float32) for r in res.results], axis=0)
    if DEBUG:
        kernel.debug = [{k: v for k, v in r.items() if k.startswith("dbg_")} for r in res.results]
    return out.reshape(1, SEQ, D)
```
